# Optimizing a Trainium2 kernel written in Bass

```python
import math
import jax, jax.numpy as jnp
from jax import lax
import numpy as np

D_MODEL = 1024
BATCH = 32
SEQ = 256
DEPTH = 4
DEC_BATCH = 2
DEC_SEQ = 4096
PAST_LEN = 512

GRID_W = 64
HEAD_DIM = 64
N_HEADS_NA = 8
N_HEADS_RWKV = 8
N_HEADS_DIFF = 8
D_NA = N_HEADS_NA * HEAD_DIM
D_RWKV = N_HEADS_RWKV * HEAD_DIM
D_DIFF = N_HEADS_DIFF * 2 * HEAD_DIM
NA_KH = 8
NA_KW = 16
DECAY_LORA = 64
ICL_LORA = 64
GATE_LORA = 128
P_RWKV = 3 * D_RWKV + DECAY_LORA + ICL_LORA + GATE_LORA
P_EVEN = 3 * D_NA + P_RWKV
RWKV_SPLITS = (D_RWKV, 2 * D_RWKV, 3 * D_RWKV, 3 * D_RWKV + DECAY_LORA, 3 * D_RWKV + DECAY_LORA + ICL_LORA)
D_FF = -(-(8 * D_MODEL) // (3 * 256)) * 256
N_EVEN = (DEPTH + 1) // 2
N_ODD = DEPTH // 2
Q_BLOCK = 128
ROPE_F = HEAD_DIM // 4
ROPE_BASE = 10000.0
NORM_EPS = 1e-6
GN_EPS = 64e-5

kernel_name = "hybrid_diffusion_na_rwkv7_diffattn_step"


def rmsnorm(x, g):
    xf = x.astype(jnp.float32)
    r = lax.rsqrt(jnp.mean(xf * xf, -1, keepdims=True) + NORM_EPS)
    return (xf * r).astype(x.dtype) * g


def adaln(cvec, w, b):
    m = jax.nn.silu(cvec) @ w + b
    return [t[:, None, :] for t in jnp.split(m, 6, axis=-1)]


def modulate(h, shift, scale):
    return h * (1.0 + scale) + shift


def swiglu(h, w1, w3, w2):
    return (jax.nn.silu(h @ w1) * (h @ w3)) @ w2


def centred_shift(y, mu):
    prev = jnp.pad(y[:, :-1], ((0, 0), (1, 0), (0, 0)))
    nxt = jnp.pad(y[:, 1:], ((0, 0), (0, 1), (0, 0)))
    return y + mu[0] * (prev - y) + mu[1] * (nxt - y)


def axial_rope(x):
    T = x.shape[1]
    t = jnp.arange(T)
    pos = jnp.stack([t // GRID_W, t % GRID_W], -1).astype(jnp.float32)
    inv = ROPE_BASE ** (-jnp.arange(ROPE_F, dtype=jnp.float32) / ROPE_F)
    ang = (pos[:, :, None] * inv).reshape((T,) + (1,) * (x.ndim - 3) + (2, ROPE_F))
    cos, sin = jnp.cos(ang).astype(x.dtype), jnp.sin(ang).astype(x.dtype)
    xr = x.reshape(x.shape[:-1] + (2, 2, ROPE_F))
    x1, x2 = xr[..., 0, :], xr[..., 1, :]
    return jnp.stack([x1 * cos - x2 * sin, x1 * sin + x2 * cos], axis=-2).reshape(x.shape)


def softmax_attn_blocked(q, k, v):
    B, T, H, d = q.shape
    qb = jnp.moveaxis(q.reshape(B, T // Q_BLOCK, Q_BLOCK, H, d), 1, 0)

    def blk(q_i):
        s = jnp.einsum('bqhd,bmhd->bhqm', q_i, k) * (d ** -0.5)
        p = jax.nn.softmax(s.astype(jnp.float32), axis=-1).astype(v.dtype)
        return jnp.einsum('bhqm,bmhd->bqhd', p, v)

    o = lax.map(blk, qb)
    return jnp.moveaxis(o, 0, 1).reshape(B, T, H * v.shape[-1])


def na_latent(q, k, v, k_ctx, v_ctx, rpb):
    B, T, H, d = q.shape
    rows = T // GRID_W
    kh = min(NA_KH, rows)
    kg = k.reshape(B, rows, GRID_W, H, d)
    vg = v.reshape(B, rows, GRID_W, H, d)
    cols = jnp.arange(GRID_W)
    c0 = jnp.clip(cols - NA_KW // 2, 0, GRID_W - NA_KW)
    col_idx = c0[:, None] + jnp.arange(NA_KW)[None, :]
    dc = col_idx - cols[:, None] + NA_KW - 1
    scale = d ** -0.5
    qg = jnp.moveaxis(q.reshape(B, rows, GRID_W, H, d), 1, 0)

    def row(args):
        i, q_i = args
        r0 = jnp.clip(i - kh // 2, 0, rows - kh)
        kb = lax.dynamic_slice_in_dim(kg, r0, kh, axis=1)[:, :, col_idx]
        vb = lax.dynamic_slice_in_dim(vg, r0, kh, axis=1)[:, :, col_idx]
        dr = r0 + jnp.arange(kh) - i + NA_KH - 1
        bias = rpb[:, dr[:, None, None], dc[None]].transpose(0, 2, 1, 3)
        s_loc = jnp.einsum('bjhd,brjchd->bhjrc', q_i, kb) * scale + bias[None]
        s_ctx = jnp.einsum('bjhd,bmhd->bhjm', q_i, k_ctx) * scale
        s = jnp.concatenate([s_loc.reshape(B, H, GRID_W, kh * NA_KW), s_ctx], axis=-1)
        p = jax.nn.softmax(s.astype(jnp.float32), axis=-1).astype(v.dtype)
        p_loc = p[..., :kh * NA_KW].reshape(B, H, GRID_W, kh, NA_KW)
        p_ctx = p[..., kh * NA_KW:]
        return (jnp.einsum('bhjrc,brjchd->bjhd', p_loc, vb)
                + jnp.einsum('bhjm,bmhd->bjhd', p_ctx, v_ctx))

    o = lax.map(row, (jnp.arange(rows), qg))
    return jnp.moveaxis(o, 0, 1).reshape(B, T, H * d)


def wkv7_scan(S0, r, decay, kk, a, v, kt, reverse):
    def step(S, inp):
        r_t, w_t, kk_t, a_t, v_t, kt_t = inp
        s_kk = jnp.einsum('bhvk,bhk->bhv', S, kk_t)
        S = (S * w_t[:, :, None, :] - s_kk[..., None] * (kk_t * a_t)[:, :, None, :]
             + v_t[..., None] * kt_t[:, :, None, :])
        return S, jnp.einsum('bhvk,bhk->bhv', S, r_t)

    xs = tuple(jnp.moveaxis(t.astype(jnp.float32), 1, 0) for t in (r, decay, kk, a, v, kt))
    S_fin, ys = lax.scan(step, S0.astype(jnp.float32), xs, reverse=reverse)
    return S_fin, jnp.moveaxis(ys, 0, 1)


def rwkv_mixer(u, S0, mu, w0, w2, a0, a2, k_k, k_a, bonus, g2, ln_w, ln_b):
    B, T, _ = u.shape
    heads = lambda t: t.reshape(B, T, N_HEADS_RWKV, HEAD_DIM)
    u = centred_shift(u, mu)
    r, k, v, w_lo, a_lo, g_lo = jnp.split(u, RWKV_SPLITS, axis=-1)
    rh, vh = heads(r), heads(v)
    kk = heads(k * k_k).astype(jnp.float32)
    kk = kk * lax.rsqrt(jnp.maximum(jnp.sum(kk * kk, -1, keepdims=True), 1e-12))
    ys, bon, finals = [], [], []
    for d in range(2):
        w_raw = (w0[d] + jnp.tanh(w_lo) @ w2[d]).astype(jnp.float32)
        decay = jnp.exp(-jnp.exp(-jax.nn.softplus(-w_raw) - 0.5))
        a = jax.nn.sigmoid(a0[d] + a_lo @ a2[d])
        kt = heads(k * (1.0 + (a - 1.0) * k_a))
        S_fin, y = wkv7_scan(S0[:, d], rh, heads(decay), kk, heads(a), vh, kt, reverse=(d == 1))
        ys.append(y)
        finals.append(S_fin)
        bon.append(jnp.sum(rh * kt * bonus[d], -1, keepdims=True) * vh)
    y = ys[0] + ys[1]
    mean = jnp.mean(y, -1, keepdims=True)
    var = jnp.mean(jnp.square(y - mean), -1, keepdims=True)
    yn = ((y - mean) * lax.rsqrt(var + GN_EPS)).reshape(B, T, D_RWKV).astype(u.dtype) * ln_w + ln_b
    out = (yn + (bon[0] + bon[1]).reshape(B, T, D_RWKV)) * (jax.nn.sigmoid(g_lo) @ g2)
    return out, jnp.stack(finals, axis=1)


def even_mixer(h, w_in, w_out, rpb, rw, na_ctx=None, S0=None):
    B, T, _ = h.shape
    proj = h @ w_in
    qa, ka, va, u = jnp.split(proj, [D_NA, 2 * D_NA, 3 * D_NA], axis=-1)
    heads = lambda t: t.reshape(B, T, N_HEADS_NA, HEAD_DIM)
    qa, ka, va = heads(qa), heads(ka), heads(va)
    if na_ctx is None:
        o_na = softmax_attn_blocked(qa, ka, va)
        S0 = jnp.zeros((B, 2, N_HEADS_RWKV, HEAD_DIM, HEAD_DIM), jnp.float32)
    else:
        o_na = na_latent(qa, ka, va, na_ctx[0], na_ctx[1], rpb)
    o_rw, S_fin = rwkv_mixer(u, S0, *rw)
    y = jnp.concatenate([o_na, o_rw], axis=-1) @ w_out
    return y, ka, va, S_fin


def diff_attn_blocked(q, k, v, lam):
    B, T, H, _, d = q.shape
    qb = jnp.moveaxis(q.reshape(B, T // Q_BLOCK, Q_BLOCK, H, 2, d), 1, 0)

    def blk(q_i):
        s = jnp.einsum('bqhsd,bmhsd->bhsqm', q_i, k) * (d ** -0.5)
        p = jax.nn.softmax(s.astype(jnp.float32), axis=-1)
        att = (p[:, :, 0] - lam * p[:, :, 1]).astype(v.dtype)
        return jnp.einsum('bhqm,bmhe->bqhe', att, v)

    o = lax.map(blk, qb)
    return jnp.moveaxis(o, 0, 1).reshape(B, T, H, v.shape[-1])


def odd_mixer(h, w_qkv, w_out, lam_q, lam_k, subln, lam_init, diff_ctx=None):
    B, T, _ = h.shape
    q, k, v = jnp.split(h @ w_qkv, 3, axis=-1)
    q = q.reshape(B, T, N_HEADS_DIFF, 2, HEAD_DIM)
    k = k.reshape(B, T, N_HEADS_DIFF, 2, HEAD_DIM)
    v = v.reshape(B, T, N_HEADS_DIFF, 2 * HEAD_DIM)
    lq, lk = lam_q.astype(jnp.float32), lam_k.astype(jnp.float32)
    lam = jnp.exp(jnp.sum(lq[0] * lk[0])) - jnp.exp(jnp.sum(lq[1] * lk[1])) + lam_init
    k_cache = k.reshape(B, T, N_HEADS_DIFF, 2 * HEAD_DIM)
    if diff_ctx is None:
        k_all, v_all = k, v
    else:
        M = diff_ctx[0].shape[1]
        q, k = axial_rope(q), axial_rope(k)
        k_all = jnp.concatenate([diff_ctx[0].reshape(B, M, N_HEADS_DIFF, 2, HEAD_DIM), k], axis=1)
        v_all = jnp.concatenate([diff_ctx[1], v], axis=1)
    o = diff_attn_blocked(q, k_all, v_all, lam)
    o = rmsnorm(o, subln) * (1.0 - lam_init)
    y = o.reshape(B, T, D_DIFF) @ w_out
    return y, k_cache, v


def setup_inputs(seed: int = 0) -> dict:
    key = jax.random.key(seed)
    ks = iter(jax.random.split(key, 48))
    nrm = lambda shape, s: jax.random.normal(next(ks), shape, jnp.float32) * s
    gain = lambda shape: 1.0 + nrm(shape, 0.02)
    D = D_MODEL
    return {
        'x_prompt': nrm((BATCH, SEQ, D), 1.0),
        'x_sample': nrm((DEC_BATCH, DEC_SEQ, D), 1.0),
        'c': nrm((DEC_BATCH, D), 1.0),
        'cache_na_k': nrm((DEC_BATCH, N_EVEN, PAST_LEN, N_HEADS_NA, HEAD_DIM), 1.0),
        'cache_na_v': nrm((DEC_BATCH, N_EVEN, PAST_LEN, N_HEADS_NA, HEAD_DIM), 1.0),
        'state_rwkv': nrm((DEC_BATCH, N_EVEN, 2, N_HEADS_RWKV, HEAD_DIM, HEAD_DIM), 0.3),
        'cache_diff_k': nrm((DEC_BATCH, N_ODD, PAST_LEN, N_HEADS_DIFF, 2 * HEAD_DIM), 1.0),
        'cache_diff_v': nrm((DEC_BATCH, N_ODD, PAST_LEN, N_HEADS_DIFF, 2 * HEAD_DIM), 1.0),
        'c_ctx': nrm((D,), 1.0),
        'w_ada': nrm((DEPTH, D, 6 * D), D ** -0.5),
        'b_ada': nrm((DEPTH, 6 * D), 0.02),
        'norm_mix': gain((DEPTH, D)),
        'norm_ffn': gain((DEPTH, D)),
        'norm_final': gain((D,)),
        'w_in_even': nrm((N_EVEN, D, P_EVEN), D ** -0.5),
        'w_out_even': nrm((N_EVEN, D_NA + D_RWKV, D), (D_NA + D_RWKV) ** -0.5),
        'na_rpb': nrm((N_EVEN, N_HEADS_NA, 2 * NA_KH - 1, 2 * NA_KW - 1), 0.1),
        'rw_mu': jax.random.uniform(next(ks), (N_EVEN, 2, P_RWKV), jnp.float32, 0.0, 0.5),
        'rw_w0': nrm((N_EVEN, 2, D_RWKV), 0.5),
        'rw_w2': nrm((N_EVEN, 2, DECAY_LORA, D_RWKV), 0.1),
        'rw_a0': nrm((N_EVEN, 2, D_RWKV), 0.1),
        'rw_a2': nrm((N_EVEN, 2, ICL_LORA, D_RWKV), 0.1),
        'rw_kk': 0.85 + nrm((N_EVEN, D_RWKV), 0.02),
        'rw_ka': gain((N_EVEN, D_RWKV)),
        'rw_bonus': nrm((N_EVEN, 2, N_HEADS_RWKV, HEAD_DIM), 0.1),
        'rw_g2': nrm((N_EVEN, GATE_LORA, D_RWKV), GATE_LORA ** -0.5),
        'rw_lnw': gain((N_EVEN, D_RWKV)),
        'rw_lnb': nrm((N_EVEN, D_RWKV), 0.02),
        'w_qkv_diff': nrm((N_ODD, D, 3 * D_DIFF), D ** -0.5),
        'w_out_diff': nrm((N_ODD, D_DIFF, D), D_DIFF ** -0.5),
        'diff_lam_q': nrm((N_ODD, 2, HEAD_DIM), 0.1),
        'diff_lam_k': nrm((N_ODD, 2, HEAD_DIM), 0.1),
        'diff_subln': gain((N_ODD, 2 * HEAD_DIM)),
        'ffn_w1': nrm((DEPTH, D, D_FF), D ** -0.5),
        'ffn_w3': nrm((DEPTH, D, D_FF), D ** -0.5),
        'ffn_w2': nrm((DEPTH, D_FF, D), D_FF ** -0.5),
    }


def reference(x_prompt, x_sample, c, cache_na_k, cache_na_v, state_rwkv, cache_diff_k, cache_diff_v,
              c_ctx, w_ada, b_ada, norm_mix, norm_ffn, norm_final, w_in_even, w_out_even, na_rpb,
              rw_mu, rw_w0, rw_w2, rw_a0, rw_a2, rw_kk, rw_ka, rw_bonus, rw_g2, rw_lnw, rw_lnb,
              w_qkv_diff, w_out_diff, diff_lam_q, diff_lam_k, diff_subln, ffn_w1, ffn_w3, ffn_w2):
    xp, xs = x_prompt, x_sample
    na_k_out, na_v_out, rw_out, dk_out, dv_out = [], [], [], [], []
    for l in range(DEPTH):
        mp = adaln(c_ctx[None, :], w_ada[l], b_ada[l])
        ms = adaln(c, w_ada[l], b_ada[l])
        hp = modulate(rmsnorm(xp, norm_mix[l]), mp[0], mp[1])
        hs = modulate(rmsnorm(xs, norm_mix[l]), ms[0], ms[1])
        if l % 2 == 0:
            e = l // 2
            rw = (rw_mu[e], rw_w0[e], rw_w2[e], rw_a0[e], rw_a2[e], rw_kk[e], rw_ka[e],
                  rw_bonus[e], rw_g2[e], rw_lnw[e], rw_lnb[e])
            yp, kp, vp, Sp = even_mixer(hp, w_in_even[e], w_out_even[e], na_rpb[e], rw)
            ys, _, _, _ = even_mixer(hs, w_in_even[e], w_out_even[e], na_rpb[e], rw,
                                     (cache_na_k[:, e], cache_na_v[:, e]), state_rwkv[:, e])
            na_k_out.append(kp)
            na_v_out.append(vp)
            rw_out.append(Sp)
        else:
            o = l // 2
            lam_init = 0.8 - 0.6 * math.exp(-0.3 * l)
            dp = (w_qkv_diff[o], w_out_diff[o], diff_lam_q[o], diff_lam_k[o], diff_subln[o], lam_init)
            yp, kp, vp = odd_mixer(hp, *dp)
            ys, _, _ = odd_mixer(hs, *dp, (cache_diff_k[:, o], cache_diff_v[:, o]))
            dk_out.append(kp)
            dv_out.append(vp)
        xp = xp + mp[2] * yp
        xs = xs + ms[2] * ys
        xp = xp + mp[5] * swiglu(modulate(rmsnorm(xp, norm_ffn[l]), mp[3], mp[4]), ffn_w1[l], ffn_w3[l], ffn_w2[l])
        xs = xs + ms[5] * swiglu(modulate(rmsnorm(xs, norm_ffn[l]), ms[3], ms[4]), ffn_w1[l], ffn_w3[l], ffn_w2[l])
    y_prompt = rmsnorm(xp, norm_final)
    y_sample = rmsnorm(xs, norm_final)
    new_na_k = jnp.stack(na_k_out, axis=1)
    new_na_v = jnp.stack(na_v_out, axis=1)
    new_state_rwkv = jnp.stack(rw_out, axis=1)
    new_diff_k = jnp.stack(dk_out, axis=1)
    new_diff_v = jnp.stack(dv_out, axis=1)
    return (y_prompt, y_sample, new_na_k, new_na_v, new_state_rwkv, new_diff_k, new_diff_v)
```

```python
import contextlib
import math
import numpy as np
import concourse.bass as bass
import concourse.mybir as mybir
from concourse.bass_utils import run_bass_kernel_spmd
F32 = mybir.dt.float32
BF16 = mybir.dt.bfloat16
I32 = mybir.dt.int32
AF = mybir.ActivationFunctionType
ALU = mybir.AluOpType
AX = mybir.AxisListType

EPOCH = 30000


class Buf:
    def __init__(self, name, t):
        self.name = name
        self.t = t
        self.st = {}

    def states(self, key):
        if key is None:
            if None not in self.st:
                self.st[None] = [None, {}]
            return list(self.st.values())
        if key not in self.st:
            self.st[key] = [None, {}]
        out = [self.st[key]]
        if None in self.st:
            out.append(self.st[None])
        return out

    def __getitem__(self, idx):
        return self.t[idx]


class KB:
    ENGS = ("pe", "dve", "act", "pool", "sp")

    def __init__(self, nc, stack):
        self.nc = nc
        self.stack = stack
        self.h = {"pe": nc.tensor, "dve": nc.vector, "act": nc.scalar, "pool": nc.gpsimd, "sp": nc.sync}
        self.stream = {e: [] for e in self.ENGS}
        self.sem = {}
        self.cnt = {}
        self.nsem = 0
        self.pe_sems = set()
        for e in self.ENGS:
            self._new_eng_sem(e)
        self.dpool = {}
        self.dnext = {}
        for e in ("sp", "act", "pool"):
            self.dpool[e] = [[self._mksem(f"d_{e}_{i}"), 0] for i in range(6)]
            self.dnext[e] = 0
        self.waited = {e: {} for e in self.ENGS}
        self.ninstr = 0

    def _mksem(self, name):
        self.nsem += 1
        return self.stack.enter_context(self.nc.semaphore(name))

    def _new_eng_sem(self, e):
        k = sum(1 for n in self.sem if n[0] == e) if False else None
        s = self._mksem(f"s_{e}_{self.nsem}")
        if e == "pe":
            self.pe_sems.add(id(s))
        self.sem[e] = s
        self.cnt[e] = 0

    def sb(self, name, shape, dtype=F32):
        t = self.stack.enter_context(self.nc.sbuf_tensor("sb_" + name, list(shape), dtype))
        return Buf(name, t)

    def ps(self, name, shape, dtype=F32):
        t = self.stack.enter_context(self.nc.psum_tensor("ps_" + name, list(shape), dtype))
        return Buf(name, t)

    def dram(self, name, shape, dtype=F32, kind="Internal"):
        if kind == "Internal":
            t = self.nc.dram_tensor(name, list(shape), dtype)
        else:
            t = self.nc.dram_tensor(name, list(shape), dtype, kind=kind)
        return Buf(name, t)

    def _deps(self, eng, reads, writes):
        toks = []
        for (b, k) in reads:
            for st in b.states(k):
                if st[0] is not None:
                    toks.append(st[0])
        for (b, k) in writes:
            for st in b.states(k):
                if st[0] is not None:
                    toks.append(st[0])
                for t in st[1].values():
                    toks.append(t)
        best = {}
        for (s, v) in toks:
            key = id(s)
            if key not in best or best[key][1] < v:
                best[key] = (s, v)
        out = []
        w = self.waited[eng]
        for key, (s, v) in best.items():
            if eng == "pe" and key in self.pe_sems:
                continue
            if w.get(key, 0) >= v:
                continue
            w[key] = v
            out.append((s, v))
        return out

    def _record(self, eng, tok, reads, writes):
        for (b, k) in reads:
            if k is None:
                for st in b.states(None):
                    st[1][eng] = tok
            else:
                b.states(k)[0][1][eng] = tok
        for (b, k) in writes:
            if k is None:
                for st in b.states(None):
                    st[0] = tok
                    st[1] = {}
            else:
                st = b.states(k)[0]
                st[0] = tok
                st[1] = {}

    @staticmethod
    def _norm(lst):
        out = []
        for x in lst:
            if isinstance(x, Buf):
                out.append((x, None))
            else:
                out.append(x)
        return out

    def op(self, eng, fn, reads=(), writes=()):
        reads = self._norm(reads)
        writes = self._norm(writes)
        waits = self._deps(eng, reads, writes)
        if self.cnt[eng] >= EPOCH:
            self._new_eng_sem(eng)
        self.cnt[eng] += 1
        s = self.sem[eng]
        v = self.cnt[eng]
        tok = (s, v)
        self.stream[eng].append((waits, fn, s, 1))
        self._record(eng, tok, reads, writes)
        self.ninstr += 1 + len(waits)
        return tok

    def I(self, eng, name, reads=(), writes=(), **kwargs):
        def fn(e, name=name, kwargs=kwargs):
            return getattr(e, name)(**kwargs)
        return self.op(eng, fn, reads, writes)

    def MM(self, out, lhsT, rhs, start, stop, reads=(), writes=()):
        def fn(e, out=out, lhsT=lhsT, rhs=rhs, start=start, stop=stop):
            return e.matmul(out, lhsT=lhsT, rhs=rhs, start=start, stop=stop)
        return self.op("pe", fn, reads, writes)

    def TR(self, out, in_, ident, reads=(), writes=()):
        def fn(e, out=out, in_=in_, ident=ident):
            return e.transpose(out, in_, ident)
        return self.op("pe", fn, reads, writes)

    def dma(self, eng, out_ap, in_ap, reads=(), writes=(), **kw):
        reads = self._norm(reads)
        writes = self._norm(writes)
        waits = self._deps(eng, reads, writes)
        pool = self.dpool[eng]
        j = self.dnext[eng]
        self.dnext[eng] = (j + 1) % len(pool)
        s, c = pool[j]
        w = self.waited[eng]
        if c > 0 and w.get(id(s), 0) < c:
            waits.append((s, c))
            w[id(s)] = c
        c += 16
        pool[j][1] = c
        tok = (s, c)

        def fn(e, out_ap=out_ap, in_ap=in_ap, kw=kw):
            o = out_ap(e) if callable(out_ap) else out_ap
            i = in_ap(e) if callable(in_ap) else in_ap
            return e.dma_start(out=o, in_=i, **kw)

        self.stream[eng].append((waits, fn, s, 16))
        self._record("dma_" + eng + str(j), tok, reads, writes)
        self.ninstr += 1 + len(waits)
        return tok

    def custom(self, eng, fn, sem_inc, reads=(), writes=()):
        reads = self._norm(reads)
        writes = self._norm(writes)
        waits = self._deps(eng, reads, writes)
        s = self._mksem(f"c_{self.nsem}")
        tok = (s, sem_inc)
        self.stream[eng].append((waits, fn, s, sem_inc))
        self._record("cc" + str(self.nsem), tok, reads, writes)
        return tok

    def wait_all(self, eng, bufs):
        reads = self._norm(bufs)
        waits = self._deps(eng, reads, [])
        self.stream[eng].append((waits, None, None, 0))

    def emit(self):
        nc = self.nc
        streams = self.stream

        def run(e, lst):
            for (waits, fn, s, inc) in lst:
                for (ws, wv) in waits:
                    e.wait_ge(ws, wv)
                if fn is not None:
                    ins = fn(e)
                    ins.then_inc(s, inc)

        with nc.Block() as block:
            @block.sync
            def _(e):
                run(e, streams["sp"])

            @block.tensor
            def _(e):
                run(e, streams["pe"])

            @block.vector
            def _(e):
                run(e, streams["dve"])

            @block.scalar
            def _(e):
                run(e, streams["act"])

            @block.gpsimd
            def _(e):
                run(e, streams["pool"])


D = 1024
KC = 8
NTOK = 2048
TT = 512
NT = 4
DFF = 2816
SCALE = 0.125
MASKV = -30000.0
GROUPS4 = [[0, 1, 2, 3], [4, 5, 6, 7]]


class Ctx:
    pass


def build_program(dbg=None, nlayers=4, mixers=True):
    nc = bass.Bass("TRN2", target_bir_lowering=False)
    st = contextlib.ExitStack()
    with st:
        kb = KB(nc, st)
        g = Ctx()
        g.nc, g.kb = nc, kb
        g.dbg = dbg or []
        g.dbg_out = {}
        _declare_io(g)
        _alloc(g)
        _consts(g)
        _load_x(g)
        for l in range(nlayers):
            _adaln(g, l)
            _norm_mod(g, l, 0)
            if mixers:
                if l % 2 == 0:
                    _even_mixer(g, l)
                else:
                    _odd_mixer(g, l)
            _norm_mod(g, l, 1)
            _ffn(g, l)
        _final(g)
        kb.wait_all("sp", g.outs)
        kb.wait_all("pool", g.outs)
        kb.wait_all("act", g.outs)
        kb.emit()
    return nc, g


def IN(g, name, shape, dt=F32):
    t = g.nc.dram_tensor(name, list(shape), dt, kind="ExternalInput")
    b = Buf(name, t)
    g.ins[name] = b
    return b


def OUT(g, name, shape, dt=F32):
    t = g.nc.dram_tensor(name, list(shape), dt, kind="ExternalOutput")
    b = Buf(name, t)
    g.outs.append(b)
    g.outd[name] = b
    return b


def _declare_io(g):
    g.ins = {}
    g.outs = []
    g.outd = {}
    IN(g, "xin", [NTOK, D])
    IN(g, "cvec", [128, KC, 2])
    IN(g, "w_ada", [4, D, 6 * D])
    IN(g, "b_ada", [128, 4, 48])
    IN(g, "norm_mix", [128, 4, 8])
    IN(g, "norm_ffn", [128, 4, 8])
    IN(g, "norm_final", [128, 8])
    IN(g, "w_in_even", [2, D, 3328])
    IN(g, "w_out_even", [2, D, D])
    IN(g, "w_qkv_diff", [2, D, 3072])
    IN(g, "w_qk_sw", [2, D, 2048])
    IN(g, "w_out_diff", [2, D, D])
    IN(g, "ffn_w1", [4, D, DFF])
    IN(g, "ffn_w3", [4, D, DFF])
    IN(g, "ffn_w2", [4, DFF, D])
    IN(g, "rope_c", [128, 1024])
    IN(g, "rope_s", [128, 1024])
    IN(g, "rw_mu", [128, 2, 17, 2])
    IN(g, "rw_w0", [128, 2, 2, 5])
    IN(g, "rw_a0", [128, 2, 2, 5])
    IN(g, "rw_kk", [128, 2, 5])
    IN(g, "rw_ka", [128, 2, 5])
    IN(g, "rw_lnw", [128, 2, 5])
    IN(g, "rw_lnb", [128, 2, 5])
    IN(g, "rw_bonus", [128, 2, 2, 5])
    IN(g, "rw_w2", [2, 2, 64, 640])
    IN(g, "rw_a2", [2, 2, 64, 640])
    IN(g, "rw_g2", [2, 128, 640])
    IN(g, "na_tab", [2, 2, 128, 5, 5, 128])
    IN(g, "cna_k", [2, 512, 128])
    IN(g, "cna_v", [2, 512, 128])
    IN(g, "st_rw", [2, 2, 64, 128])
    IN(g, "cdf_k", [2, 2, 512, 128])
    IN(g, "cdf_v", [2, 2, 512, 128])
    IN(g, "lamq", [128, 2, 128])
    IN(g, "lamk", [128, 2, 128])
    IN(g, "subln", [128, 2, 128])
    IN(g, "masks", [128, 4, 128])
    IN(g, "ident", [128, 128])
    IN(g, "lvlmask", [128, 7, 128])
    OUT(g, "y", [NTOK, D])
    OUT(g, "o_na_k", [2, 1024, 512])
    OUT(g, "o_na_v", [2, 1024, 512])
    OUT(g, "o_rw", [2, 4, 2, 4, 64, 128])
    OUT(g, "o_df_k", [2, 1024, 1024])
    OUT(g, "o_df_v", [2, 1024, 1024])
    for (name, shape, dt) in g.dbg:
        g.dbg_out[name] = OUT(g, "dbg_" + name, shape, dt)


class Rot:
    def __init__(self, bufs):
        self.bufs = bufs
        self.i = 0

    def __call__(self):
        b = self.bufs[self.i % len(self.bufs)]
        self.i += 1
        return b


def _alloc(g):
    kb = g.kb
    g.xT = kb.sb("xT", [128, KC, NTOK], F32)
    g.hT = kb.sb("hT", [128, KC, NTOK], BF16)
    g.PBL = [kb.ps(f"pb{i}", [128, 512], F32) for i in range(6)]
    g.nb = Rot(g.PBL)
    g.nbt = Rot([kb.ps(f"pt{i}", [128, 1024], BF16) for i in range(2)])
    g.nw = Rot([kb.sb(f"wbuf{i}", [128, KC * 512], BF16) for i in range(3)])
    g.mod = kb.sb("mod", [128, 48, 2], F32)
    g.modA = kb.sb("modA", [128, 2, 8, 2], F32)
    g.actb_flat = kb.sb("actb", [128, 4 * NTOK], BF16)
    g.actb = g.actb_flat
    g.scr = kb.sb("scr", [128, 22528], BF16)
    g.nstage = Rot([kb.sb(f"stage{i}", [128, 512], BF16) for i in range(3)])
    g.ntmp = Rot([kb.sb(f"tmpf{i}", [128, 512], F32) for i in range(3)])
    g.nR = Rot([kb.sb(f"Rb{i}", [128, 512], F32) for i in range(1)])
    g.nsmall = Rot([kb.sb(f"small{i}", [128, 8], F32) for i in range(4)])
    g.sm_neglam = kb.sb("neglam", [128, 1], F32)
    g.sm_gsub = kb.sb("gsub", [128, 128], F32)


def _consts(g):
    kb = g.kb
    I = g.ins
    g.ident_f = kb.sb("ident_f", [128, 128], F32)
    g.ident_b = kb.sb("ident_b", [128, 128], BF16)
    kb.dma("sp", g.ident_f[:, :], I["ident"][:, :], reads=[I["ident"]], writes=[g.ident_f])
    kb.dma("pool", g.ident_b[:, :], I["ident"][:, :], reads=[I["ident"]], writes=[g.ident_b])
    g.ones_b = kb.sb("ones_b", [128, 128], BF16)
    kb.I("pool", "memset", writes=[g.ones_b], ap=g.ones_b[:, :], constant=1.0)
    g.blk_b = kb.sb("blk_b", [128, 128], BF16)
    kb.I("pool", "memset", writes=[g.blk_b], ap=g.blk_b[:, :], constant=0.0)
    kb.I("pool", "memset", writes=[g.blk_b], ap=g.blk_b[0:64, 0:64], constant=1.0)
    kb.I("pool", "memset", writes=[g.blk_b], ap=g.blk_b[64:128, 64:128], constant=1.0)
    g.cv = kb.sb("cv", [128, KC, 2], F32)
    kb.dma("sp", g.cv[:, :, :], I["cvec"][:, :, :], reads=[I["cvec"]], writes=[g.cv])
    g.cvf = kb.sb("cvf", [128, KC, 2], F32)
    kb.I("act", "activation", reads=[g.cv], writes=[g.cvf], out=g.cvf[:, :, :], in_=g.cv[:, :, :], func=AF.Silu)
    g.b_ada = kb.sb("b_ada", [128, 4, 48], F32)
    kb.dma("sp", g.b_ada[:, :, :], I["b_ada"][:, :, :], reads=[I["b_ada"]], writes=[g.b_ada])
    g.nrm = kb.sb("nrm", [128, 2, 4, 8], F32)
    kb.dma("sp", g.nrm[:, 0, :, :], I["norm_mix"][:, :, :], reads=[I["norm_mix"]], writes=[g.nrm])
    kb.dma("sp", g.nrm[:, 1, :, :], I["norm_ffn"][:, :, :], reads=[I["norm_ffn"]], writes=[g.nrm])
    g.nrmf = kb.sb("nrmf", [128, 8], F32)
    kb.dma("sp", g.nrmf[:, :], I["norm_final"][:, :], reads=[I["norm_final"]], writes=[g.nrmf])
    g.eps_t = kb.sb("eps_t", [128, 2], F32)
    kb.I("pool", "memset", writes=[g.eps_t], ap=g.eps_t[:, 0:1], constant=1e-6)
    kb.I("pool", "memset", writes=[g.eps_t], ap=g.eps_t[:, 1:2], constant=64e-5)


def dbg_dump(g, name, ap, buf, key=None):
    if name in g.dbg_out:
        o = g.dbg_out[name]
        g.kb.dma("sp", o.t.ap(), ap, reads=[(buf, key)], writes=[o])


def evac(g, i, out, in_, reads, writes):
    if i % 2 == 0:
        g.kb.I("dve", "tensor_copy", reads=reads, writes=writes, out=out, in_=in_)
    else:
        g.kb.I("act", "copy", reads=reads, writes=writes, out=out, in_=in_)


def _load_x(g):
    kb = g.kb
    X = g.ins["xin"]
    for tb in range(NTOK // 128):
        xt = g.ntmp()
        xt2 = g.ntmp()
        kb.dma("sp", xt[:, :], X[tb * 128:(tb + 1) * 128, 0:512], reads=[X], writes=[xt])
        kb.dma("act", xt2[:, :], X[tb * 128:(tb + 1) * 128, 512:1024], reads=[X], writes=[xt2])
        for half, src in ((0, xt), (1, xt2)):
            pb = g.nb()
            for j in range(4):
                kb.TR(pb[:, j * 128:(j + 1) * 128], src[:, j * 128:(j + 1) * 128], g.ident_f[:, :], reads=[src, g.ident_f], writes=[pb])
            dst = g.xT[:, half * 4:(half + 1) * 4, tb * 128:(tb + 1) * 128]
            evac(g, half, dst, pb[:, :].rearrange("p (j t) -> p j t", j=4), [pb], [(g.xT, tb // 4)])


def wload(g, src_buf, ap, kc, wdt):
    wb = g.nw()
    view = wb[:, 0:kc * wdt].rearrange("p (k n) -> p k n", k=kc)
    g.kb.dma("pool", view, ap, reads=[src_buf], writes=[wb])
    return wb, view


def wcols(W, idx, c0, c1):
    return W[idx].rearrange("(kc p) n -> p kc n", p=128)[:, :, c0:c1]


def _adaln(g, l):
    kb = g.kb
    W = g.ins["w_ada"]
    pb = g.nb()
    for gi, c0 in enumerate(range(0, 6 * D, 256)):
        wb = g.nw()
        wv = wb[:, 0:KC * 512].bitcast(F32).rearrange("p (k n) -> p k n", k=KC)
        kb.dma("sp" if gi % 2 == 0 else "act", wv, wcols(W, l, c0, c0 + 256), reads=[W], writes=[wb])
        for j in range(2):
            blk = c0 // 128 + j
            for kc in range(KC):
                kb.MM(pb[:, blk * 2:(blk + 1) * 2], wv[:, kc, j * 128:(j + 1) * 128], g.cvf[:, kc, :], kc == 0, kc == KC - 1, reads=[wb, g.cvf], writes=[pb])
    kb.I("dve", "tensor_tensor", reads=[pb, g.b_ada], writes=[g.mod], out=g.mod[:, :, :], in0=pb[:, 0:96].rearrange("p (b j) -> p b j", j=2),
         in1=g.b_ada[:, l, :].unsqueeze(2).broadcast_to([128, 48, 2]), op=ALU.add)
    for which in range(2):
        sc0 = 8 if which == 0 else 32
        kb.I("dve", "scalar_tensor_tensor", reads=[g.mod, g.nrm], writes=[g.modA], out=g.modA[:, which, :, :], in0=g.mod[:, sc0:sc0 + 8, :], scalar=1.0,
             in1=g.nrm[:, which, l, :].unsqueeze(2).broadcast_to([128, 8, 2]), op0=ALU.add, op1=ALU.mult)


def _rms_scale(g, ti, R):
    kb = g.kb
    t0 = ti * TT
    pb = g.nb()
    for kc in range(KC):
        sq = g.nstage()
        kb.I("act", "activation", reads=[(g.xT, ti)], writes=[sq], out=sq[:, :], in_=g.xT[:, kc, t0:t0 + TT], func=AF.Square)
        kb.MM(pb[:, :], g.ones_b[:, :], sq[:, :], kc == 0, kc == KC - 1, reads=[sq, g.ones_b], writes=[pb])
    kb.I("act", "activation", reads=[pb, g.eps_t], writes=[R], out=R[:, :], in_=pb[:, :], func=AF.Sqrt, scale=1.0 / D, bias=g.eps_t[:, 0:1])
    kb.I("dve", "reciprocal", reads=[R], writes=[R], out=R[:, :], in_=R[:, :])


def _norm_mod(g, l, which):
    kb = g.kb
    sh0 = 0 if which == 0 else 24
    for ti in range(NT):
        t0 = ti * TT
        grp = 0 if ti < 2 else 1
        R = g.nR()
        _rms_scale(g, ti, R)
        for kc in range(KC):
            tmp = g.ntmp()
            kb.I("dve", "tensor_tensor", reads=[(g.xT, ti), R], writes=[tmp], out=tmp[:, :], in0=g.xT[:, kc, t0:t0 + TT], in1=R[:, :], op=ALU.mult)
            kb.I("act", "activation", reads=[tmp, g.modA, g.mod], writes=[(g.hT, ti)], out=g.hT[:, kc, t0:t0 + TT], in_=tmp[:, :], func=AF.Identity,
                 scale=g.modA[:, which, kc, grp:grp + 1], bias=g.mod[:, sh0 + kc, grp:grp + 1])


def _ffn(g, l):
    kb = g.kb
    W1, W3, W2 = g.ins["ffn_w1"], g.ins["ffn_w3"], g.ins["ffn_w2"]
    actv = g.actb_flat[:, :].rearrange("p (b t) -> p b t", b=4)
    for c0 in range(0, DFF, 512):
        wdt = min(512, DFF - c0)
        nblk = wdt // 128
        w1b, w1v = wload(g, W1, wcols(W1, l, c0, c0 + wdt), KC, wdt)
        w3b, w3v = wload(g, W3, wcols(W3, l, c0, c0 + wdt), KC, wdt)
        for j in range(nblk):
            for ti in range(NT):
                t0 = ti * TT
                pa = g.nb()
                pbb = g.nb()
                for kc in range(KC):
                    kb.MM(pa[:, :], w1v[:, kc, j * 128:(j + 1) * 128], g.hT[:, kc, t0:t0 + TT], kc == 0, kc == KC - 1, reads=[w1b, (g.hT, ti)], writes=[pa])
                for kc in range(KC):
                    kb.MM(pbb[:, :], w3v[:, kc, j * 128:(j + 1) * 128], g.hT[:, kc, t0:t0 + TT], kc == 0, kc == KC - 1, reads=[w3b, (g.hT, ti)], writes=[pbb])
                sa = g.ntmp()
                kb.I("act", "activation", reads=[pa], writes=[sa], out=sa[:, :], in_=pa[:, :], func=AF.Silu)
                kb.I("dve", "tensor_tensor", reads=[sa, pbb], writes=[(g.actb, ti)], out=actv[:, j, t0:t0 + TT], in0=sa[:, :], in1=pbb[:, :], op=ALU.mult)
        w2b = g.nw()
        w2v = w2b[:, 0:nblk * 1024].rearrange("p (k n) -> p k n", k=nblk)
        kb.dma("pool", w2v, W2[l, c0:c0 + wdt, :].rearrange("(kc p) n -> p kc n", p=128), reads=[W2], writes=[w2b])
        for ob in range(8):
            for ti in range(NT):
                t0 = ti * TT
                grp = 0 if ti < 2 else 1
                py = g.nb()
                for j in range(nblk):
                    kb.MM(py[:, :], w2v[:, j, ob * 128:(ob + 1) * 128], actv[:, j, t0:t0 + TT], j == 0, j == nblk - 1, reads=[w2b, (g.actb, ti)], writes=[py])
                kb.I("dve", "scalar_tensor_tensor", reads=[py, g.mod, (g.xT, ti)], writes=[(g.xT, ti)], out=g.xT[:, ob, t0:t0 + TT], in0=py[:, :],
                     scalar=g.mod[:, 40 + ob, grp:grp + 1], in1=g.xT[:, ob, t0:t0 + TT], op0=ALU.mult, op1=ALU.add)


def _final(g):
    kb = g.kb
    Y = g.outd["y"]
    for ti in range(NT):
        t0 = ti * TT
        R = g.nR()
        _rms_scale(g, ti, R)
        for kc in range(KC):
            kb.I("dve", "scalar_tensor_tensor", reads=[(g.xT, ti), R, g.nrmf], writes=[(g.xT, ti)], out=g.xT[:, kc, t0:t0 + TT], in0=g.xT[:, kc, t0:t0 + TT],
                 scalar=g.nrmf[:, kc:kc + 1], in1=R[:, :], op0=ALU.mult, op1=ALU.mult)
        for tb in range(4):
            tt0 = t0 + tb * 128
            for half in range(2):
                pb = g.nb()
                for j in range(4):
                    kc = half * 4 + j
                    kb.TR(pb[:, j * 128:(j + 1) * 128], g.xT[:, kc, tt0:tt0 + 128], g.ident_f[:, :], reads=[(g.xT, ti), g.ident_f], writes=[pb])
                o = g.ntmp()
                evac(g, half, o[:, :], pb[:, :], [pb], [o])
                kb.dma("sp", Y[tt0:tt0 + 128, half * 512:(half + 1) * 512], o[:, :], reads=[o], writes=[(Y, (tt0, half))])


def carve(buf, off_bytes, shape, dtype):
    n = 1
    for s in shape[1:]:
        n *= s
    esz = 4 if dtype == F32 else 2
    a = buf[:, off_bytes // 2: off_bytes // 2 + n * esz // 2]
    if dtype == F32:
        a = a.bitcast(F32)
    if len(shape) == 2:
        return a
    names = " ".join(f"d{i}" for i in range(1, len(shape)))
    kw = {f"d{i}": shape[i] for i in range(1, len(shape) - 1)}
    return a.rearrange(f"p ({names}) -> p {names}", **kw)


_RANKC = {}


def rank_expr(e):
    k = id(e)
    if k not in _RANKC:
        _RANKC.clear()
        _RANKC[k] = e.snap(e.partition_id() % 4)
    return _RANKC[k]


class Xchg:
    def __init__(self, g, name, nb):
        kb = g.kb
        self.g, self.nb, self.h = g, nb, nb // 2
        self.gin = kb.dram(name + "_gin", [4, nb, 128, 1024], BF16)
        self.gout = kb.dram(name + "_gout", [4, 2, 4, self.h * 128, 1024], BF16)
        self.mine = kb.dram(name + "_mine", [2, 4, self.h * 128, 1024], BF16)

    def row(self, rk, j):
        return self.gin[rk, j]

    def run(self):
        kb, h = self.g.kb, self.h
        for rk in range(4):
            for part in range(2):
                i_ap = self.gin[rk, part * h:(part + 1) * h].rearrange("a p t -> (a p) t")
                o_ap = self.gout[rk, part].rearrange("r q t -> (r q) t")
                kb.custom("pool", (lambda en, i_ap=i_ap, o_ap=o_ap: en.collective_compute("AllGather", ALU.bypass, replica_groups=GROUPS4, ins=[i_ap.opt()], outs=[o_ap.opt()])), 1,
                          reads=[self.gin], writes=[self.gout])
        for part in range(2):
            src = self.gout.t.ap()
            kb.dma("pool", self.mine[part].rearrange("r q t -> r (q t)"),
                   (lambda en, part=part, src=src: src[bass.ds(rank_expr(en), 1), part].rearrange("a r q t -> (a r) (q t)")), reads=[self.gout], writes=[self.mine])

    def get(self, rr, j):
        return self.mine[j // self.h, rr, (j % self.h) * 128:(j % self.h + 1) * 128, :]


class XchgOut:
    def __init__(self, g, name):
        kb = g.kb
        self.g = g
        self.gin = kb.dram(name + "_gin", [2, 128, 4096], BF16)
        self.gout = kb.dram(name + "_gout", [2, 4, 128, 4096], BF16)
        self.mine = kb.dram(name + "_mine", [1024, 1024], BF16)

    def run(self):
        kb = self.g.kb
        for w in range(2):
            i_ap = self.gin[w]
            o_ap = self.gout[w].rearrange("r p t -> (r p) t")
            kb.custom("pool", (lambda en, i_ap=i_ap, o_ap=o_ap: en.collective_compute("AllGather", ALU.bypass, replica_groups=GROUPS4, ins=[i_ap.opt()], outs=[o_ap.opt()])), 1,
                      reads=[self.gin], writes=[self.gout])
        src = self.gout.t.ap().rearrange("w r p (k t) -> (w r p) k t", k=4)
        kb.dma("pool", self.mine.t.ap(), (lambda en: src[:, bass.ds(rank_expr(en), 1), :].rearrange("f a t -> f (a t)")), reads=[self.gout], writes=[self.mine])


def out_proj(g, W, widx, gate0):
    kb = g.kb
    for c0 in range(0, D, 512):
        wb, wv = wload(g, W, wcols(W, widx, c0, c0 + 512), KC, 512)
        for j in range(4):
            ob = c0 // 128 + j
            for ti in range(NT):
                t0 = ti * TT
                grp = 0 if ti < 2 else 1
                py = g.nb()
                for kc in range(KC):
                    kb.MM(py[:, :], wv[:, kc, j * 128:(j + 1) * 128], g.hT[:, kc, t0:t0 + TT], kc == 0, kc == KC - 1, reads=[wb, (g.hT, ti)], writes=[py])
                kb.I("dve", "scalar_tensor_tensor", reads=[py, g.mod, (g.xT, ti)], writes=[(g.xT, ti)], out=g.xT[:, ob, t0:t0 + TT], in0=py[:, :],
                     scalar=g.mod[:, gate0 + ob, grp:grp + 1], in1=g.xT[:, ob, t0:t0 + TT], op0=ALU.mult, op1=ALU.add)


def tokmajor_out(g, wb, wv, ncols, OUTB, oidx, col0, extra=None):
    kb = g.kb
    for tb in range(8):
        pb = g.nb()
        for kc in range(KC):
            kb.MM(pb[:, 0:ncols], g.hT[:, kc, tb * 128:(tb + 1) * 128], wv[:, kc, 0:ncols], kc == 0, kc == KC - 1, reads=[wb, (g.hT, tb // 4)], writes=[pb])
        o32 = g.ntmp()
        evac(g, tb, o32[:, 0:ncols], pb[:, 0:ncols], [pb], [o32])
        kb.dma("sp", OUTB[oidx, tb * 128:(tb + 1) * 128, col0:col0 + ncols], o32[:, 0:ncols], reads=[o32], writes=[(OUTB, (oidx, tb, col0))])
        if extra is not None:
            extra(tb, o32)


def diff_combine(g, A0, B0, A1, B1, dst, neglam, gsub):
    kb = g.kb
    st = g.nsmall()
    kb.I("dve", "reciprocal", reads=[B0], writes=[st], out=st[:, 0:1], in_=A0[:, 128:129])
    kb.I("dve", "reciprocal", reads=[B1], writes=[st], out=st[:, 1:2], in_=A1[:, 128:129])
    kb.I("dve", "tensor_tensor", reads=[st, neglam], writes=[st], out=st[:, 2:3], in0=st[:, 1:2], in1=neglam[:, 0:1], op=ALU.mult)
    O = g.ntmp()
    kb.I("dve", "tensor_scalar", reads=[B0, st], writes=[O], out=O[:, 0:128], in0=A0[:, 0:128], scalar1=st[:, 0:1], scalar2=None, op0=ALU.mult)
    kb.I("dve", "scalar_tensor_tensor", reads=[B1, st, O], writes=[O], out=O[:, 0:128], in0=A1[:, 0:128], scalar=st[:, 2:3], in1=O[:, 0:128], op0=ALU.mult, op1=ALU.add)
    kb.I("act", "activation", reads=[O], writes=[O, st], out=O[:, 128:256], in_=O[:, 0:128], func=AF.Square, accum_out=st[:, 3:4])
    kb.I("act", "activation", reads=[st, g.eps_t], writes=[st], out=st[:, 4:5], in_=st[:, 3:4], func=AF.Sqrt, scale=1.0 / 128, bias=g.eps_t[:, 0:1])
    kb.I("dve", "reciprocal", reads=[st], writes=[st], out=st[:, 5:6], in_=st[:, 4:5])
    kb.I("dve", "scalar_tensor_tensor", reads=[O, st, gsub], writes=[dst[1]], out=dst[0], in0=O[:, 0:128], scalar=st[:, 5:6], in1=gsub[:, :], op0=ALU.mult, op1=ALU.mult)


def _odd_mixer(g, l):
    kb = g.kb
    I = g.ins
    o = l // 2
    lam_init = 0.8 - 0.6 * math.exp(-0.3 * l)
    W, Wsw, Wout = I["w_qkv_diff"], I["w_qk_sw"], I["w_out_diff"]
    KO, VO = g.outd["o_df_k"], g.outd["o_df_v"]
    pinP = kb.dram(f"pinP{l}", [3072, 1024], BF16)
    XS = Xchg(g, f"oxs{l}", 6)
    XO = XchgOut(g, f"oxo{l}")
    scr = g.scr
    neglam = g.sm_neglam
    gsub = g.sm_gsub
    t = g.ntmp()
    kb.dma("sp", t[:, 256:384], I["lamq"][:, o, :], reads=[I["lamq"]], writes=[t])
    kb.dma("sp", t[:, 384:512], I["lamk"][:, o, :], reads=[I["lamk"]], writes=[t])
    kb.I("dve", "tensor_tensor", reads=[t], writes=[t], out=t[:, 0:128], in0=t[:, 256:384], in1=t[:, 384:512], op=ALU.mult)
    kb.I("dve", "tensor_reduce", reads=[t], writes=[t], out=t[:, 128:130], in_=t[:, 0:128].rearrange("p (a b) -> p a b", a=2), axis=AX.X, op=ALU.add)
    kb.I("act", "activation", reads=[t], writes=[t], out=t[:, 130:132], in_=t[:, 128:130], func=AF.Exp)
    kb.I("dve", "tensor_tensor", reads=[t], writes=[neglam], out=neglam[:, 0:1], in0=t[:, 131:132], in1=t[:, 130:131], op=ALU.subtract)
    kb.I("dve", "tensor_scalar", reads=[neglam], writes=[neglam], out=neglam[:, 0:1], in0=neglam[:, 0:1], scalar1=-lam_init, scalar2=None, op0=ALU.add)
    kb.dma("sp", gsub[:, :], I["subln"][:, o, :], reads=[I["subln"]], writes=[gsub])
    kb.I("dve", "tensor_scalar", reads=[gsub], writes=[gsub], out=gsub[:, :], in0=gsub[:, :], scalar1=1.0 - lam_init, scalar2=None, op0=ALU.mult)
    ropeC = carve(g.actb_flat, 0, [128, 1024], F32)
    ropeS = carve(g.actb_flat, 4096, [128, 1024], F32)
    kb.dma("sp", ropeC, I["rope_c"][:, :], reads=[I["rope_c"]], writes=[g.actb_flat])
    kb.dma("sp", ropeS, I["rope_s"][:, :], reads=[I["rope_s"]], writes=[g.actb_flat])
    vtokP = carve(scr, 0, [128, 8, 8, 129], BF16)
    kb.I("pool", "memset", writes=[(scr, "vtokP")], ap=vtokP[:, :, :, 128:129], constant=1.0)
    ev = 0
    for gi in range(6):
        c0 = gi * 512
        wb, wv = wload(g, W, wcols(W, o, c0, c0 + 512), KC, 512)
        if gi < 4:
            wsb, wsv = wload(g, Wsw, wcols(Wsw, o, c0, c0 + 512), KC, 512)
        for j in range(4):
            blk = gi * 4 + j
            for ti in range(NT):
                t0 = ti * TT
                pb = g.nb()
                for kc in range(KC):
                    kb.MM(pb[:, :], wv[:, kc, j * 128:(j + 1) * 128], g.hT[:, kc, t0:t0 + TT], kc == 0, kc == KC - 1, reads=[wb, (g.hT, ti)], writes=[pb])
                stg = g.nstage()
                if ti < 2 or gi >= 4:
                    evac(g, ev, stg[:, :], pb[:, :], [pb], [stg]); ev += 1
                else:
                    pb2 = g.nb()
                    for kc in range(KC):
                        kb.MM(pb2[:, :], wsv[:, kc, j * 128:(j + 1) * 128], g.hT[:, kc, t0:t0 + TT], kc == 0, kc == KC - 1, reads=[wsb, (g.hT, ti)], writes=[pb2])
                    cs = (ti - 2) * 512
                    t1 = g.ntmp(); t2 = g.ntmp()
                    kb.I("dve", "tensor_tensor", reads=[pb, g.actb_flat], writes=[t1], out=t1[:, :], in0=pb[:, :], in1=ropeC[:, cs:cs + 512], op=ALU.mult)
                    kb.I("dve", "tensor_tensor", reads=[pb2, g.actb_flat], writes=[t2], out=t2[:, :], in0=pb2[:, :], in1=ropeS[:, cs:cs + 512], op=ALU.mult)
                    kb.I("pool", "tensor_tensor", reads=[t1, t2], writes=[stg], out=stg[:, :], in0=t1[:, :], in1=t2[:, :], op=ALU.add)
                cc = (ti % 2) * 512
                if ti < 2:
                    kb.dma("sp", pinP[blk * 128:(blk + 1) * 128, cc:cc + 512], stg[:, :], reads=[stg], writes=[(pinP, (blk, ti))])
                else:
                    kind, H = blk // 8, blk % 8
                    kb.dma("sp", XS.row(H // 2, kind * 2 + H % 2)[:, cc:cc + 512], stg[:, :], reads=[stg], writes=[(XS.gin, (blk, ti))])
        if gi >= 2:
            if gi < 4:
                tokmajor_out(g, wb, wv, 512, KO, o, (gi - 2) * 512)
            else:
                h0 = (gi - 4) * 4

                def extra(tb, o32, h0=h0):
                    kb.I("pool", "tensor_copy", reads=[o32], writes=[(scr, "vtokP")], out=vtokP[:, tb, h0:h0 + 4, 0:128], in_=o32[:, :].rearrange("p (h d) -> p h d", h=4))
                tokmajor_out(g, wb, wv, 512, VO, o, (gi - 4) * 512, extra)
    import os
    STOP = os.environ.get("ODDSTOP", "")
    XS.run()
    if STOP == "proj":
        return
    qk = carve(scr, 16512, [128, 2, 8, 256], BF16)
    pTs = [carve(scr, 24704 + i * 1024, [128, 512], BF16) for i in range(2)]
    Otok = carve(scr, 26752, [128, 2, 1024], BF16)
    for b in range(4):
        for w in range(2):
            kb.dma("sp", qk[:, w, :, :], pinP[w * 1024:(w + 1) * 1024, b * 256:(b + 1) * 256].rearrange("(h p) t -> p h t", p=128), reads=[pinP], writes=[(scr, "qk")])
        for h in range(8):
            pos = []
            for s in range(2):
                ps = g.nb()
                for kc2 in range(2):
                    kb.MM(ps[:, kc2 * 256:(kc2 + 1) * 256], qk[64 * s:64 * s + 64, 1, h, kc2 * 128:(kc2 + 1) * 128], qk[64 * s:64 * s + 64, 0, h, :], True, True,
                          reads=[(scr, "qk")], writes=[ps])
                pT = pTs[s]
                kb.I("act", "activation", reads=[ps], writes=[(scr, ("pT", s))], out=pT, in_=ps[:, :], func=AF.Exp, scale=SCALE)
                po = g.nb()
                for qb in range(2):
                    for kc2 in range(2):
                        kb.MM(po[:, qb * 129:(qb + 1) * 129], pT[:, kc2 * 256 + qb * 128:kc2 * 256 + (qb + 1) * 128], vtokP[:, b * 2 + kc2, h, :], kc2 == 0, kc2 == 1,
                              reads=[(scr, ("pT", s)), (scr, "vtokP")], writes=[po])
                pos.append(po)
            for qb in range(2):
                diff_combine(g, pos[0][:, qb * 129:(qb + 1) * 129], pos[0], pos[1][:, qb * 129:(qb + 1) * 129], pos[1],
                             (Otok[:, qb, h * 128:(h + 1) * 128], (scr, "Otok")), neglam, gsub)
        for qb in range(2):
            pt = g.nbt()
            for blk in range(8):
                kb.TR(pt[:, blk * 128:(blk + 1) * 128], Otok[:, qb, blk * 128:(blk + 1) * 128], g.ident_b[:, :], reads=[(scr, "Otok"), g.ident_b], writes=[pt])
            evac(g, qb, g.hT[:, :, b * 256 + qb * 128:b * 256 + (qb + 1) * 128], pt[:, :].rearrange("p (k t) -> p k t", k=8), [pt], [(g.hT, b // 2)])
    if STOP == "prompt":
        return
    QT = carve(scr, 0, [128, 4096], BF16)
    KT = carve(scr, 8192, [128, 4608], BF16)
    Vtok = carve(scr, 17408, [128, 36, 129], BF16)
    pTs = [carve(scr, 26696 + i * 1024, [128, 512], BF16) for i in range(2)]
    OtS = carve(scr, 28744, [128, 32, 128], BF16)
    VTt = carve(scr, 28744, [128, 4096], BF16)
    O1s = carve(scr, 36936, [128, 4, 129], F32)
    ACC = g.PBL[0:4]
    PSR = Rot(g.PBL[4:6])
    for hh in range(2):
        allS = [(scr, None)]
        for rr in range(4):
            for (dst, kind, nm) in ((QT, 0, "QT"), (KT, 1, "KT"), (VTt, 2, "VT")):
                off = 512 if nm == "KT" else 0
                kb.dma("sp" if rr % 2 == 0 else "act", dst[:, off + rr * 1024:off + (rr + 1) * 1024], XS.get(rr, kind * 2 + hh), reads=[XS.mine], writes=allS)
        kst = g.nstage()
        kstv = kst[:, :].rearrange("p (c d) -> p c d", c=4)
        kb.dma("pool", kstv, I["cdf_k"][o, hh].rearrange("(c p) d -> p c d", p=128), reads=[I["cdf_k"]], writes=[kst])
        pt = g.nbt()
        for c in range(4):
            kb.TR(pt[:, c * 128:(c + 1) * 128], kstv[:, c, :], g.ident_b[:, :], reads=[kst, g.ident_b], writes=[pt])
        evac(g, 0, KT[:, 0:512], pt[:, 0:512], [pt], allS)
        vst = g.nstage()
        vstv = vst[:, :].rearrange("p (c d) -> p c d", c=4)
        kb.dma("pool", vstv, I["cdf_v"][o, hh].rearrange("(c p) d -> p c d", p=128), reads=[I["cdf_v"]], writes=[vst])
        kb.I("pool", "tensor_copy", reads=[vst], writes=allS, out=Vtok[:, 0:4, 0:128], in_=vstv)
        kb.I("pool", "memset", writes=allS, ap=Vtok[:, :, 128:129], constant=1.0)
        for c8 in range(4):
            pt = g.nbt()
            for j in range(8):
                c = c8 * 8 + j
                kb.TR(pt[:, j * 128:(j + 1) * 128], VTt[:, c * 128:(c + 1) * 128], g.ident_b[:, :], reads=allS + [g.ident_b], writes=[pt])
            evac(g, c8, Vtok[:, 4 + c8 * 8:4 + (c8 + 1) * 8, 0:128], pt[:, :].rearrange("p (c d) -> p c d", c=8), [pt], allS)
        if STOP == "sload":
            return
        for qg in range(8 if STOP != "sattn1" else 1):
            for s in range(2):
                for kc in range(36):
                    ps = PSR()
                    kb.MM(ps[:, :], KT[64 * s:64 * s + 64, kc * 128:(kc + 1) * 128], QT[64 * s:64 * s + 64, qg * 512:(qg + 1) * 512], True, True, reads=allS, writes=[ps])
                    pT = pTs[kc % 2]
                    kb.I("act", "activation", reads=[ps], writes=[(scr, ("pTs", kc % 2))], out=pT, in_=ps[:, :], func=AF.Exp, scale=SCALE)
                    for qb in range(4):
                        kb.MM(ACC[qb][:, 0:129], pT[:, qb * 128:(qb + 1) * 128], Vtok[:, kc, :], kc == 0, kc == 35,
                              reads=[(scr, ("pTs", kc % 2)), (scr, "static")], writes=[ACC[qb]])
                if s == 0:
                    for qb in range(4):
                        evac(g, qb, O1s[:, qb, :], ACC[qb][:, 0:129], [ACC[qb]], [(scr, ("O1s", qb))])
            for qb in range(4):
                diff_combine(g, O1s[:, qb, :], (scr, ("O1s", qb)), ACC[qb][:, 0:129], ACC[qb], (OtS[:, qg * 4 + qb, :], (scr, ("OtS", qg))), neglam, gsub)
        oTs = QT
        for c8 in range(4):
            pt = g.nbt()
            for j in range(8):
                c = c8 * 8 + j
                kb.TR(pt[:, j * 128:(j + 1) * 128], OtS[:, c, :], g.ident_b[:, :], reads=[(scr, None), g.ident_b], writes=[pt])
            evac(g, c8, oTs[:, c8 * 1024:(c8 + 1) * 1024], pt[:, :], [pt], [(scr, None)])
        kb.dma("sp", XO.gin[hh], oTs[:, :], reads=[(scr, None)], writes=[(XO.gin, hh)])
    XO.run()
    mo = XO.mine.t.ap().rearrange("(w rr p) t -> p w rr t", w=2, rr=4)
    for w in range(2):
        kb.dma("sp" if w == 0 else "act", g.hT[:, :, 1024:2048].rearrange("p (rr w) t -> p w rr t", w=2)[:, w], mo[:, w], reads=[XO.mine], writes=[(g.hT, 2), (g.hT, 3)])
    out_proj(g, Wout, o, 16)


def _even_mixer(g, l):
    kb = g.kb
    I = g.ins
    e = l // 2
    W, Wout = I["w_in_even"], I["w_out_even"]
    KO, VO = g.outd["o_na_k"], g.outd["o_na_v"]
    pinP = kb.dram(f"epinP{l}", [3328, 1024], BF16)
    XS = Xchg(g, f"exs{l}", 6)
    XO = XchgOut(g, f"exo{l}")
    lin = kb.dram(f"elin{l}", [256, 1024], BF16)
    lout = kb.dram(f"elout{l}", [4, 256, 1024], BF16)
    scr = g.scr
    vtokP = carve(scr, 0, [128, 8, 8, 65], BF16)
    kb.I("pool", "memset", writes=[(scr, "vtokP")], ap=vtokP[:, :, :, 64:65], constant=1.0)
    ev = 0
    for gi, c0 in enumerate(range(0, 3328, 512)):
        wdt = min(512, 3328 - c0)
        wb, wv = wload(g, W, wcols(W, e, c0, c0 + wdt), KC, wdt)
        for j in range(wdt // 128):
            blk = c0 // 128 + j
            for ti in range(NT):
                t0 = ti * TT
                pb = g.nb()
                for kc in range(KC):
                    kb.MM(pb[:, :], wv[:, kc, j * 128:(j + 1) * 128], g.hT[:, kc, t0:t0 + TT], kc == 0, kc == KC - 1, reads=[wb, (g.hT, ti)], writes=[pb])
                stg = g.nstage()
                evac(g, ev, stg[:, :], pb[:, :], [pb], [stg]); ev += 1
                cc = (ti % 2) * 512
                if ti < 2:
                    kb.dma("sp", pinP[blk * 128:(blk + 1) * 128, cc:cc + 512], stg[:, :], reads=[stg], writes=[(pinP, (blk, ti))])
                elif blk < 24:
                    kb.dma("sp", XS.row(blk % 4, blk // 4)[:, cc:cc + 512], stg[:, :], reads=[stg], writes=[(XS.gin, (blk, ti))])
                else:
                    kb.dma("sp", lin[(blk - 24) * 128:(blk - 23) * 128, cc:cc + 512], stg[:, :], reads=[stg], writes=[(lin, (blk, ti))])
        if gi == 1:
            tokmajor_out(g, wb, wv, 512, KO, e, 0)
        if gi == 2:
            def extra(tb, o32):
                kb.I("pool", "tensor_copy", reads=[o32], writes=[(scr, "vtokP")], out=vtokP[:, tb, :, 0:64], in_=o32[:, :].rearrange("p (h d) -> p h d", h=8))
            tokmajor_out(g, wb, wv, 512, VO, e, 0, extra)
    XS.run()
    kb.custom("pool", (lambda en: en.collective_compute("AllGather", ALU.bypass, replica_groups=GROUPS4, ins=[lin.t.ap().opt()], outs=[lout.t.ap().rearrange("r q t -> (r q) t").opt()])), 1,
              reads=[lin], writes=[lout])

    qk = carve(scr, 8320, [128, 2, 4, 256], BF16)
    pTp = [carve(scr, 12416 + i * 1024, [128, 512], BF16) for i in range(2)]
    Otok = carve(scr, 14464, [128, 2, 512], BF16)
    for b in range(4):
        kb.dma("sp", qk, pinP[0:1024, b * 256:(b + 1) * 256].rearrange("(w j p) t -> p w j t", w=2, j=4), reads=[pinP], writes=[(scr, "qk")])
        for h in range(8):
            j, hb = h // 2, 64 * (h % 2)
            ps = g.nb()
            for kc2 in range(2):
                kb.MM(ps[:, kc2 * 256:(kc2 + 1) * 256], qk[hb:hb + 64, 1, j, kc2 * 128:(kc2 + 1) * 128], qk[hb:hb + 64, 0, j, :], True, True, reads=[(scr, "qk")], writes=[ps])
            pT = pTp[h % 2]
            kb.I("act", "activation", reads=[ps], writes=[(scr, ("pTp", h % 2))], out=pT, in_=ps[:, :], func=AF.Exp, scale=SCALE)
            po = g.nb()
            for qb in range(2):
                for kc2 in range(2):
                    kb.MM(po[:, qb * 65:(qb + 1) * 65], pT[:, kc2 * 256 + qb * 128:kc2 * 256 + (qb + 1) * 128], vtokP[:, b * 2 + kc2, h, :], kc2 == 0, kc2 == 1,
                          reads=[(scr, ("pTp", h % 2)), (scr, "vtokP")], writes=[po])
            st = g.nsmall()
            kb.I("dve", "reciprocal", reads=[po], writes=[st], out=st[:, 0:2], in_=po[:, 0:130].rearrange("p (q c) -> p q c", q=2)[:, :, 64])
            for qb in range(2):
                kb.I("dve", "tensor_scalar", reads=[po, st], writes=[(scr, "Otok")], out=Otok[:, qb, h * 64:(h + 1) * 64], in0=po[:, qb * 65:qb * 65 + 64],
                     scalar1=st[:, qb:qb + 1], scalar2=None, op0=ALU.mult)
        for qb in range(2):
            pt = g.nbt()
            for blk in range(4):
                kb.TR(pt[:, blk * 128:(blk + 1) * 128], Otok[:, qb, blk * 128:(blk + 1) * 128], g.ident_b[:, :], reads=[(scr, "Otok"), g.ident_b], writes=[pt])
            evac(g, qb, g.hT[:, 0:4, b * 256 + qb * 128:b * 256 + (qb + 1) * 128], pt[:, 0:512].rearrange("p (k t) -> p k t", k=4), [pt], [(g.hT, b // 2)])

    allS = [(scr, None)]
    QT = carve(scr, 0, [128, 4096], BF16)
    KT = carve(scr, 8192, [128, 4608], BF16)
    VtokS = carve(scr, 17408, [128, 2, 36, 65], BF16)
    pTL = [carve(scr, 26768 + i * 1280, [128, 640], BF16) for i in range(2)]
    pTC = [carve(scr, 29328 + i * 1024, [128, 512], BF16) for i in range(2)]
    tmpL = [carve(scr, 31376 + i * 2560, [128, 640], F32) for i in range(2)]
    OtS = carve(scr, 36496, [128, 32, 128], BF16)
    VTt = carve(scr, 36496, [128, 4096], BF16)
    tabv = carve(g.actb_flat, 0, [128, 5, 5, 128], F32)
    for rr in range(4):
        for (dst, kind) in ((QT, 0), (KT, 1), (VTt, 2)):
            kb.dma("sp" if rr % 2 == 0 else "act", dst[:, rr * 1024:(rr + 1) * 1024], XS.get(rr, kind), reads=[XS.mine], writes=allS)
    kst = g.nstage()
    kstv = kst[:, :].rearrange("p (c d) -> p c d", c=4)
    kb.dma("pool", kstv, I["cna_k"][e].rearrange("(c p) d -> p c d", p=128), reads=[I["cna_k"]], writes=[kst])
    pt = g.nbt()
    for c in range(4):
        kb.TR(pt[:, c * 128:(c + 1) * 128], kstv[:, c, :], g.ident_b[:, :], reads=[kst, g.ident_b], writes=[pt])
    evac(g, 0, KT[:, 4096:4608], pt[:, 0:512], [pt], allS)
    vst = g.nstage()
    vstv = vst[:, :].rearrange("p (c d) -> p c d", c=4)
    kb.dma("pool", vstv, I["cna_v"][e].rearrange("(c p) d -> p c d", p=128), reads=[I["cna_v"]], writes=[vst])
    kb.I("pool", "tensor_copy", reads=[vst], writes=allS, out=VtokS[:, :, 32:36, 0:64], in_=vstv.rearrange("p c (h d) -> p h c d", h=2))
    kb.I("pool", "memset", writes=allS, ap=VtokS[:, :, :, 64:65], constant=1.0)
    for c8 in range(4):
        pt = g.nbt()
        for jj in range(8):
            c = c8 * 8 + jj
            kb.TR(pt[:, jj * 128:(jj + 1) * 128], VTt[:, c * 128:(c + 1) * 128], g.ident_b[:, :], reads=allS + [g.ident_b], writes=[pt])
        evac(g, c8, VtokS[:, :, c8 * 8:(c8 + 1) * 8, 0:64], pt[:, :].rearrange("p (c h d) -> p h c d", c=8, h=2), [pt], allS)
    for hh in range(2):
        hb = 64 * hh
        kb.dma("sp", tabv, I["na_tab"][e, hh], reads=[I["na_tab"]], writes=[g.actb_flat])
        for ip in range(32):
            i = 2 * ip
            if i < 4:
                var, r0c, nch = 1 + i // 2, 0, 4
            elif i >= 60:
                var, r0c, nch = 3 + (i - 60) // 2, 56, 4
            else:
                var, r0c, nch = 0, i - 4, 5
            q_ap = QT[hb:hb + 64, i * 64:i * 64 + 128]
            psL0 = g.nb()
            for m in range(4):
                kb.MM(psL0[:, m * 128:(m + 1) * 128], KT[hb:hb + 64, (r0c + 2 * m) * 64:(r0c + 2 * m) * 64 + 128], q_ap, True, True, reads=allS, writes=[psL0])
            tl = tmpL[ip % 2]
            kb.I("dve", "scalar_tensor_tensor", reads=[psL0, g.actb_flat], writes=[(scr, ("tmpL", ip % 2))], out=tl[:, 0:512], in0=psL0[:, :], scalar=SCALE,
                 in1=tabv[:, var, 0:4, :].rearrange("p m q -> p (m q)"), op0=ALU.mult, op1=ALU.add)
            if nch == 5:
                psL1 = g.nb()
                kb.MM(psL1[:, 0:128], KT[hb:hb + 64, (r0c + 8) * 64:(r0c + 8) * 64 + 128], q_ap, True, True, reads=allS, writes=[psL1])
                kb.I("dve", "scalar_tensor_tensor", reads=[psL1, g.actb_flat], writes=[(scr, ("tmpL", ip % 2))], out=tl[:, 512:640], in0=psL1[:, 0:128], scalar=SCALE,
                     in1=tabv[:, var, 4, :], op0=ALU.mult, op1=ALU.add)
            psC = g.nb()
            for c in range(4):
                kb.MM(psC[:, c * 128:(c + 1) * 128], KT[hb:hb + 64, 4096 + c * 128:4096 + (c + 1) * 128], q_ap, True, True, reads=allS, writes=[psC])
            pl, pc = pTL[ip % 2], pTC[ip % 2]
            kb.I("act", "activation", reads=[(scr, ("tmpL", ip % 2))], writes=[(scr, ("pTL", ip % 2))], out=pl[:, 0:nch * 128], in_=tl[:, 0:nch * 128], func=AF.Exp)
            kb.I("act", "activation", reads=[psC], writes=[(scr, ("pTC", ip % 2))], out=pc, in_=psC[:, :], func=AF.Exp, scale=SCALE)
            po = g.nb()
            for m in range(nch):
                kb.MM(po[:, 0:65], pl[:, m * 128:(m + 1) * 128], VtokS[:, hh, r0c // 2 + m, :], m == 0, False, reads=[(scr, ("pTL", ip % 2)), (scr, "static")], writes=[po])
            for c in range(4):
                kb.MM(po[:, 0:65], pc[:, c * 128:(c + 1) * 128], VtokS[:, hh, 32 + c, :], False, c == 3, reads=[(scr, ("pTC", ip % 2)), (scr, "static")], writes=[po])
            st = g.nsmall()
            kb.I("dve", "reciprocal", reads=[po], writes=[st], out=st[:, 0:1], in_=po[:, 64:65])
            kb.I("dve", "tensor_scalar", reads=[po, st], writes=[(scr, ("OtS", ip))], out=OtS[:, ip, hb:hb + 64], in0=po[:, 0:64], scalar1=st[:, 0:1], scalar2=None, op0=ALU.mult)
    oTs = QT
    for c8 in range(4):
        pt = g.nbt()
        for jj in range(8):
            c = c8 * 8 + jj
            kb.TR(pt[:, jj * 128:(jj + 1) * 128], OtS[:, c, :], g.ident_b[:, :], reads=[(scr, None), g.ident_b], writes=[pt])
        evac(g, c8, oTs[:, c8 * 1024:(c8 + 1) * 1024], pt[:, :], [pt], [(scr, None)])
    kb.dma("sp", XO.gin[0], oTs[:, :], reads=[(scr, None)], writes=[(XO.gin, "na")])

    _rwkv(g, l, pinP, XS, lout, XO)

    XO.run()
    kb.dma("sp", g.hT[:, :, 1024:2048], XO.mine.t.ap().rearrange("(k p) t -> p k t", p=128), reads=[XO.mine], writes=[(g.hT, 2), (g.hT, 3)])
    out_proj(g, Wout, e, 16)


NS = 512
NBS = 4
NEG_E05 = -math.exp(-0.5)


def _rw_setup(g):
    if hasattr(g, "rw"):
        return g.rw
    kb, I = g.kb, g.ins
    R = Ctx()
    g.rw = R
    for nm, shape in (("rw_mu", [128, 2, 17, 2]), ("rw_w0", [128, 2, 2, 5]), ("rw_a0", [128, 2, 2, 5]), ("rw_bonus", [128, 2, 2, 5]),
                      ("rw_kk", [128, 2, 5]), ("rw_ka", [128, 2, 5]), ("rw_lnw", [128, 2, 5]), ("rw_lnb", [128, 2, 5])):
        t = kb.sb("p_" + nm, shape, F32)
        kb.dma("sp", t.t.ap(), I[nm].t.ap(), reads=[I[nm]], writes=[t])
        setattr(R, nm, t)
    R.c0 = kb.sb("rw_c0", [128, 2, 17], F32)
    kb.I("dve", "tensor_tensor", reads=[R.rw_mu], writes=[R.c0], out=R.c0[:, :, :], in0=R.rw_mu[:, :, :, 0], in1=R.rw_mu[:, :, :, 1], op=ALU.add)
    kb.I("dve", "tensor_scalar", reads=[R.c0], writes=[R.c0], out=R.c0[:, :, :], in0=R.c0[:, :, :], scalar1=-1.0, scalar2=1.0, op0=ALU.mult, op1=ALU.add)
    R.omk = kb.sb("rw_omk", [128, 2, 5], F32)
    kb.I("dve", "tensor_scalar", reads=[R.rw_ka], writes=[R.omk], out=R.omk[:, :, :], in0=R.rw_ka[:, :, :], scalar1=-1.0, scalar2=1.0, op0=ALU.mult, op1=ALU.add)
    R.wa2 = kb.sb("rw_wa2", [128, 2, 640], BF16)
    R.g2 = kb.sb("rw_g2s", [128, 640], BF16)
    R.m01 = kb.sb("rw_m01", [128, NS], F32)
    kb.I("pool", "memset", writes=[R.m01], ap=R.m01[:, :], constant=1.0)
    kb.I("pool", "memset", writes=[R.m01], ap=R.m01[:, :].rearrange("p (b t) -> p b t", b=NBS)[:, :, 0:1], constant=0.0)
    R.cm3 = kb.sb("rw_cm3", [128, 2, 384], BF16)
    R.cm2 = kb.sb("rw_cm2", [128, 2, 256], BF16)
    MK = I["masks"]
    for d in range(2):
        mS, mI, mN = (0, 1, 2) if d == 0 else (2, 3, 0)
        for k, m in enumerate((mS, mN, mI)):
            kb.dma("pool", R.cm3[:, d, k * 128:(k + 1) * 128], MK[:, m, :], reads=[MK], writes=[R.cm3])
        for k, m in enumerate((mS, mI)):
            kb.dma("pool", R.cm2[:, d, k * 128:(k + 1) * 128], MK[:, m, :], reads=[MK], writes=[R.cm2])
    R.lvl = kb.sb("rw_lvl", [128, 7, 128], BF16)
    kb.dma("pool", R.lvl[:, :, :], I["lvlmask"][:, :, :], reads=[I["lvlmask"]], writes=[R.lvl])
    R.S32 = kb.sb("rw_S32", [128, 2, 128], F32)
    R.S16 = kb.sb("rw_S16", [128, 2, 128], BF16)
    R.gam = kb.sb("rw_gam", [128, NBS], F32)
    return R


def _rw_layout(g):
    scr, ab = g.scr, g.actb_flat
    L = Ctx()
    o = 0

    def take(buf, nbytes, shape, dt):
        nonlocal o
        a = carve(buf, o, shape, dt)
        o += (nbytes + 63) // 64 * 64
        return a
    L.U5 = take(scr, 5 * 544 * 2, [128, 5, 544], BF16)
    L.hst = take(scr, 5 * 2 * 32, [128, 5, 2, 16], BF16)
    L.sr = take(scr, 2048, [128, NS], F32)
    L.sk = take(scr, 2048, [128, NS], F32)
    L.sv = take(scr, 2048, [128, NS], F32)
    L.kk = take(scr, 2048, [128, NS], F32)
    L.swa = take(scr, 1024, [128, NS], BF16)
    L.sg = take(scr, 1024, [128, NS], BF16)
    L.svb = take(scr, 1024, [128, NS], BF16)
    L.prod = take(scr, 6 * 1024, [128, 6, NS], BF16)
    L.tok = take(scr, 4 * 1024, [128, 4, NBS, 128], BF16)
    L.Vpad = take(scr, 2048, [128, NBS, 2, 128], BF16)
    L.G3 = [take(scr, 768, [128, 384], BF16) for _ in range(2)]
    L.G2 = [take(scr, 512, [128, 256], BF16) for _ in range(2)]
    L.TA = [[take(scr, 512, [128, 256], BF16) for _ in range(2)] for _ in range(2)]
    L.XB = [take(scr, 512, [128, 256], BF16) for _ in range(2)]
    L.NLl = [[take(scr, 512, [128, 256], BF16) for _ in range(2)] for _ in range(2)]
    L.Y1b = [take(scr, 128, [128, 64], BF16) for _ in range(2)]
    L.U0 = take(scr, 512, [128, 128], F32)
    L.Ub = take(scr, 256, [128, 128], BF16)
    L.Upad = take(scr, 768, [128, 384], BF16)
    L.WT = take(scr, 256, [128, 128], BF16)
    L.Yacc = take(scr, 2048, [128, NS], F32)
    L.Yb = take(scr, 1024, [128, NS], BF16)
    L.ob = take(scr, 1024, [128, NS], BF16)
    assert o <= 45056, o
    o = 0
    L.LW = take(ab, 2048, [128, NS], F32)
    L.L = take(ab, 2048, [128, NS], F32)
    L.A = take(ab, 2048, [128, NS], F32)
    L.KT = take(ab, 2048, [128, NS], F32)
    L.E = take(ab, 2048, [128, NS], F32)
    L.E2 = take(ab, 2048, [128, NS], F32)
    L.T1 = take(ab, 2048, [128, NS], F32)
    assert o <= 16384, o
    return L


def _rw_shift(g, R, L, e, jp, sample):
    kb = g.kb
    SEG = [(g.scr, "seg")]
    AB = [(g.actb_flat, "seg")]
    mub = [jp if jp < 4 else 14, 4 + jp if jp < 4 else 15, 8 + jp if jp < 4 else 16, 12, 13]
    dsts = [L.sr, L.sk, L.sv, L.T1, L.E]
    for a in range(5):
        dst, mb = dsts[a], mub[a]
        wr = SEG if a < 3 else AB
        if sample:
            cur, prv, nxt = L.U5[:, a, 16:528], L.U5[:, a, 15:527], L.U5[:, a, 17:529]
            d0, d1, d2 = dst[:, :], dst[:, :], dst[:, :]
        else:
            uv = L.U5[:, a, 16:528].rearrange("p (s t) -> p s t", s=2)
            dv = dst[:, :].rearrange("p (s t) -> p s t", s=2)
            cur, prv, nxt = L.U5[:, a, 16:528], uv[:, :, 0:255], uv[:, :, 1:256]
            d0, d1, d2 = dst[:, :], dv[:, :, 1:256], dv[:, :, 0:255]
        kb.I("dve", "tensor_scalar", reads=SEG + [R.c0], writes=wr, out=d0, in0=cur, scalar1=R.c0[:, e, mb:mb + 1], scalar2=None, op0=ALU.mult)
        kb.I("dve", "scalar_tensor_tensor", reads=SEG + wr + [R.rw_mu], writes=wr, out=d1, in0=prv, scalar=R.rw_mu[:, e, mb, 0:1], in1=d1, op0=ALU.mult, op1=ALU.add)
        kb.I("dve", "scalar_tensor_tensor", reads=SEG + wr + [R.rw_mu], writes=wr, out=d2, in0=nxt, scalar=R.rw_mu[:, e, mb, 1:2], in1=d2, op0=ALU.mult, op1=ALU.add)
    kb.I("act", "activation", reads=AB, writes=SEG, out=L.swa[0:64, :], in_=L.T1[0:64, :], func=AF.Tanh)
    kb.I("act", "copy", reads=AB, writes=SEG, out=L.swa[64:128, :], in_=L.T1[64:128, :])
    kb.I("act", "activation", reads=AB, writes=SEG, out=L.sg[:, :], in_=L.E[:, :], func=AF.Sigmoid)
    kb.I("act", "copy", reads=SEG, writes=SEG, out=L.svb[:, :], in_=L.sv[:, :])
    kb.I("dve", "tensor_scalar", reads=SEG + [R.rw_kk], writes=SEG, out=L.kk[:, :], in0=L.sk[:, :], scalar1=R.rw_kk[:, e, jp:jp + 1], scalar2=None, op0=ALU.mult)
    sq = g.nstage()
    kb.I("act", "activation", reads=SEG, writes=[sq], out=sq[:, :], in_=L.kk[:, :], func=AF.Square)
    pn = g.nb()
    kb.MM(pn[:, :], g.blk_b[:, :], sq[:, :], True, True, reads=[sq, g.blk_b], writes=[pn])
    kb.I("dve", "tensor_scalar", reads=[pn], writes=AB, out=L.E[:, :], in0=pn[:, :], scalar1=1e-12, scalar2=None, op0=ALU.max)
    kb.I("act", "activation", reads=AB, writes=AB, out=L.E[:, :], in_=L.E[:, :], func=AF.Sqrt)
    kb.I("dve", "reciprocal", reads=AB, writes=AB, out=L.E[:, :], in_=L.E[:, :])
    kb.I("dve", "tensor_tensor", reads=SEG + AB, writes=SEG, out=L.kk[:, :], in0=L.kk[:, :], in1=L.E[:, :], op=ALU.mult)


def _rw_lora_a_kt(g, R, L, e, jp, d, want_w):
    kb = g.kb
    SEG = [(g.scr, "seg")]
    AB = [(g.actb_flat, "seg")]
    c0 = jp * 128
    pa = g.nb()
    kb.MM(pa[:, :], R.wa2[64:128, d, c0:c0 + 128], L.swa[64:128, :], True, True, reads=SEG + [R.wa2], writes=[pa])
    kb.I("act", "activation", reads=[pa, R.rw_a0], writes=AB, out=L.A[:, :], in_=pa[:, :], func=AF.Sigmoid, bias=R.rw_a0[:, e, d, jp:jp + 1])
    kb.I("dve", "tensor_scalar", reads=AB + [R.rw_ka, R.omk], writes=AB, out=L.KT[:, :], in0=L.A[:, :], scalar1=R.rw_ka[:, e, jp:jp + 1], scalar2=R.omk[:, e, jp:jp + 1],
         op0=ALU.mult, op1=ALU.add)
    kb.I("dve", "tensor_tensor", reads=AB + SEG, writes=AB, out=L.KT[:, :], in0=L.KT[:, :], in1=L.sk[:, :], op=ALU.mult)
    if want_w:
        pw = g.nb()
        kb.MM(pw[:, :], R.wa2[0:64, d, c0:c0 + 128], L.swa[0:64, :], True, True, reads=SEG + [R.wa2], writes=[pw])
        kb.I("act", "activation", reads=[pw, R.rw_w0], writes=AB, out=L.LW[:, :], in_=pw[:, :], func=AF.Sigmoid, bias=R.rw_w0[:, e, d, jp:jp + 1])
        kb.I("dve", "tensor_scalar", reads=AB, writes=AB, out=L.LW[:, :], in0=L.LW[:, :], scalar1=NEG_E05, scalar2=None, op0=ALU.mult)


def _rw_dir_prep(g, R, L, e, jp, d):
    kb = g.kb
    SEG = [(g.scr, "seg")]
    AB = [(g.actb_flat, "seg")]
    _rw_lora_a_kt(g, R, L, e, jp, d, True)
    b3 = lambda ap: ap.rearrange("p (b t) -> p b t", b=NBS)
    if d == 0:
        kb.I("dve", "tensor_tensor_scan", reads=AB + [R.m01], writes=AB, out=L.L[:, :], data0=R.m01[:, :], data1=L.LW[:, :], initial=0.0, op0=ALU.mult, op1=ALU.add)
        ltot = b3(L.L[:, :])[:, :, 127:128]
    else:
        kb.I("dve", "tensor_tensor_scan", reads=AB + [R.m01], writes=AB, out=L.E[:, :], data0=R.m01[:, :], data1=L.LW[:, :], initial=0.0, op0=ALU.mult, op1=ALU.add)
        kb.I("dve", "tensor_tensor", reads=AB, writes=AB, out=L.L[:, :], in0=L.LW[:, :], in1=L.E[:, :], op=ALU.subtract)
        kb.I("dve", "tensor_tensor", reads=AB, writes=AB, out=b3(L.L[:, :]), in0=b3(L.L[:, :]), in1=b3(L.E[:, :])[:, :, 127:128].broadcast_to([128, NBS, 128]), op=ALU.add)
        ltot = b3(L.L[:, :])[:, :, 0:1]
    kb.I("act", "activation", reads=AB, writes=[R.gam], out=R.gam[:, :], in_=ltot.rearrange("p b o -> p (b o)"), func=AF.Exp)
    al, be, ka, rt, bh, kh = [L.prod[:, i, :] for i in range(6)]
    kb.I("dve", "tensor_tensor", reads=AB + SEG, writes=AB, out=L.A[:, :], in0=L.A[:, :], in1=L.kk[:, :], op=ALU.mult)
    kb.I("dve", "tensor_tensor", reads=AB, writes=AB, out=L.E2[:, :], in0=L.L[:, :], in1=L.LW[:, :], op=ALU.subtract)
    kb.I("act", "activation", reads=AB, writes=AB, out=L.E2[:, :], in_=L.E2[:, :], func=AF.Exp)
    kb.I("dve", "tensor_tensor", reads=AB + SEG, writes=SEG, out=al, in0=L.kk[:, :], in1=L.E2[:, :], op=ALU.mult)
    kb.I("act", "activation", reads=AB, writes=AB, out=L.E[:, :], in_=L.L[:, :], func=AF.Exp, scale=-1.0)
    kb.I("dve", "scalar_tensor_tensor", reads=AB, writes=SEG, out=be, in0=L.A[:, :], scalar=-1.0, in1=L.E[:, :], op0=ALU.mult, op1=ALU.mult)
    kb.I("dve", "tensor_tensor", reads=AB, writes=SEG, out=ka, in0=L.KT[:, :], in1=L.E[:, :], op=ALU.mult)
    kb.I("act", "activation", reads=AB, writes=AB, out=L.E[:, :], in_=L.L[:, :], func=AF.Exp)
    kb.I("dve", "tensor_tensor", reads=AB + SEG, writes=SEG, out=rt, in0=L.sr[:, :], in1=L.E[:, :], op=ALU.mult)
    kb.I("dve", "scalar_tensor_tensor", reads=AB, writes=AB, out=b3(L.E2[:, :]), in0=b3(L.L[:, :]), scalar=-1.0, in1=ltot.broadcast_to([128, NBS, 128]), op0=ALU.mult, op1=ALU.add)
    kb.I("act", "activation", reads=AB, writes=AB, out=L.E2[:, :], in_=L.E2[:, :], func=AF.Exp)
    kb.I("dve", "scalar_tensor_tensor", reads=AB, writes=SEG, out=bh, in0=L.A[:, :], scalar=-1.0, in1=L.E2[:, :], op0=ALU.mult, op1=ALU.mult)
    kb.I("dve", "tensor_tensor", reads=AB, writes=SEG, out=kh, in0=L.KT[:, :], in1=L.E2[:, :], op=ALU.mult)
    for bi in range(NBS):
        pt = g.nbt()
        for k, src in enumerate((al, bh, kh, L.svb[:, :])):
            kb.TR(pt[:, k * 128:(k + 1) * 128], src[:, bi * 128:(bi + 1) * 128], g.ident_b[:, :], reads=SEG + [g.ident_b], writes=[pt])
        evac(g, bi, L.tok[:, :, bi, :], pt[:, 0:512].rearrange("p (k c) -> p k c", k=4), [pt], SEG)
        for hh in range(2):
            kb.I("pool", "tensor_copy", reads=SEG, writes=SEG, out=L.Vpad[:, bi, hh, hh * 64:(hh + 1) * 64], in_=L.tok[:, 3, bi, hh * 64:(hh + 1) * 64])


def _rw_block(g, R, L, d, bi, ycb):
    kb = g.kb
    SEG = [(g.scr, "seg")]
    BLK = [(g.scr, "blk")]
    al, be, ka, rt, bh, kh = [L.prod[:, i, :] for i in range(6)]
    ts = slice(bi * 128, (bi + 1) * 128)
    S32, S16 = R.S32[:, d, :], R.S16[:, d, :]
    ST = [(R.S32, d), (R.S16, d)]
    for hh in range(2):
        hp = slice(64 * hh, 64 * hh + 64)
        pg = g.nb()
        kb.MM(pg[:, 0:128], be[hp, ts], al[hp, ts], True, True, reads=SEG, writes=[pg])
        kb.MM(pg[:, 128:256], al[hp, ts], be[hp, ts], True, True, reads=SEG, writes=[pg])
        kb.MM(pg[:, 256:384], be[hp, ts], rt[hp, ts], True, True, reads=SEG, writes=[pg])
        kb.I("dve", "tensor_tensor", reads=[pg, R.cm3], writes=BLK, out=L.G3[hh], in0=pg[:, 0:384], in1=R.cm3[:, d, :], op=ALU.mult)
        pg2 = g.nb()
        kb.MM(pg2[:, 0:128], ka[hp, ts], al[hp, ts], True, True, reads=SEG, writes=[pg2])
        kb.MM(pg2[:, 128:256], ka[hp, ts], rt[hp, ts], True, True, reads=SEG, writes=[pg2])
        kb.I("dve", "tensor_tensor", reads=[pg2, R.cm2], writes=BLK, out=L.G2[hh], in0=pg2[:, 0:256], in1=R.cm2[:, d, :], op=ALU.mult)
    import os
    RWB = os.environ.get("RWB", "")
    if RWB == "gram":
        return
    cur = [0, 0]
    for lv in range(7):
        for hh in range(2):
            nl = L.NLl[hh][lv % 2]
            HB = [(g.scr, ("blk", hh))]
            kb.I("pool" if hh == 0 else "dve", "tensor_tensor", reads=BLK + [R.lvl], writes=HB, out=nl.rearrange("p (a b) -> p a b", a=2),
                 in0=L.G3[hh][:, 0:256].rearrange("p (a b) -> p a b", a=2), in1=R.lvl[:, lv, :].unsqueeze(1).broadcast_to([128, 2, 128]), op=ALU.mult)
            if lv == 0:
                ta = L.TA[hh][0]
                kb.I("pool", "tensor_tensor", reads=HB + [g.ident_b], writes=HB, out=ta[:, 0:128], in0=nl[:, 128:256], in1=g.ident_b[:, :], op=ALU.add)
                kb.I("pool", "tensor_tensor", reads=HB + [g.ident_b], writes=HB, out=ta[:, 128:256], in0=nl[:, 0:128], in1=g.ident_b[:, :], op=ALU.add)
                continue
            ta, tn = L.TA[hh][cur[hh]], L.TA[hh][1 - cur[hh]]
            px = g.nb()
            kb.MM(px[:, 0:128], nl[:, 0:128], ta[:, 0:128], True, True, reads=HB, writes=[px])
            kb.MM(px[:, 128:256], nl[:, 128:256], ta[:, 128:256], True, True, reads=HB, writes=[px])
            evac(g, hh + lv, L.XB[hh][:, 0:256], px[:, 0:256], [px], HB)
            pq = g.nb()
            kb.MM(pq[:, 0:128], ta[:, 128:256], L.XB[hh][:, 0:128], True, True, reads=HB, writes=[pq])
            kb.MM(pq[:, 128:256], ta[:, 0:128], L.XB[hh][:, 128:256], True, True, reads=HB, writes=[pq])
            kb.I("dve", "tensor_tensor", reads=[pq] + HB, writes=HB, out=tn[:, 0:256], in0=pq[:, 0:256], in1=ta[:, 0:256], op=ALU.add)
            cur[hh] = 1 - cur[hh]
    if RWB == "solve":
        return
    pu0 = g.nb()
    for hh in range(2):
        hp = slice(64 * hh, 64 * hh + 64)
        TT = L.TA[hh][cur[hh]][:, 128:256]
        pw = g.nb()
        HB = [(g.scr, ("blk", hh))]
        kb.MM(pw[:, 0:128], L.tok[:, 0, bi, :], TT, True, True, reads=SEG + BLK + HB, writes=[pw])
        kb.I("act", "copy", reads=[pw], writes=BLK, out=L.WT[hp, :], in_=pw[hp, 0:128])
        kb.MM(pw[:, 128:192], L.G2[hh][:, 0:128], L.tok[:, 3, bi, 64 * hh:64 * hh + 64], True, True, reads=SEG + BLK, writes=[pw])
        kb.I("dve", "tensor_copy", reads=[pw], writes=BLK, out=L.Y1b[hh], in_=pw[:, 128:192])
        kb.MM(pu0[:, 64 * hh:64 * hh + 64], TT, L.Y1b[hh], True, True, reads=BLK + HB, writes=[pu0])
    kb.I("act", "copy", reads=[pu0], writes=BLK, out=L.U0, in_=pu0[:, 0:128])
    if RWB == "wt":
        return
    pu = g.nb()
    kb.MM(pu[:, 0:128], L.WT[:, :], S16[:, :], True, True, reads=BLK + ST, writes=[pu])
    if RWB == "pu1":
        return
    kb.I("dve", "tensor_tensor", reads=[pu] + BLK, writes=BLK, out=L.Ub, in0=pu[:, 0:128], in1=L.U0, op=ALU.add)
    if RWB == "pu2":
        return
    kb.I("pool", "tensor_copy", reads=BLK, writes=BLK, out=L.Upad.rearrange("p (a b) -> p a b", a=2)[:, :, 0:64], in_=L.Ub.rearrange("p (a b) -> p a b", a=2))
    if RWB == "pu":
        return
    py = g.nb()
    n = 0
    kb.MM(py[:, 0:128], S16[:, :], rt[:, ts], True, False, reads=SEG + ST, writes=[py])
    for hh in range(2):
        kb.MM(py[:, 0:128], L.Upad[:, hh * 128:(hh + 1) * 128], L.G3[hh][:, 256:384], False, False, reads=BLK, writes=[py])
        kb.MM(py[:, 0:128], L.Vpad[:, bi, hh, :], L.G2[hh][:, 128:256], False, hh == 1, reads=SEG + BLK, writes=[py])
    ycb(bi, py)
    if RWB == "py":
        return
    psn = g.nb()
    kb.MM(psn[:, 0:128], L.tok[:, 2, bi, :], L.tok[:, 3, bi, :], True, False, reads=SEG, writes=[psn])
    kb.MM(psn[:, 0:128], L.tok[:, 1, bi, :], L.Ub, False, True, reads=SEG + BLK, writes=[psn])
    for hh in range(2):
        hp = slice(64 * hh, 64 * hh + 64)
        cs = slice(64 * hh, 64 * hh + 64)
        kb.I("dve", "scalar_tensor_tensor", reads=[psn, R.gam] + ST, writes=[(R.S32, d)], out=S32[hp, cs], in0=S32[hp, cs], scalar=R.gam[hp, bi:bi + 1], in1=psn[hp, cs],
             op0=ALU.mult, op1=ALU.add)
    kb.I("act", "copy", reads=[(R.S32, d)], writes=[(R.S16, d)], out=S16, in_=S32)


def _rw_epilogue(g, R, L, e, jp, emit_out):
    kb = g.kb
    SEG = [(g.scr, "seg")]
    AB = [(g.actb_flat, "seg")]
    kb.I("act", "copy", reads=SEG, writes=SEG, out=L.Yb[:, :], in_=L.Yacc[:, :])
    pm = g.nb()
    kb.MM(pm[:, :], g.blk_b[:, :], L.Yb[:, :], True, True, reads=SEG + [g.blk_b], writes=[pm])
    kb.I("dve", "scalar_tensor_tensor", reads=[pm] + SEG, writes=SEG, out=L.Yacc[:, :], in0=pm[:, :], scalar=-1.0 / 64, in1=L.Yacc[:, :], op0=ALU.mult, op1=ALU.add)
    kb.I("act", "activation", reads=SEG, writes=SEG, out=L.Yb[:, :], in_=L.Yacc[:, :], func=AF.Square)
    pv = g.nb()
    kb.MM(pv[:, :], g.blk_b[:, :], L.Yb[:, :], True, True, reads=SEG + [g.blk_b], writes=[pv])
    kb.I("act", "activation", reads=[pv, g.eps_t], writes=AB, out=L.E[:, :], in_=pv[:, :], func=AF.Sqrt, scale=1.0 / 64, bias=g.eps_t[:, 1:2])
    kb.I("dve", "reciprocal", reads=AB, writes=AB, out=L.E[:, :], in_=L.E[:, :])
    kb.I("dve", "tensor_tensor", reads=SEG + AB, writes=SEG, out=L.Yacc[:, :], in0=L.Yacc[:, :], in1=L.E[:, :], op=ALU.mult)
    kb.I("dve", "tensor_scalar", reads=SEG + [R.rw_lnw, R.rw_lnb], writes=SEG, out=L.Yacc[:, :], in0=L.Yacc[:, :], scalar1=R.rw_lnw[:, e, jp:jp + 1], scalar2=R.rw_lnb[:, e, jp:jp + 1],
         op0=ALU.mult, op1=ALU.add)
    for d in range(2):
        _rw_lora_a_kt(g, R, L, e, jp, d, False)
        if d == 0:
            kb.I("dve", "tensor_scalar", reads=AB + [R.rw_bonus], writes=AB, out=L.E2[:, :], in0=L.KT[:, :], scalar1=R.rw_bonus[:, e, d, jp:jp + 1], scalar2=None, op0=ALU.mult)
        else:
            kb.I("dve", "scalar_tensor_tensor", reads=AB + [R.rw_bonus], writes=AB, out=L.E2[:, :], in0=L.KT[:, :], scalar=R.rw_bonus[:, e, d, jp:jp + 1], in1=L.E2[:, :],
                 op0=ALU.mult, op1=ALU.add)
    kb.I("dve", "tensor_tensor", reads=AB + SEG, writes=SEG, out=L.Yb[:, :], in0=L.E2[:, :], in1=L.sr[:, :], op=ALU.mult)
    pbn = g.nb()
    kb.MM(pbn[:, :], g.blk_b[:, :], L.Yb[:, :], True, True, reads=SEG + [g.blk_b], writes=[pbn])
    kb.I("dve", "tensor_tensor", reads=[pbn] + SEG, writes=AB, out=L.E[:, :], in0=pbn[:, :], in1=L.sv[:, :], op=ALU.mult)
    kb.I("dve", "tensor_tensor", reads=SEG + AB, writes=SEG, out=L.Yacc[:, :], in0=L.Yacc[:, :], in1=L.E[:, :], op=ALU.add)
    pgt = g.nb()
    kb.MM(pgt[:, :], R.g2[:, jp * 128:(jp + 1) * 128], L.sg[:, :], True, True, reads=SEG + [R.g2], writes=[pgt])
    kb.I("dve", "tensor_tensor", reads=[pgt] + SEG, writes=SEG, out=L.ob[:, :], in0=L.Yacc[:, :], in1=pgt[:, :], op=ALU.mult)
    emit_out(L.ob)


def _rwkv_stub(g, XO):
    kb = g.kb
    z = g.nstage()
    kb.I("pool", "memset", writes=[z], ap=z[:, :], constant=0.0)
    for b in range(8):
        kb.dma("sp", XO.gin[1][:, b * 512:(b + 1) * 512], z[:, :], reads=[z], writes=[(XO.gin, ("rw", b))])
    for k in range(4, 8):
        for ti in range(2):
            kb.I("pool", "memset", writes=[(g.hT, ti)], ap=g.hT[:, k, ti * 512:(ti + 1) * 512], constant=0.0)


def _rwkv(g, l, pinP, XS, lout, XO):
    kb, I = g.kb, g.ins
    e = l // 2
    R = _rw_setup(g)
    L = _rw_layout(g)
    SEG = [(g.scr, "seg")]
    ORW = g.outd["o_rw"]
    kb.dma("pool", R.wa2[0:64, :, :], I["rw_w2"][e].rearrange("d k n -> k d n"), reads=[I["rw_w2"]], writes=[R.wa2])
    kb.dma("pool", R.wa2[64:128, :, :], I["rw_a2"][e].rearrange("d k n -> k d n"), reads=[I["rw_a2"]], writes=[R.wa2])
    kb.dma("pool", R.g2[:, :], I["rw_g2"][e], reads=[I["rw_g2"]], writes=[R.g2])
    kb.I("pool", "memset", writes=[(g.scr, None)], ap=L.Vpad, constant=0.0)
    kb.I("pool", "memset", writes=[(g.scr, None)], ap=L.Upad, constant=0.0)
    import os
    STOP = os.environ.get("RWSTOP", "")
    if STOP == "setup":
        return _rwkv_stub(g, XO)

    def zero_state(d):
        kb.I("pool", "memset", writes=[(R.S32, d)], ap=R.S32[:, d, :], constant=0.0)
        kb.I("pool", "memset", writes=[(R.S16, d)], ap=R.S16[:, d, :], constant=0.0)

    for jp in range(4):
        for half in range(2):
            t0 = half * NS
            for a, blk in enumerate((12 + jp, 16 + jp, 20 + jp, 24, 25)):
                kb.dma("sp" if a % 2 == 0 else "act", L.U5[:, a, 16:528], pinP[blk * 128:(blk + 1) * 128, t0:t0 + NS], reads=[pinP], writes=SEG)
            _rw_shift(g, R, L, e, jp, False)
            if STOP == "shift":
                return _rwkv_stub(g, XO)
            for d in range(2):
                _rw_dir_prep(g, R, L, e, jp, d)
                if STOP == "prep":
                    return _rwkv_stub(g, XO)

                def ycb(bi, py, d=d):
                    ts = slice(bi * 128, (bi + 1) * 128)
                    if d == 0:
                        kb.I("act", "copy", reads=[py], writes=SEG, out=L.Yacc[:, ts], in_=py[:, 0:128])
                    else:
                        kb.I("dve", "tensor_tensor", reads=[py] + SEG, writes=SEG, out=L.Yacc[:, ts], in0=py[:, 0:128], in1=L.Yacc[:, ts], op=ALU.add)
                for sq_ in range(2):
                    zero_state(d)
                    order = (2 * sq_, 2 * sq_ + 1) if d == 0 else (2 * sq_ + 1, 2 * sq_)
                    for bi in order:
                        _rw_block(g, R, L, d, bi, ycb)
                        if STOP == "block":
                            return _rwkv_stub(g, XO)
                    pst = g.nb()
                    kb.TR(pst[:, 0:128], R.S32[:, d, :], g.ident_f[:, :], reads=[(R.S32, d), g.ident_f], writes=[pst])
                    so = g.ntmp()
                    kb.I("act", "copy", reads=[pst], writes=[so], out=so[:, 0:128], in_=pst[:, 0:128])
                    seq = half * 2 + sq_
                    for hh in range(2):
                        kb.dma("sp", ORW[e, seq, d, jp, :, 64 * hh:64 * hh + 64], so[64 * hh:64 * hh + 64, 64 * hh:64 * hh + 64], reads=[so], writes=[(ORW, (e, seq, d, jp, hh))])

            def emit_out(ob, jp=jp, t0=t0):
                kb.I("pool", "tensor_copy", reads=SEG, writes=[(g.hT, t0 // TT)], out=g.hT[:, 4 + jp, t0:t0 + NS], in_=ob[:, :])
            _rw_epilogue(g, R, L, e, 0 + jp, emit_out)

    if STOP == "prompt":
        z = g.nstage()
        kb.I("pool", "memset", writes=[z], ap=z[:, :], constant=0.0)
        for b in range(8):
            kb.dma("sp", XO.gin[1][:, b * 512:(b + 1) * 512], z[:, :], reads=[z], writes=[(XO.gin, ("rw", b))])
        return
    jp = 4
    yfw = kb.dram(f"yfw{l}", [8, 128, NS], F32)

    def load_seg(sg):
        T0 = sg * NS
        rr, cc = T0 // 1024, T0 % 1024
        for a in range(5):
            def src(rr_, lo, hi, a=a):
                if a < 3:
                    return XS.get(rr_, 3 + a)[:, lo:hi], XS.mine
                return lout[rr_, (a - 3) * 128:(a - 2) * 128, lo:hi], lout
            eng = "sp" if a % 2 == 0 else "act"
            ap, sb_ = src(rr, cc, cc + NS)
            kb.dma(eng, L.U5[:, a, 16:528], ap, reads=[sb_], writes=SEG)
            if T0 == 0:
                kb.I("pool", "memset", writes=SEG, ap=L.U5[:, a, 15:16], constant=0.0)
            else:
                Tm = T0 - 1
                ap, sb_ = src(Tm // 1024, Tm % 1024, Tm % 1024 + 1)
                kb.dma(eng, L.hst[:, a, 0, 0:1], ap, reads=[sb_], writes=SEG, allow_slow_non_contiguous=True)
                kb.I("pool", "tensor_copy", reads=SEG, writes=SEG, out=L.U5[:, a, 15:16], in_=L.hst[:, a, 0, 0:1])
            if T0 + NS == 4096:
                kb.I("pool", "memset", writes=SEG, ap=L.U5[:, a, 528:529], constant=0.0)
            else:
                Tp = T0 + NS
                ap, sb_ = src(Tp // 1024, Tp % 1024, Tp % 1024 + 1)
                kb.dma(eng, L.hst[:, a, 1, 0:1], ap, reads=[sb_], writes=SEG, allow_slow_non_contiguous=True)
                kb.I("pool", "tensor_copy", reads=SEG, writes=SEG, out=L.U5[:, a, 528:529], in_=L.hst[:, a, 1, 0:1])

    for d in range(2):
        zero_state(d)
        s0 = g.ntmp()
        kb.dma("sp", s0[0:64, 0:128], I["st_rw"][e, d], reads=[I["st_rw"]], writes=[s0])
        pst = g.nb()
        kb.TR(pst[:, 0:64], s0[0:64, 0:128], g.ident_f[0:64, 0:64], reads=[s0, g.ident_f], writes=[pst])
        for hh in range(2):
            hp = slice(64 * hh, 64 * hh + 64)
            kb.I("dve", "tensor_copy", reads=[pst], writes=[(R.S32, d)], out=R.S32[hp, d, 64 * hh:64 * hh + 64], in_=pst[hp, 0:64])
        kb.I("act", "copy", reads=[(R.S32, d)], writes=[(R.S16, d)], out=R.S16[:, d, :], in_=R.S32[:, d, :])
    for d in range(2):
        segs = range(8) if d == 0 else range(7, -1, -1)
        for sg in segs:
            load_seg(sg)
            _rw_shift(g, R, L, e, jp, True)
            _rw_dir_prep(g, R, L, e, jp, d)
            if d == 1:
                kb.dma("sp", L.Yacc[:, :], yfw[sg], reads=[(yfw, sg)], writes=SEG)

            def ycb(bi, py, d=d):
                ts = slice(bi * 128, (bi + 1) * 128)
                if d == 0:
                    kb.I("act", "copy", reads=[py], writes=SEG, out=L.Yacc[:, ts], in_=py[:, 0:128])
                else:
                    kb.I("dve", "tensor_tensor", reads=[py] + SEG, writes=SEG, out=L.Yacc[:, ts], in0=py[:, 0:128], in1=L.Yacc[:, ts], op=ALU.add)
            for bi in (range(NBS) if d == 0 else range(NBS - 1, -1, -1)):
                _rw_block(g, R, L, d, bi, ycb)
            if d == 0:
                kb.dma("sp", yfw[sg], L.Yacc[:, :], reads=SEG, writes=[(yfw, sg)])
            else:
                def emit_out(ob, sg=sg):
                    kb.dma("sp", XO.gin[1][:, sg * NS:(sg + 1) * NS], ob[:, :], reads=SEG, writes=[(XO.gin, ("rw", sg))])
                _rw_epilogue(g, R, L, e, jp, emit_out)


def _fm(v, nblk):
    v = np.asarray(v)
    lead = v.shape[:-1]
    a = v.reshape(lead + (nblk, 128))
    a = np.moveaxis(a, -1, 0)
    return np.ascontiguousarray(a)


def _na_tables(rpb):
    E, H = rpb.shape[0], rpb.shape[1]
    tab = np.full((E, H, 128, 5, 5, 128), MASKV, np.float32)
    var_i = {1: 0, 2: 2, 3: 60, 4: 62}
    half = np.arange(128) // 64
    kcol = np.arange(128) % 64
    qo = np.arange(128) // 64
    j = np.arange(128) % 64
    c0 = np.clip(j - 8, 0, 48)
    colok = (kcol[:, None] >= c0[None, :]) & (kcol[:, None] < c0[None, :] + 16)
    dc = kcol[:, None] - j[None, :] + 15
    dcc = np.clip(dc, 0, 30)
    for v in range(5):
        for m in range(5):
            if v == 0:
                i = 10
                r0c = i - 4
            else:
                i = var_i[v]
                r0c = min(max(i - 4, 0), 56)
                if m == 4:
                    continue
            kr = r0c + 2 * m + half
            iq = i + qo
            r0q = np.clip(iq - 4, 0, 56)
            rowok = (kr[:, None] >= r0q[None, :]) & (kr[:, None] < r0q[None, :] + 8)
            dr = kr[:, None] - iq[None, :] + 7
            drc = np.clip(dr, 0, 14)
            ok = rowok & colok
            vals = rpb[:, :, drc, dcc]
            cur = tab[:, :, :, v, m, :]
            tab[:, :, :, v, m, :] = np.where(ok[None, None], vals, cur)
    return tab


def _rope_tables():
    t = np.arange(4096)
    pos = np.stack([t // 64, t % 64], -1).astype(np.float32)
    inv = (10000.0 ** (-np.arange(16, dtype=np.float32) / 16)).astype(np.float32)
    ang = pos[:, :, None] * inv
    cos, sin = np.cos(ang).astype(np.float32), np.sin(ang).astype(np.float32)
    C = np.zeros((4096, 2, 2, 16), np.float32)
    S = np.zeros((4096, 2, 2, 16), np.float32)
    C[:, :, 0, :] = cos; C[:, :, 1, :] = cos
    S[:, :, 0, :] = -sin; S[:, :, 1, :] = sin
    C = C.reshape(4096, 64); S = S.reshape(4096, 64)
    C = np.concatenate([C, C], 1).T
    S = np.concatenate([S, S], 1).T
    return np.ascontiguousarray(C), np.ascontiguousarray(S)


def _sw_perm():
    idx = np.arange(2048).reshape(-1, 2, 2, 16)
    return idx[:, :, ::-1, :].reshape(-1)


_CACHE = {}


def prep_inputs(inp):
    f32 = lambda a: np.ascontiguousarray(np.asarray(a, np.float32))
    x_prompt = f32(inp["x_prompt"]); x_sample = f32(inp["x_sample"]); c = f32(inp["c"])
    common = {}
    common["w_ada"] = f32(inp["w_ada"])
    common["b_ada"] = _fm(f32(inp["b_ada"]), 48)
    common["norm_mix"] = _fm(f32(inp["norm_mix"]), 8)
    common["norm_ffn"] = _fm(f32(inp["norm_ffn"]), 8)
    common["norm_final"] = _fm(f32(inp["norm_final"]), 8)
    common["w_in_even"] = f32(inp["w_in_even"])
    common["w_out_even"] = f32(inp["w_out_even"])
    wq = f32(inp["w_qkv_diff"])
    common["w_qkv_diff"] = wq
    common["w_qk_sw"] = np.ascontiguousarray(wq[:, :, :2048][:, :, _sw_perm()])
    common["w_out_diff"] = f32(inp["w_out_diff"])
    common["ffn_w1"] = f32(inp["ffn_w1"]); common["ffn_w3"] = f32(inp["ffn_w3"]); common["ffn_w2"] = f32(inp["ffn_w2"])
    mu = f32(inp["rw_mu"])
    common["rw_mu"] = np.ascontiguousarray(np.transpose(_fm(mu, 14), (0, 1, 3, 2)))
    common["rw_w0"] = _fm(f32(inp["rw_w0"]), 4)
    common["rw_a0"] = _fm(f32(inp["rw_a0"]), 4)
    common["rw_kk"] = _fm(f32(inp["rw_kk"]), 4)
    common["rw_ka"] = _fm(f32(inp["rw_ka"]), 4)
    common["rw_lnw"] = _fm(f32(inp["rw_lnw"]), 4)
    common["rw_lnb"] = _fm(f32(inp["rw_lnb"]), 4)
    common["rw_bonus"] = _fm(f32(inp["rw_bonus"]).reshape(2, 2, 512), 4)
    common["rw_w2"] = f32(inp["rw_w2"]); common["rw_a2"] = f32(inp["rw_a2"]); common["rw_g2"] = f32(inp["rw_g2"])
    rep = lambda a: np.ascontiguousarray(np.broadcast_to(a[None], (128,) + a.shape))
    common["lamq"] = rep(f32(inp["diff_lam_q"]).reshape(2, 128))
    common["lamk"] = rep(f32(inp["diff_lam_k"]).reshape(2, 128))
    common["subln"] = rep(f32(inp["diff_subln"]))
    s = np.arange(128)[:, None]; t = np.arange(128)[None, :]
    common["masks"] = np.ascontiguousarray(np.stack([(s < t), (s <= t), (s > t), (s >= t)], 1).astype(np.float32))
    common["ident"] = np.eye(128, dtype=np.float32)
    pi = np.arange(128)[:, None]; fi = np.arange(128)[None, :]
    common["lvlmask"] = np.ascontiguousarray(np.stack([((pi >> (lv + 1)) == (fi >> (lv + 1))) & ((pi >> lv) != (fi >> lv)) for lv in range(7)], 1).astype(np.float32))
    tab = _na_tables(f32(inp["na_rpb"]))
    ropeC, ropeS = _rope_tables()
    cna_k = f32(inp["cache_na_k"]); cna_v = f32(inp["cache_na_v"]); st = f32(inp["state_rwkv"])
    cdk = f32(inp["cache_diff_k"]); cdv = f32(inp["cache_diff_v"])
    c_ctx = f32(inp["c_ctx"])
    maps = []
    for core in range(8):
        gi, r = core // 4, core % 4
        m = dict(common)
        xin = np.concatenate([x_prompt[4 * core:4 * core + 4].reshape(1024, 1024), x_sample[gi, 1024 * r:1024 * (r + 1)]], 0)
        m["xin"] = np.ascontiguousarray(xin)
        m["cvec"] = np.ascontiguousarray(np.transpose(_fm(np.stack([c_ctx, c[gi]], 0), 8), (0, 2, 1)))
        m["rope_c"] = np.ascontiguousarray(ropeC[:, 1024 * r:1024 * (r + 1)])
        m["rope_s"] = np.ascontiguousarray(ropeS[:, 1024 * r:1024 * (r + 1)])
        m["na_tab"] = np.ascontiguousarray(tab[:, 2 * r:2 * r + 2])
        m["cna_k"] = np.ascontiguousarray(cna_k[gi][:, :, 2 * r:2 * r + 2, :].reshape(2, 512, 128))
        m["cna_v"] = np.ascontiguousarray(cna_v[gi][:, :, 2 * r:2 * r + 2, :].reshape(2, 512, 128))
        s2 = st[gi][:, :, 2 * r:2 * r + 2]
        m["st_rw"] = np.ascontiguousarray(np.transpose(s2, (0, 1, 3, 2, 4)).reshape(2, 2, 64, 128))
        m["cdf_k"] = np.ascontiguousarray(np.transpose(cdk[gi][:, :, 2 * r:2 * r + 2, :], (0, 2, 1, 3)))
        m["cdf_v"] = np.ascontiguousarray(np.transpose(cdv[gi][:, :, 2 * r:2 * r + 2, :], (0, 2, 1, 3)))
        mu = common["rw_mu"]
        m["rw_mu"] = np.ascontiguousarray(np.concatenate([mu, mu[:, :, [r, 4 + r, 8 + r], :]], axis=2))
        for nm in ("rw_w0", "rw_a0", "rw_bonus", "rw_kk", "rw_ka", "rw_lnw", "rw_lnb"):
            a = common[nm]
            m[nm] = np.ascontiguousarray(np.concatenate([a, a[..., r:r + 1]], axis=-1))
        for nm in ("rw_w2", "rw_a2", "rw_g2"):
            a = common[nm]
            m[nm] = np.ascontiguousarray(np.concatenate([a, a[..., 128 * r:128 * (r + 1)]], axis=-1))
        maps.append(m)
    return maps


def assemble(results):
    y = np.stack([r["y"] for r in results], 0)
    y_prompt = y[:, :1024].reshape(32, 256, 1024)
    y_sample = y[:, 1024:].reshape(2, 4096, 1024)
    nk = np.stack([r["o_na_k"] for r in results], 0)
    nv = np.stack([r["o_na_v"] for r in results], 0)
    to_cache = lambda a, hd: np.ascontiguousarray(np.transpose(a.reshape(8, 2, 4, 256, 8, hd), (0, 2, 1, 3, 4, 5)).reshape(32, 2, 256, 8, hd))
    new_na_k = to_cache(nk, 64); new_na_v = to_cache(nv, 64)
    dk = np.stack([r["o_df_k"] for r in results], 0); dv = np.stack([r["o_df_v"] for r in results], 0)
    new_diff_k = to_cache(dk, 128); new_diff_v = to_cache(dv, 128)
    rw = np.stack([r["o_rw"] for r in results], 0)
    rw = rw.reshape(8, 2, 4, 2, 4, 64, 2, 64)
    rw = np.transpose(rw, (0, 2, 1, 3, 4, 6, 5, 7)).reshape(32, 2, 2, 8, 64, 64)
    return (np.ascontiguousarray(y_prompt), np.ascontiguousarray(y_sample), new_na_k, new_na_v,
            np.ascontiguousarray(rw), new_diff_k, new_diff_v)


def kernel(**inputs):
    maps = prep_inputs(inputs)
    if "nc" not in _CACHE:
        _CACHE["nc"] = build_program()[0]
    nc = _CACHE["nc"]
    res = run_bass_kernel_spmd(nc, maps, core_ids=list(range(8)))
    return assemble(res.results)
```

```python
import contextlib
import math
import numpy as np
import concourse.bass as bass
import concourse.mybir as mybir
from concourse.bass_utils import run_bass_kernel_spmd
F32 = mybir.dt.float32
BF16 = mybir.dt.bfloat16
I32 = mybir.dt.int32
AF = mybir.ActivationFunctionType
ALU = mybir.AluOpType
AX = mybir.AxisListType

EPOCH = 30000


class Buf:
    def __init__(self, name, t):
        self.name = name
        self.t = t
        self.st = {}

    def states(self, key):
        if key is None:
            if None not in self.st:
                self.st[None] = [None, {}]
            return list(self.st.values())
        if key not in self.st:
            self.st[key] = [None, {}]
        out = [self.st[key]]
        if None in self.st:
            out.append(self.st[None])
        return out

    def __getitem__(self, idx):
        return self.t[idx]


class KB:
    ENGS = ("pe", "dve", "act", "pool", "sp")

    def __init__(self, nc, stack):
        self.nc = nc
        self.stack = stack
        self.h = {"pe": nc.tensor, "dve": nc.vector, "act": nc.scalar, "pool": nc.gpsimd, "sp": nc.sync}
        self.stream = {e: [] for e in self.ENGS}
        self.sem = {}
        self.cnt = {}
        self.nsem = 0
        self.pe_sems = set()
        for e in self.ENGS:
            self._new_eng_sem(e)
        self.dpool = {}
        self.dnext = {}
        for e in ("sp", "act", "pool"):
            self.dpool[e] = [[self._mksem(f"d_{e}_{i}"), 0] for i in range(6)]
            self.dnext[e] = 0
        self.waited = {e: {} for e in self.ENGS}
        self.ninstr = 0

    def _mksem(self, name):
        self.nsem += 1
        return self.stack.enter_context(self.nc.semaphore(name))

    def _new_eng_sem(self, e):
        k = sum(1 for n in self.sem if n[0] == e) if False else None
        s = self._mksem(f"s_{e}_{self.nsem}")
        if e == "pe":
            self.pe_sems.add(id(s))
        self.sem[e] = s
        self.cnt[e] = 0

    def sb(self, name, shape, dtype=F32):
        t = self.stack.enter_context(self.nc.sbuf_tensor("sb_" + name, list(shape), dtype))
        return Buf(name, t)

    def ps(self, name, shape, dtype=F32):
        t = self.stack.enter_context(self.nc.psum_tensor("ps_" + name, list(shape), dtype))
        return Buf(name, t)

    def dram(self, name, shape, dtype=F32, kind="Internal"):
        if kind == "Internal":
            t = self.nc.dram_tensor(name, list(shape), dtype)
        else:
            t = self.nc.dram_tensor(name, list(shape), dtype, kind=kind)
        return Buf(name, t)

    def _deps(self, eng, reads, writes):
        toks = []
        for (b, k) in reads:
            for st in b.states(k):
                if st[0] is not None:
                    toks.append(st[0])
        for (b, k) in writes:
            for st in b.states(k):
                if st[0] is not None:
                    toks.append(st[0])
                for t in st[1].values():
                    toks.append(t)
        best = {}
        for (s, v) in toks:
            key = id(s)
            if key not in best or best[key][1] < v:
                best[key] = (s, v)
        out = []
        w = self.waited[eng]
        for key, (s, v) in best.items():
            if eng == "pe" and key in self.pe_sems:
                continue
            if w.get(key, 0) >= v:
                continue
            w[key] = v
            out.append((s, v))
        return out

    def _record(self, eng, tok, reads, writes):
        for (b, k) in reads:
            if k is None:
                for st in b.states(None):
                    st[1][eng] = tok
            else:
                b.states(k)[0][1][eng] = tok
        for (b, k) in writes:
            if k is None:
                for st in b.states(None):
                    st[0] = tok
                    st[1] = {}
            else:
                st = b.states(k)[0]
                st[0] = tok
                st[1] = {}

    @staticmethod
    def _norm(lst):
        out = []
        for x in lst:
            if isinstance(x, Buf):
                out.append((x, None))
            else:
                out.append(x)
        return out

    def op(self, eng, fn, reads=(), writes=()):
        reads = self._norm(reads)
        writes = self._norm(writes)
        waits = self._deps(eng, reads, writes)
        if self.cnt[eng] >= EPOCH:
            self._new_eng_sem(eng)
        self.cnt[eng] += 1
        s = self.sem[eng]
        v = self.cnt[eng]
        tok = (s, v)
        self.stream[eng].append((waits, fn, s, 1))
        self._record(eng, tok, reads, writes)
        self.ninstr += 1 + len(waits)
        return tok

    def I(self, eng, name, reads=(), writes=(), **kwargs):
        def fn(e, name=name, kwargs=kwargs):
            return getattr(e, name)(**kwargs)
        return self.op(eng, fn, reads, writes)

    def MM(self, out, lhsT, rhs, start, stop, reads=(), writes=()):
        def fn(e, out=out, lhsT=lhsT, rhs=rhs, start=start, stop=stop):
            return e.matmul(out, lhsT=lhsT, rhs=rhs, start=start, stop=stop)
        return self.op("pe", fn, reads, writes)

    def TR(self, out, in_, ident, reads=(), writes=()):
        def fn(e, out=out, in_=in_, ident=ident):
            return e.transpose(out, in_, ident)
        return self.op("pe", fn, reads, writes)

    def dma(self, eng, out_ap, in_ap, reads=(), writes=(), **kw):
        reads = self._norm(reads)
        writes = self._norm(writes)
        waits = self._deps(eng, reads, writes)
        pool = self.dpool[eng]
        j = self.dnext[eng]
        self.dnext[eng] = (j + 1) % len(pool)
        s, c = pool[j]
        w = self.waited[eng]
        if c > 0 and w.get(id(s), 0) < c:
            waits.append((s, c))
            w[id(s)] = c
        c += 16
        pool[j][1] = c
        tok = (s, c)

        def fn(e, out_ap=out_ap, in_ap=in_ap, kw=kw):
            o = out_ap(e) if callable(out_ap) else out_ap
            i = in_ap(e) if callable(in_ap) else in_ap
            return e.dma_start(out=o, in_=i, **kw)

        self.stream[eng].append((waits, fn, s, 16))
        self._record("dma_" + eng + str(j), tok, reads, writes)
        self.ninstr += 1 + len(waits)
        return tok

    def custom(self, eng, fn, sem_inc, reads=(), writes=()):
        reads = self._norm(reads)
        writes = self._norm(writes)
        waits = self._deps(eng, reads, writes)
        s = self._mksem(f"c_{self.nsem}")
        tok = (s, sem_inc)
        self.stream[eng].append((waits, fn, s, sem_inc))
        self._record("cc" + str(self.nsem), tok, reads, writes)
        return tok

    def wait_all(self, eng, bufs):
        reads = self._norm(bufs)
        waits = self._deps(eng, reads, [])
        self.stream[eng].append((waits, None, None, 0))

    def emit(self):
        nc = self.nc
        streams = self.stream

        def run(e, lst):
            for (waits, fn, s, inc) in lst:
                for (ws, wv) in waits:
                    e.wait_ge(ws, wv)
                if fn is not None:
                    ins = fn(e)
                    ins.then_inc(s, inc)

        with nc.Block() as block:
            @block.sync
            def _(e):
                run(e, streams["sp"])

            @block.tensor
            def _(e):
                run(e, streams["pe"])

            @block.vector
            def _(e):
                run(e, streams["dve"])

            @block.scalar
            def _(e):
                run(e, streams["act"])

            @block.gpsimd
            def _(e):
                run(e, streams["pool"])


D = 1024
KC = 8
NTOK = 2048
TT = 512
NT = 4
DFF = 2816
SCALE = 0.125
MASKV = -30000.0
GROUPS4 = [[0, 1, 2, 3], [4, 5, 6, 7]]


class Ctx:
    pass


def build_program(dbg=None, nlayers=4, mixers=True):
    nc = bass.Bass("TRN2", target_bir_lowering=False)
    st = contextlib.ExitStack()
    with st:
        kb = KB(nc, st)
        g = Ctx()
        g.nc, g.kb = nc, kb
        g.dbg = dbg or []
        g.dbg_out = {}
        _declare_io(g)
        _alloc(g)
        _consts(g)
        _load_x(g)
        for l in range(nlayers):
            _adaln(g, l)
            _norm_mod(g, l, 0)
            if mixers:
                if l % 2 == 0:
                    _even_mixer(g, l)
                else:
                    _odd_mixer(g, l)
            _norm_mod(g, l, 1)
            _ffn(g, l)
        _final(g)
        kb.wait_all("sp", g.outs)
        kb.wait_all("pool", g.outs)
        kb.wait_all("act", g.outs)
        kb.emit()
    return nc, g


def IN(g, name, shape, dt=F32):
    t = g.nc.dram_tensor(name, list(shape), dt, kind="ExternalInput")
    b = Buf(name, t)
    g.ins[name] = b
    return b


def OUT(g, name, shape, dt=F32):
    t = g.nc.dram_tensor(name, list(shape), dt, kind="ExternalOutput")
    b = Buf(name, t)
    g.outs.append(b)
    g.outd[name] = b
    return b


def _declare_io(g):
    g.ins = {}
    g.outs = []
    g.outd = {}
    IN(g, "xin", [NTOK, D])
    IN(g, "cvec", [128, KC, 2])
    IN(g, "w_ada", [4, D, 6 * D])
    IN(g, "b_ada", [128, 4, 48])
    IN(g, "norm_mix", [128, 4, 8])
    IN(g, "norm_ffn", [128, 4, 8])
    IN(g, "norm_final", [128, 8])
    IN(g, "w_in_even", [2, D, 3328])
    IN(g, "w_out_even", [2, D, D])
    IN(g, "w_qkv_diff", [2, D, 3072])
    IN(g, "w_qk_sw", [2, D, 2048])
    IN(g, "w_out_diff", [2, D, D])
    IN(g, "ffn_w1", [4, D, DFF])
    IN(g, "ffn_w3", [4, D, DFF])
    IN(g, "ffn_w2", [4, DFF, D])
    IN(g, "rope_c", [128, 1024])
    IN(g, "rope_s", [128, 1024])
    IN(g, "rw_mu", [128, 2, 17, 2])
    IN(g, "rw_w0", [128, 2, 2, 5])
    IN(g, "rw_a0", [128, 2, 2, 5])
    IN(g, "rw_kk", [128, 2, 5])
    IN(g, "rw_ka", [128, 2, 5])
    IN(g, "rw_lnw", [128, 2, 5])
    IN(g, "rw_lnb", [128, 2, 5])
    IN(g, "rw_bonus", [128, 2, 2, 5])
    IN(g, "rw_w2", [2, 2, 64, 640])
    IN(g, "rw_a2", [2, 2, 64, 640])
    IN(g, "rw_g2", [2, 128, 640])
    IN(g, "na_tab", [2, 2, 128, 5, 5, 128])
    IN(g, "cna_k", [2, 512, 128])
    IN(g, "cna_v", [2, 512, 128])
    IN(g, "st_rw", [2, 2, 64, 128])
    IN(g, "cdf_k", [2, 2, 512, 128])
    IN(g, "cdf_v", [2, 2, 512, 128])
    IN(g, "lamq", [128, 2, 128])
    IN(g, "lamk", [128, 2, 128])
    IN(g, "subln", [128, 2, 128])
    IN(g, "masks", [128, 4, 128])
    IN(g, "ident", [128, 128])
    IN(g, "lvlmask", [128, 7, 128])
    OUT(g, "y", [NTOK, D])
    OUT(g, "o_na_k", [2, 1024, 512])
    OUT(g, "o_na_v", [2, 1024, 512])
    OUT(g, "o_rw", [2, 4, 2, 4, 64, 128])
    OUT(g, "o_df_k", [2, 1024, 1024])
    OUT(g, "o_df_v", [2, 1024, 1024])
    for (name, shape, dt) in g.dbg:
        g.dbg_out[name] = OUT(g, "dbg_" + name, shape, dt)


class Rot:
    def __init__(self, bufs):
        self.bufs = bufs
        self.i = 0

    def __call__(self):
        b = self.bufs[self.i % len(self.bufs)]
        self.i += 1
        return b


def _alloc(g):
    kb = g.kb
    g.xT = kb.sb("xT", [128, KC, NTOK], F32)
    g.hT = kb.sb("hT", [128, KC, NTOK], BF16)
    g.PBL = [kb.ps(f"pb{i}", [128, 512], F32) for i in range(6)]
    g.nb = Rot(g.PBL)
    g.nbt = Rot([kb.ps(f"pt{i}", [128, 1024], BF16) for i in range(2)])
    g.nw = Rot([kb.sb(f"wbuf{i}", [128, KC * 512], BF16) for i in range(3)])
    g.mod = kb.sb("mod", [128, 48, 2], F32)
    g.modA = kb.sb("modA", [128, 2, 8, 2], F32)
    g.actb_flat = kb.sb("actb", [128, 4 * NTOK], BF16)
    g.actb = g.actb_flat
    g.scr = kb.sb("scr", [128, 22528], BF16)
    g.nstage = Rot([kb.sb(f"stage{i}", [128, 512], BF16) for i in range(3)])
    g.ntmp = Rot([kb.sb(f"tmpf{i}", [128, 512], F32) for i in range(3)])
    g.nR = Rot([kb.sb(f"Rb{i}", [128, 512], F32) for i in range(1)])
    g.nsmall = Rot([kb.sb(f"small{i}", [128, 8], F32) for i in range(4)])
    g.sm_neglam = kb.sb("neglam", [128, 1], F32)
    g.sm_gsub = kb.sb("gsub", [128, 128], F32)


def _consts(g):
    kb = g.kb
    I = g.ins
    g.ident_f = kb.sb("ident_f", [128, 128], F32)
    g.ident_b = kb.sb("ident_b", [128, 128], BF16)
    kb.dma("sp", g.ident_f[:, :], I["ident"][:, :], reads=[I["ident"]], writes=[g.ident_f])
    kb.dma("pool", g.ident_b[:, :], I["ident"][:, :], reads=[I["ident"]], writes=[g.ident_b])
    g.ones_b = kb.sb("ones_b", [128, 128], BF16)
    kb.I("pool", "memset", writes=[g.ones_b], ap=g.ones_b[:, :], constant=1.0)
    g.blk_b = kb.sb("blk_b", [128, 128], BF16)
    kb.I("pool", "memset", writes=[g.blk_b], ap=g.blk_b[:, :], constant=0.0)
    kb.I("pool", "memset", writes=[g.blk_b], ap=g.blk_b[0:64, 0:64], constant=1.0)
    kb.I("pool", "memset", writes=[g.blk_b], ap=g.blk_b[64:128, 64:128], constant=1.0)
    g.cv = kb.sb("cv", [128, KC, 2], F32)
    kb.dma("sp", g.cv[:, :, :], I["cvec"][:, :, :], reads=[I["cvec"]], writes=[g.cv])
    g.cvf = kb.sb("cvf", [128, KC, 2], F32)
    kb.I("act", "activation", reads=[g.cv], writes=[g.cvf], out=g.cvf[:, :, :], in_=g.cv[:, :, :], func=AF.Silu)
    g.b_ada = kb.sb("b_ada", [128, 4, 48], F32)
    kb.dma("sp", g.b_ada[:, :, :], I["b_ada"][:, :, :], reads=[I["b_ada"]], writes=[g.b_ada])
    g.nrm = kb.sb("nrm", [128, 2, 4, 8], F32)
    kb.dma("sp", g.nrm[:, 0, :, :], I["norm_mix"][:, :, :], reads=[I["norm_mix"]], writes=[g.nrm])
    kb.dma("sp", g.nrm[:, 1, :, :], I["norm_ffn"][:, :, :], reads=[I["norm_ffn"]], writes=[g.nrm])
    g.nrmf = kb.sb("nrmf", [128, 8], F32)
    kb.dma("sp", g.nrmf[:, :], I["norm_final"][:, :], reads=[I["norm_final"]], writes=[g.nrmf])
    g.eps_t = kb.sb("eps_t", [128, 2], F32)
    kb.I("pool", "memset", writes=[g.eps_t], ap=g.eps_t[:, 0:1], constant=1e-6)
    kb.I("pool", "memset", writes=[g.eps_t], ap=g.eps_t[:, 1:2], constant=64e-5)


def dbg_dump(g, name, ap, buf, key=None):
    if name in g.dbg_out:
        o = g.dbg_out[name]
        g.kb.dma("sp", o.t.ap(), ap, reads=[(buf, key)], writes=[o])


def evac(g, i, out, in_, reads, writes):
    if i % 2 == 0:
        g.kb.I("dve", "tensor_copy", reads=reads, writes=writes, out=out, in_=in_)
    else:
        g.kb.I("act", "copy", reads=reads, writes=writes, out=out, in_=in_)


def _load_x(g):
    kb = g.kb
    X = g.ins["xin"]
    for tb in range(NTOK // 128):
        xt = g.ntmp()
        xt2 = g.ntmp()
        kb.dma("sp", xt[:, :], X[tb * 128:(tb + 1) * 128, 0:512], reads=[X], writes=[xt])
        kb.dma("act", xt2[:, :], X[tb * 128:(tb + 1) * 128, 512:1024], reads=[X], writes=[xt2])
        for half, src in ((0, xt), (1, xt2)):
            pb = g.nb()
            for j in range(4):
                kb.TR(pb[:, j * 128:(j + 1) * 128], src[:, j * 128:(j + 1) * 128], g.ident_f[:, :], reads=[src, g.ident_f], writes=[pb])
            dst = g.xT[:, half * 4:(half + 1) * 4, tb * 128:(tb + 1) * 128]
            evac(g, half, dst, pb[:, :].rearrange("p (j t) -> p j t", j=4), [pb], [(g.xT, tb // 4)])


def wload(g, src_buf, ap, kc, wdt):
    wb = g.nw()
    view = wb[:, 0:kc * wdt].rearrange("p (k n) -> p k n", k=kc)
    g.kb.dma("pool", view, ap, reads=[src_buf], writes=[wb])
    return wb, view


def wcols(W, idx, c0, c1):
    return W[idx].rearrange("(kc p) n -> p kc n", p=128)[:, :, c0:c1]


def _adaln(g, l):
    kb = g.kb
    W = g.ins["w_ada"]
    pb = g.nb()
    for gi, c0 in enumerate(range(0, 6 * D, 256)):
        wb = g.nw()
        wv = wb[:, 0:KC * 512].bitcast(F32).rearrange("p (k n) -> p k n", k=KC)
        kb.dma("sp" if gi % 2 == 0 else "act", wv, wcols(W, l, c0, c0 + 256), reads=[W], writes=[wb])
        for j in range(2):
            blk = c0 // 128 + j
            for kc in range(KC):
                kb.MM(pb[:, blk * 2:(blk + 1) * 2], wv[:, kc, j * 128:(j + 1) * 128], g.cvf[:, kc, :], kc == 0, kc == KC - 1, reads=[wb, g.cvf], writes=[pb])
    kb.I("dve", "tensor_tensor", reads=[pb, g.b_ada], writes=[g.mod], out=g.mod[:, :, :], in0=pb[:, 0:96].rearrange("p (b j) -> p b j", j=2),
         in1=g.b_ada[:, l, :].unsqueeze(2).broadcast_to([128, 48, 2]), op=ALU.add)
    for which in range(2):
        sc0 = 8 if which == 0 else 32
        kb.I("dve", "scalar_tensor_tensor", reads=[g.mod, g.nrm], writes=[g.modA], out=g.modA[:, which, :, :], in0=g.mod[:, sc0:sc0 + 8, :], scalar=1.0,
             in1=g.nrm[:, which, l, :].unsqueeze(2).broadcast_to([128, 8, 2]), op0=ALU.add, op1=ALU.mult)


def _rms_scale(g, ti, R):
    kb = g.kb
    t0 = ti * TT
    pb = g.nb()
    for kc in range(KC):
        sq = g.nstage()
        kb.I("act", "activation", reads=[(g.xT, ti)], writes=[sq], out=sq[:, :], in_=g.xT[:, kc, t0:t0 + TT], func=AF.Square)
        kb.MM(pb[:, :], g.ones_b[:, :], sq[:, :], kc == 0, kc == KC - 1, reads=[sq, g.ones_b], writes=[pb])
    kb.I("act", "activation", reads=[pb, g.eps_t], writes=[R], out=R[:, :], in_=pb[:, :], func=AF.Sqrt, scale=1.0 / D, bias=g.eps_t[:, 0:1])
    kb.I("dve", "reciprocal", reads=[R], writes=[R], out=R[:, :], in_=R[:, :])


def _norm_mod(g, l, which):
    kb = g.kb
    sh0 = 0 if which == 0 else 24
    for ti in range(NT):
        t0 = ti * TT
        grp = 0 if ti < 2 else 1
        R = g.nR()
        _rms_scale(g, ti, R)
        for kc in range(KC):
            tmp = g.ntmp()
            kb.I("dve", "tensor_tensor", reads=[(g.xT, ti), R], writes=[tmp], out=tmp[:, :], in0=g.xT[:, kc, t0:t0 + TT], in1=R[:, :], op=ALU.mult)
            kb.I("act", "activation", reads=[tmp, g.modA, g.mod], writes=[(g.hT, ti)], out=g.hT[:, kc, t0:t0 + TT], in_=tmp[:, :], func=AF.Identity,
                 scale=g.modA[:, which, kc, grp:grp + 1], bias=g.mod[:, sh0 + kc, grp:grp + 1])


def _ffn(g, l):
    kb = g.kb
    W1, W3, W2 = g.ins["ffn_w1"], g.ins["ffn_w3"], g.ins["ffn_w2"]
    actv = g.actb_flat[:, :].rearrange("p (b t) -> p b t", b=4)
    for c0 in range(0, DFF, 512):
        wdt = min(512, DFF - c0)
        nblk = wdt // 128
        w1b, w1v = wload(g, W1, wcols(W1, l, c0, c0 + wdt), KC, wdt)
        w3b, w3v = wload(g, W3, wcols(W3, l, c0, c0 + wdt), KC, wdt)
        for j in range(nblk):
            for ti in range(NT):
                t0 = ti * TT
                pa = g.nb()
                pbb = g.nb()
                for kc in range(KC):
                    kb.MM(pa[:, :], w1v[:, kc, j * 128:(j + 1) * 128], g.hT[:, kc, t0:t0 + TT], kc == 0, kc == KC - 1, reads=[w1b, (g.hT, ti)], writes=[pa])
                for kc in range(KC):
                    kb.MM(pbb[:, :], w3v[:, kc, j * 128:(j + 1) * 128], g.hT[:, kc, t0:t0 + TT], kc == 0, kc == KC - 1, reads=[w3b, (g.hT, ti)], writes=[pbb])
                sa = g.ntmp()
                kb.I("act", "activation", reads=[pa], writes=[sa], out=sa[:, :], in_=pa[:, :], func=AF.Silu)
                kb.I("dve", "tensor_tensor", reads=[sa, pbb], writes=[(g.actb, ti)], out=actv[:, j, t0:t0 + TT], in0=sa[:, :], in1=pbb[:, :], op=ALU.mult)
        w2b = g.nw()
        w2v = w2b[:, 0:nblk * 1024].rearrange("p (k n) -> p k n", k=nblk)
        kb.dma("pool", w2v, W2[l, c0:c0 + wdt, :].rearrange("(kc p) n -> p kc n", p=128), reads=[W2], writes=[w2b])
        for ob in range(8):
            for ti in range(NT):
                t0 = ti * TT
                grp = 0 if ti < 2 else 1
                py = g.nb()
                for j in range(nblk):
                    kb.MM(py[:, :], w2v[:, j, ob * 128:(ob + 1) * 128], actv[:, j, t0:t0 + TT], j == 0, j == nblk - 1, reads=[w2b, (g.actb, ti)], writes=[py])
                kb.I("dve", "scalar_tensor_tensor", reads=[py, g.mod, (g.xT, ti)], writes=[(g.xT, ti)], out=g.xT[:, ob, t0:t0 + TT], in0=py[:, :],
                     scalar=g.mod[:, 40 + ob, grp:grp + 1], in1=g.xT[:, ob, t0:t0 + TT], op0=ALU.mult, op1=ALU.add)


def _final(g):
    kb = g.kb
    Y = g.outd["y"]
    for ti in range(NT):
        t0 = ti * TT
        R = g.nR()
        _rms_scale(g, ti, R)
        for kc in range(KC):
            kb.I("dve", "scalar_tensor_tensor", reads=[(g.xT, ti), R, g.nrmf], writes=[(g.xT, ti)], out=g.xT[:, kc, t0:t0 + TT], in0=g.xT[:, kc, t0:t0 + TT],
                 scalar=g.nrmf[:, kc:kc + 1], in1=R[:, :], op0=ALU.mult, op1=ALU.mult)
        for tb in range(4):
            tt0 = t0 + tb * 128
            for half in range(2):
                pb = g.nb()
                for j in range(4):
                    kc = half * 4 + j
                    kb.TR(pb[:, j * 128:(j + 1) * 128], g.xT[:, kc, tt0:tt0 + 128], g.ident_f[:, :], reads=[(g.xT, ti), g.ident_f], writes=[pb])
                o = g.ntmp()
                evac(g, half, o[:, :], pb[:, :], [pb], [o])
                kb.dma("sp", Y[tt0:tt0 + 128, half * 512:(half + 1) * 512], o[:, :], reads=[o], writes=[(Y, (tt0, half))])


def carve(buf, off_bytes, shape, dtype):
    n = 1
    for s in shape[1:]:
        n *= s
    esz = 4 if dtype == F32 else 2
    a = buf[:, off_bytes // 2: off_bytes // 2 + n * esz // 2]
    if dtype == F32:
        a = a.bitcast(F32)
    if len(shape) == 2:
        return a
    names = " ".join(f"d{i}" for i in range(1, len(shape)))
    kw = {f"d{i}": shape[i] for i in range(1, len(shape) - 1)}
    return a.rearrange(f"p ({names}) -> p {names}", **kw)


_RANKC = {}


def rank_expr(e):
    k = id(e)
    if k not in _RANKC:
        _RANKC.clear()
        _RANKC[k] = e.snap(e.partition_id() % 4)
    return _RANKC[k]


class Xchg:
    def __init__(self, g, name, nb):
        kb = g.kb
        self.g, self.nb, self.h = g, nb, nb // 2
        self.gin = kb.dram(name + "_gin", [4, nb, 128, 1024], BF16)
        self.gout = kb.dram(name + "_gout", [4, 2, 4, self.h * 128, 1024], BF16)
        self.mine = kb.dram(name + "_mine", [2, 4, self.h * 128, 1024], BF16)

    def row(self, rk, j):
        return self.gin[rk, j]

    def run(self):
        kb, h = self.g.kb, self.h
        for rk in range(4):
            for part in range(2):
                i_ap = self.gin[rk, part * h:(part + 1) * h].rearrange("a p t -> (a p) t")
                o_ap = self.gout[rk, part].rearrange("r q t -> (r q) t")
                kb.custom("pool", (lambda en, i_ap=i_ap, o_ap=o_ap: en.collective_compute("AllGather", ALU.bypass, replica_groups=GROUPS4, ins=[i_ap.opt()], outs=[o_ap.opt()])), 1,
                          reads=[self.gin], writes=[self.gout])
        for part in range(2):
            src = self.gout.t.ap()
            kb.dma("pool", self.mine[part].rearrange("r q t -> r (q t)"),
                   (lambda en, part=part, src=src: src[bass.ds(rank_expr(en), 1), part].rearrange("a r q t -> (a r) (q t)")), reads=[self.gout], writes=[self.mine])

    def get(self, rr, j):
        return self.mine[j // self.h, rr, (j % self.h) * 128:(j % self.h + 1) * 128, :]


class XchgOut:
    def __init__(self, g, name):
        kb = g.kb
        self.g = g
        self.gin = kb.dram(name + "_gin", [2, 128, 4096], BF16)
        self.gout = kb.dram(name + "_gout", [2, 4, 128, 4096], BF16)
        self.mine = kb.dram(name + "_mine", [1024, 1024], BF16)

    def run(self):
        kb = self.g.kb
        for w in range(2):
            i_ap = self.gin[w]
            o_ap = self.gout[w].rearrange("r p t -> (r p) t")
            kb.custom("pool", (lambda en, i_ap=i_ap, o_ap=o_ap: en.collective_compute("AllGather", ALU.bypass, replica_groups=GROUPS4, ins=[i_ap.opt()], outs=[o_ap.opt()])), 1,
                      reads=[self.gin], writes=[self.gout])
        src = self.gout.t.ap().rearrange("w r p (k t) -> (w r p) k t", k=4)
        kb.dma("pool", self.mine.t.ap(), (lambda en: src[:, bass.ds(rank_expr(en), 1), :].rearrange("f a t -> f (a t)")), reads=[self.gout], writes=[self.mine])


def out_proj(g, W, widx, gate0):
    kb = g.kb
    for c0 in range(0, D, 512):
        wb, wv = wload(g, W, wcols(W, widx, c0, c0 + 512), KC, 512)
        for j in range(4):
            ob = c0 // 128 + j
            for ti in range(NT):
                t0 = ti * TT
                grp = 0 if ti < 2 else 1
                py = g.nb()
                for kc in range(KC):
                    kb.MM(py[:, :], wv[:, kc, j * 128:(j + 1) * 128], g.hT[:, kc, t0:t0 + TT], kc == 0, kc == KC - 1, reads=[wb, (g.hT, ti)], writes=[py])
                kb.I("dve", "scalar_tensor_tensor", reads=[py, g.mod, (g.xT, ti)], writes=[(g.xT, ti)], out=g.xT[:, ob, t0:t0 + TT], in0=py[:, :],
                     scalar=g.mod[:, gate0 + ob, grp:grp + 1], in1=g.xT[:, ob, t0:t0 + TT], op0=ALU.mult, op1=ALU.add)


def tokmajor_out(g, wb, wv, ncols, OUTB, oidx, col0, extra=None):
    kb = g.kb
    for tb in range(8):
        pb = g.nb()
        for kc in range(KC):
            kb.MM(pb[:, 0:ncols], g.hT[:, kc, tb * 128:(tb + 1) * 128], wv[:, kc, 0:ncols], kc == 0, kc == KC - 1, reads=[wb, (g.hT, tb // 4)], writes=[pb])
        o32 = g.ntmp()
        evac(g, tb, o32[:, 0:ncols], pb[:, 0:ncols], [pb], [o32])
        kb.dma("sp", OUTB[oidx, tb * 128:(tb + 1) * 128, col0:col0 + ncols], o32[:, 0:ncols], reads=[o32], writes=[(OUTB, (oidx, tb, col0))])
        if extra is not None:
            extra(tb, o32)


def diff_combine(g, A0, B0, A1, B1, dst, neglam, gsub):
    kb = g.kb
    st = g.nsmall()
    kb.I("dve", "reciprocal", reads=[B0], writes=[st], out=st[:, 0:1], in_=A0[:, 128:129])
    kb.I("dve", "reciprocal", reads=[B1], writes=[st], out=st[:, 1:2], in_=A1[:, 128:129])
    kb.I("dve", "tensor_tensor", reads=[st, neglam], writes=[st], out=st[:, 2:3], in0=st[:, 1:2], in1=neglam[:, 0:1], op=ALU.mult)
    O = g.ntmp()
    kb.I("dve", "tensor_scalar", reads=[B0, st], writes=[O], out=O[:, 0:128], in0=A0[:, 0:128], scalar1=st[:, 0:1], scalar2=None, op0=ALU.mult)
    kb.I("dve", "scalar_tensor_tensor", reads=[B1, st, O], writes=[O], out=O[:, 0:128], in0=A1[:, 0:128], scalar=st[:, 2:3], in1=O[:, 0:128], op0=ALU.mult, op1=ALU.add)
    kb.I("act", "activation", reads=[O], writes=[O, st], out=O[:, 128:256], in_=O[:, 0:128], func=AF.Square, accum_out=st[:, 3:4])
    kb.I("act", "activation", reads=[st, g.eps_t], writes=[st], out=st[:, 4:5], in_=st[:, 3:4], func=AF.Sqrt, scale=1.0 / 128, bias=g.eps_t[:, 0:1])
    kb.I("dve", "reciprocal", reads=[st], writes=[st], out=st[:, 5:6], in_=st[:, 4:5])
    kb.I("dve", "scalar_tensor_tensor", reads=[O, st, gsub], writes=[dst[1]], out=dst[0], in0=O[:, 0:128], scalar=st[:, 5:6], in1=gsub[:, :], op0=ALU.mult, op1=ALU.mult)


def _odd_mixer(g, l):
    kb = g.kb
    I = g.ins
    o = l // 2
    lam_init = 0.8 - 0.6 * math.exp(-0.3 * l)
    W, Wsw, Wout = I["w_qkv_diff"], I["w_qk_sw"], I["w_out_diff"]
    KO, VO = g.outd["o_df_k"], g.outd["o_df_v"]
    pinP = kb.dram(f"pinP{l}", [3072, 1024], BF16)
    XS = Xchg(g, f"oxs{l}", 6)
    XO = XchgOut(g, f"oxo{l}")
    scr = g.scr
    neglam = g.sm_neglam
    gsub = g.sm_gsub
    t = g.ntmp()
    kb.dma("sp", t[:, 256:384], I["lamq"][:, o, :], reads=[I["lamq"]], writes=[t])
    kb.dma("sp", t[:, 384:512], I["lamk"][:, o, :], reads=[I["lamk"]], writes=[t])
    kb.I("dve", "tensor_tensor", reads=[t], writes=[t], out=t[:, 0:128], in0=t[:, 256:384], in1=t[:, 384:512], op=ALU.mult)
    kb.I("dve", "tensor_reduce", reads=[t], writes=[t], out=t[:, 128:130], in_=t[:, 0:128].rearrange("p (a b) -> p a b", a=2), axis=AX.X, op=ALU.add)
    kb.I("act", "activation", reads=[t], writes=[t], out=t[:, 130:132], in_=t[:, 128:130], func=AF.Exp)
    kb.I("dve", "tensor_tensor", reads=[t], writes=[neglam], out=neglam[:, 0:1], in0=t[:, 131:132], in1=t[:, 130:131], op=ALU.subtract)
    kb.I("dve", "tensor_scalar", reads=[neglam], writes=[neglam], out=neglam[:, 0:1], in0=neglam[:, 0:1], scalar1=-lam_init, scalar2=None, op0=ALU.add)
    kb.dma("sp", gsub[:, :], I["subln"][:, o, :], reads=[I["subln"]], writes=[gsub])
    kb.I("dve", "tensor_scalar", reads=[gsub], writes=[gsub], out=gsub[:, :], in0=gsub[:, :], scalar1=1.0 - lam_init, scalar2=None, op0=ALU.mult)
    ropeC = carve(g.actb_flat, 0, [128, 1024], F32)
    ropeS = carve(g.actb_flat, 4096, [128, 1024], F32)
    kb.dma("sp", ropeC, I["rope_c"][:, :], reads=[I["rope_c"]], writes=[g.actb_flat])
    kb.dma("sp", ropeS, I["rope_s"][:, :], reads=[I["rope_s"]], writes=[g.actb_flat])
    vtokP = carve(scr, 0, [128, 8, 8, 129], BF16)
    kb.I("pool", "memset", writes=[(scr, "vtokP")], ap=vtokP[:, :, :, 128:129], constant=1.0)
    ev = 0
    for gi in range(6):
        c0 = gi * 512
        wb, wv = wload(g, W, wcols(W, o, c0, c0 + 512), KC, 512)
        if gi < 4:
            wsb, wsv = wload(g, Wsw, wcols(Wsw, o, c0, c0 + 512), KC, 512)
        for j in range(4):
            blk = gi * 4 + j
            for ti in range(NT):
                t0 = ti * TT
                pb = g.nb()
                for kc in range(KC):
                    kb.MM(pb[:, :], wv[:, kc, j * 128:(j + 1) * 128], g.hT[:, kc, t0:t0 + TT], kc == 0, kc == KC - 1, reads=[wb, (g.hT, ti)], writes=[pb])
                stg = g.nstage()
                if ti < 2 or gi >= 4:
                    evac(g, ev, stg[:, :], pb[:, :], [pb], [stg]); ev += 1
                else:
                    pb2 = g.nb()
                    for kc in range(KC):
                        kb.MM(pb2[:, :], wsv[:, kc, j * 128:(j + 1) * 128], g.hT[:, kc, t0:t0 + TT], kc == 0, kc == KC - 1, reads=[wsb, (g.hT, ti)], writes=[pb2])
                    cs = (ti - 2) * 512
                    t1 = g.ntmp(); t2 = g.ntmp()
                    kb.I("dve", "tensor_tensor", reads=[pb, g.actb_flat], writes=[t1], out=t1[:, :], in0=pb[:, :], in1=ropeC[:, cs:cs + 512], op=ALU.mult)
                    kb.I("dve", "tensor_tensor", reads=[pb2, g.actb_flat], writes=[t2], out=t2[:, :], in0=pb2[:, :], in1=ropeS[:, cs:cs + 512], op=ALU.mult)
                    kb.I("pool", "tensor_tensor", reads=[t1, t2], writes=[stg], out=stg[:, :], in0=t1[:, :], in1=t2[:, :], op=ALU.add)
                cc = (ti % 2) * 512
                if ti < 2:
                    kb.dma("sp", pinP[blk * 128:(blk + 1) * 128, cc:cc + 512], stg[:, :], reads=[stg], writes=[(pinP, (blk, ti))])
                else:
                    kind, H = blk // 8, blk % 8
                    kb.dma("sp", XS.row(H // 2, kind * 2 + H % 2)[:, cc:cc + 512], stg[:, :], reads=[stg], writes=[(XS.gin, (blk, ti))])
        if gi >= 2:
            if gi < 4:
                tokmajor_out(g, wb, wv, 512, KO, o, (gi - 2) * 512)
            else:
                h0 = (gi - 4) * 4

                def extra(tb, o32, h0=h0):
                    kb.I("pool", "tensor_copy", reads=[o32], writes=[(scr, "vtokP")], out=vtokP[:, tb, h0:h0 + 4, 0:128], in_=o32[:, :].rearrange("p (h d) -> p h d", h=4))
                tokmajor_out(g, wb, wv, 512, VO, o, (gi - 4) * 512, extra)
    import os
    STOP = os.environ.get("ODDSTOP", "")
    XS.run()
    if STOP == "proj":
        return
    qk = carve(scr, 16512, [128, 2, 8, 256], BF16)
    pTs = [carve(scr, 24704 + i * 1024, [128, 512], BF16) for i in range(2)]
    Otok = carve(scr, 26752, [128, 2, 1024], BF16)
    for b in range(4):
        for w in range(2):
            kb.dma("sp", qk[:, w, :, :], pinP[w * 1024:(w + 1) * 1024, b * 256:(b + 1) * 256].rearrange("(h p) t -> p h t", p=128), reads=[pinP], writes=[(scr, "qk")])
        for h in range(8):
            pos = []
            for s in range(2):
                ps = g.nb()
                for kc2 in range(2):
                    kb.MM(ps[:, kc2 * 256:(kc2 + 1) * 256], qk[64 * s:64 * s + 64, 1, h, kc2 * 128:(kc2 + 1) * 128], qk[64 * s:64 * s + 64, 0, h, :], True, True,
                          reads=[(scr, "qk")], writes=[ps])
                pT = pTs[s]
                kb.I("act", "activation", reads=[ps], writes=[(scr, ("pT", s))], out=pT, in_=ps[:, :], func=AF.Exp, scale=SCALE)
                po = g.nb()
                for qb in range(2):
                    for kc2 in range(2):
                        kb.MM(po[:, qb * 129:(qb + 1) * 129], pT[:, kc2 * 256 + qb * 128:kc2 * 256 + (qb + 1) * 128], vtokP[:, b * 2 + kc2, h, :], kc2 == 0, kc2 == 1,
                              reads=[(scr, ("pT", s)), (scr, "vtokP")], writes=[po])
                pos.append(po)
            for qb in range(2):
                diff_combine(g, pos[0][:, qb * 129:(qb + 1) * 129], pos[0], pos[1][:, qb * 129:(qb + 1) * 129], pos[1],
                             (Otok[:, qb, h * 128:(h + 1) * 128], (scr, "Otok")), neglam, gsub)
        for qb in range(2):
            pt = g.nbt()
            for blk in range(8):
                kb.TR(pt[:, blk * 128:(blk + 1) * 128], Otok[:, qb, blk * 128:(blk + 1) * 128], g.ident_b[:, :], reads=[(scr, "Otok"), g.ident_b], writes=[pt])
            evac(g, qb, g.hT[:, :, b * 256 + qb * 128:b * 256 + (qb + 1) * 128], pt[:, :].rearrange("p (k t) -> p k t", k=8), [pt], [(g.hT, b // 2)])
    if STOP == "prompt":
        return
    QT = carve(scr, 0, [128, 4096], BF16)
    KT = carve(scr, 8192, [128, 4608], BF16)
    Vtok = carve(scr, 17408, [128, 36, 129], BF16)
    pTs = [carve(scr, 26696 + i * 1024, [128, 512], BF16) for i in range(2)]
    OtS = carve(scr, 28744, [128, 32, 128], BF16)
    VTt = carve(scr, 28744, [128, 4096], BF16)
    O1s = carve(scr, 36936, [128, 4, 129], F32)
    ACC = g.PBL[0:4]
    PSR = Rot(g.PBL[4:6])
    for hh in range(2):
        allS = [(scr, None)]
        for rr in range(4):
            for (dst, kind, nm) in ((QT, 0, "QT"), (KT, 1, "KT"), (VTt, 2, "VT")):
                off = 512 if nm == "KT" else 0
                kb.dma("sp" if rr % 2 == 0 else "act", dst[:, off + rr * 1024:off + (rr + 1) * 1024], XS.get(rr, kind * 2 + hh), reads=[XS.mine], writes=allS)
        kst = g.nstage()
        kstv = kst[:, :].rearrange("p (c d) -> p c d", c=4)
        kb.dma("pool", kstv, I["cdf_k"][o, hh].rearrange("(c p) d -> p c d", p=128), reads=[I["cdf_k"]], writes=[kst])
        pt = g.nbt()
        for c in range(4):
            kb.TR(pt[:, c * 128:(c + 1) * 128], kstv[:, c, :], g.ident_b[:, :], reads=[kst, g.ident_b], writes=[pt])
        evac(g, 0, KT[:, 0:512], pt[:, 0:512], [pt], allS)
        vst = g.nstage()
        vstv = vst[:, :].rearrange("p (c d) -> p c d", c=4)
        kb.dma("pool", vstv, I["cdf_v"][o, hh].rearrange("(c p) d -> p c d", p=128), reads=[I["cdf_v"]], writes=[vst])
        kb.I("pool", "tensor_copy", reads=[vst], writes=allS, out=Vtok[:, 0:4, 0:128], in_=vstv)
        kb.I("pool", "memset", writes=allS, ap=Vtok[:, :, 128:129], constant=1.0)
        for c8 in range(4):
            pt = g.nbt()
            for j in range(8):
                c = c8 * 8 + j
                kb.TR(pt[:, j * 128:(j + 1) * 128], VTt[:, c * 128:(c + 1) * 128], g.ident_b[:, :], reads=allS + [g.ident_b], writes=[pt])
            evac(g, c8, Vtok[:, 4 + c8 * 8:4 + (c8 + 1) * 8, 0:128], pt[:, :].rearrange("p (c d) -> p c d", c=8), [pt], allS)
        if STOP == "sload":
            return
        for qg in range(8 if STOP != "sattn1" else 1):
            for s in range(2):
                for kc in range(36):
                    ps = PSR()
                    kb.MM(ps[:, :], KT[64 * s:64 * s + 64, kc * 128:(kc + 1) * 128], QT[64 * s:64 * s + 64, qg * 512:(qg + 1) * 512], True, True, reads=allS, writes=[ps])
                    pT = pTs[kc % 2]
                    kb.I("act", "activation", reads=[ps], writes=[(scr, ("pTs", kc % 2))], out=pT, in_=ps[:, :], func=AF.Exp, scale=SCALE)
                    for qb in range(4):
                        kb.MM(ACC[qb][:, 0:129], pT[:, qb * 128:(qb + 1) * 128], Vtok[:, kc, :], kc == 0, kc == 35,
                              reads=[(scr, ("pTs", kc % 2)), (scr, "static")], writes=[ACC[qb]])
                if s == 0:
                    for qb in range(4):
                        evac(g, qb, O1s[:, qb, :], ACC[qb][:, 0:129], [ACC[qb]], [(scr, ("O1s", qb))])
            for qb in range(4):
                diff_combine(g, O1s[:, qb, :], (scr, ("O1s", qb)), ACC[qb][:, 0:129], ACC[qb], (OtS[:, qg * 4 + qb, :], (scr, ("OtS", qg))), neglam, gsub)
        oTs = QT
        for c8 in range(4):
            pt = g.nbt()
            for j in range(8):
                c = c8 * 8 + j
                kb.TR(pt[:, j * 128:(j + 1) * 128], OtS[:, c, :], g.ident_b[:, :], reads=[(scr, None), g.ident_b], writes=[pt])
            evac(g, c8, oTs[:, c8 * 1024:(c8 + 1) * 1024], pt[:, :], [pt], [(scr, None)])
        kb.dma("sp", XO.gin[hh], oTs[:, :], reads=[(scr, None)], writes=[(XO.gin, hh)])
    XO.run()
    mo = XO.mine.t.ap().rearrange("(w rr p) t -> p w rr t", w=2, rr=4)
    for w in range(2):
        kb.dma("sp" if w == 0 else "act", g.hT[:, :, 1024:2048].rearrange("p (rr w) t -> p w rr t", w=2)[:, w], mo[:, w], reads=[XO.mine], writes=[(g.hT, 2), (g.hT, 3)])
    out_proj(g, Wout, o, 16)


def _even_mixer(g, l):
    kb = g.kb
    I = g.ins
    e = l // 2
    W, Wout = I["w_in_even"], I["w_out_even"]
    KO, VO = g.outd["o_na_k"], g.outd["o_na_v"]
    pinP = kb.dram(f"epinP{l}", [3328, 1024], BF16)
    XS = Xchg(g, f"exs{l}", 6)
    XO = XchgOut(g, f"exo{l}")
    lin = kb.dram(f"elin{l}", [256, 1024], BF16)
    lout = kb.dram(f"elout{l}", [4, 256, 1024], BF16)
    scr = g.scr
    vtokP = carve(scr, 0, [128, 8, 8, 65], BF16)
    kb.I("pool", "memset", writes=[(scr, "vtokP")], ap=vtokP[:, :, :, 64:65], constant=1.0)
    ev = 0
    for gi, c0 in enumerate(range(0, 3328, 512)):
        wdt = min(512, 3328 - c0)
        wb, wv = wload(g, W, wcols(W, e, c0, c0 + wdt), KC, wdt)
        for j in range(wdt // 128):
            blk = c0 // 128 + j
            for ti in range(NT):
                t0 = ti * TT
                pb = g.nb()
                for kc in range(KC):
                    kb.MM(pb[:, :], wv[:, kc, j * 128:(j + 1) * 128], g.hT[:, kc, t0:t0 + TT], kc == 0, kc == KC - 1, reads=[wb, (g.hT, ti)], writes=[pb])
                stg = g.nstage()
                evac(g, ev, stg[:, :], pb[:, :], [pb], [stg]); ev += 1
                cc = (ti % 2) * 512
                if ti < 2:
                    kb.dma("sp", pinP[blk * 128:(blk + 1) * 128, cc:cc + 512], stg[:, :], reads=[stg], writes=[(pinP, (blk, ti))])
                elif blk < 24:
                    kb.dma("sp", XS.row(blk % 4, blk // 4)[:, cc:cc + 512], stg[:, :], reads=[stg], writes=[(XS.gin, (blk, ti))])
                else:
                    kb.dma("sp", lin[(blk - 24) * 128:(blk - 23) * 128, cc:cc + 512], stg[:, :], reads=[stg], writes=[(lin, (blk, ti))])
        if gi == 1:
            tokmajor_out(g, wb, wv, 512, KO, e, 0)
        if gi == 2:
            def extra(tb, o32):
                kb.I("pool", "tensor_copy", reads=[o32], writes=[(scr, "vtokP")], out=vtokP[:, tb, :, 0:64], in_=o32[:, :].rearrange("p (h d) -> p h d", h=8))
            tokmajor_out(g, wb, wv, 512, VO, e, 0, extra)
    XS.run()
    kb.custom("pool", (lambda en: en.collective_compute("AllGather", ALU.bypass, replica_groups=GROUPS4, ins=[lin.t.ap().opt()], outs=[lout.t.ap().rearrange("r q t -> (r q) t").opt()])), 1,
              reads=[lin], writes=[lout])

    qk = carve(scr, 8320, [128, 2, 4, 256], BF16)
    pTp = [carve(scr, 12416 + i * 1024, [128, 512], BF16) for i in range(2)]
    Otok = carve(scr, 14464, [128, 2, 512], BF16)
    for b in range(4):
        kb.dma("sp", qk, pinP[0:1024, b * 256:(b + 1) * 256].rearrange("(w j p) t -> p w j t", w=2, j=4), reads=[pinP], writes=[(scr, "qk")])
        for h in range(8):
            j, hb = h // 2, 64 * (h % 2)
            ps = g.nb()
            for kc2 in range(2):
                kb.MM(ps[:, kc2 * 256:(kc2 + 1) * 256], qk[hb:hb + 64, 1, j, kc2 * 128:(kc2 + 1) * 128], qk[hb:hb + 64, 0, j, :], True, True, reads=[(scr, "qk")], writes=[ps])
            pT = pTp[h % 2]
            kb.I("act", "activation", reads=[ps], writes=[(scr, ("pTp", h % 2))], out=pT, in_=ps[:, :], func=AF.Exp, scale=SCALE)
            po = g.nb()
            for qb in range(2):
                for kc2 in range(2):
                    kb.MM(po[:, qb * 65:(qb + 1) * 65], pT[:, kc2 * 256 + qb * 128:kc2 * 256 + (qb + 1) * 128], vtokP[:, b * 2 + kc2, h, :], kc2 == 0, kc2 == 1,
                          reads=[(scr, ("pTp", h % 2)), (scr, "vtokP")], writes=[po])
            st = g.nsmall()
            kb.I("dve", "reciprocal", reads=[po], writes=[st], out=st[:, 0:2], in_=po[:, 0:130].rearrange("p (q c) -> p q c", q=2)[:, :, 64])
            for qb in range(2):
                kb.I("dve", "tensor_scalar", reads=[po, st], writes=[(scr, "Otok")], out=Otok[:, qb, h * 64:(h + 1) * 64], in0=po[:, qb * 65:qb * 65 + 64],
                     scalar1=st[:, qb:qb + 1], scalar2=None, op0=ALU.mult)
        for qb in range(2):
            pt = g.nbt()
            for blk in range(4):
                kb.TR(pt[:, blk * 128:(blk + 1) * 128], Otok[:, qb, blk * 128:(blk + 1) * 128], g.ident_b[:, :], reads=[(scr, "Otok"), g.ident_b], writes=[pt])
            evac(g, qb, g.hT[:, 0:4, b * 256 + qb * 128:b * 256 + (qb + 1) * 128], pt[:, 0:512].rearrange("p (k t) -> p k t", k=4), [pt], [(g.hT, b // 2)])

    allS = [(scr, None)]
    QT = carve(scr, 0, [128, 4096], BF16)
    KT = carve(scr, 8192, [128, 4608], BF16)
    VtokS = carve(scr, 17408, [128, 2, 36, 65], BF16)
    pTL = [carve(scr, 26768 + i * 1280, [128, 640], BF16) for i in range(2)]
    pTC = [carve(scr, 29328 + i * 1024, [128, 512], BF16) for i in range(2)]
    tmpL = [carve(scr, 31376 + i * 2560, [128, 640], F32) for i in range(2)]
    OtS = carve(scr, 36496, [128, 32, 128], BF16)
    VTt = carve(scr, 36496, [128, 4096], BF16)
    tabv = carve(g.actb_flat, 0, [128, 5, 5, 128], F32)
    for rr in range(4):
        for (dst, kind) in ((QT, 0), (KT, 1), (VTt, 2)):
            kb.dma("sp" if rr % 2 == 0 else "act", dst[:, rr * 1024:(rr + 1) * 1024], XS.get(rr, kind), reads=[XS.mine], writes=allS)
    kst = g.nstage()
    kstv = kst[:, :].rearrange("p (c d) -> p c d", c=4)
    kb.dma("pool", kstv, I["cna_k"][e].rearrange("(c p) d -> p c d", p=128), reads=[I["cna_k"]], writes=[kst])
    pt = g.nbt()
    for c in range(4):
        kb.TR(pt[:, c * 128:(c + 1) * 128], kstv[:, c, :], g.ident_b[:, :], reads=[kst, g.ident_b], writes=[pt])
    evac(g, 0, KT[:, 4096:4608], pt[:, 0:512], [pt], allS)
    vst = g.nstage()
    vstv = vst[:, :].rearrange("p (c d) -> p c d", c=4)
    kb.dma("pool", vstv, I["cna_v"][e].rearrange("(c p) d -> p c d", p=128), reads=[I["cna_v"]], writes=[vst])
    kb.I("pool", "tensor_copy", reads=[vst], writes=allS, out=VtokS[:, :, 32:36, 0:64], in_=vstv.rearrange("p c (h d) -> p h c d", h=2))
    kb.I("pool", "memset", writes=allS, ap=VtokS[:, :, :, 64:65], constant=1.0)
    for c8 in range(4):
        pt = g.nbt()
        for jj in range(8):
            c = c8 * 8 + jj
            kb.TR(pt[:, jj * 128:(jj + 1) * 128], VTt[:, c * 128:(c + 1) * 128], g.ident_b[:, :], reads=allS + [g.ident_b], writes=[pt])
        evac(g, c8, VtokS[:, :, c8 * 8:(c8 + 1) * 8, 0:64], pt[:, :].rearrange("p (c h d) -> p h c d", c=8, h=2), [pt], allS)
    for hh in range(2):
        hb = 64 * hh
        kb.dma("sp", tabv, I["na_tab"][e, hh], reads=[I["na_tab"]], writes=[g.actb_flat])
        for ip in range(32):
            i = 2 * ip
            if i < 4:
                var, r0c, nch = 1 + i // 2, 0, 4
            elif i >= 60:
                var, r0c, nch = 3 + (i - 60) // 2, 56, 4
            else:
                var, r0c, nch = 0, i - 4, 5
            q_ap = QT[hb:hb + 64, i * 64:i * 64 + 128]
            psL0 = g.nb()
            for m in range(4):
                kb.MM(psL0[:, m * 128:(m + 1) * 128], KT[hb:hb + 64, (r0c + 2 * m) * 64:(r0c + 2 * m) * 64 + 128], q_ap, True, True, reads=allS, writes=[psL0])
            tl = tmpL[ip % 2]
            kb.I("dve", "scalar_tensor_tensor", reads=[psL0, g.actb_flat], writes=[(scr, ("tmpL", ip % 2))], out=tl[:, 0:512], in0=psL0[:, :], scalar=SCALE,
                 in1=tabv[:, var, 0:4, :].rearrange("p m q -> p (m q)"), op0=ALU.mult, op1=ALU.add)
            if nch == 5:
                psL1 = g.nb()
                kb.MM(psL1[:, 0:128], KT[hb:hb + 64, (r0c + 8) * 64:(r0c + 8) * 64 + 128], q_ap, True, True, reads=allS, writes=[psL1])
                kb.I("dve", "scalar_tensor_tensor", reads=[psL1, g.actb_flat], writes=[(scr, ("tmpL", ip % 2))], out=tl[:, 512:640], in0=psL1[:, 0:128], scalar=SCALE,
                     in1=tabv[:, var, 4, :], op0=ALU.mult, op1=ALU.add)
            psC = g.nb()
            for c in range(4):
                kb.MM(psC[:, c * 128:(c + 1) * 128], KT[hb:hb + 64, 4096 + c * 128:4096 + (c + 1) * 128], q_ap, True, True, reads=allS, writes=[psC])
            pl, pc = pTL[ip % 2], pTC[ip % 2]
            kb.I("act", "activation", reads=[(scr, ("tmpL", ip % 2))], writes=[(scr, ("pTL", ip % 2))], out=pl[:, 0:nch * 128], in_=tl[:, 0:nch * 128], func=AF.Exp)
            kb.I("act", "activation", reads=[psC], writes=[(scr, ("pTC", ip % 2))], out=pc, in_=psC[:, :], func=AF.Exp, scale=SCALE)
            po = g.nb()
            for m in range(nch):
                kb.MM(po[:, 0:65], pl[:, m * 128:(m + 1) * 128], VtokS[:, hh, r0c // 2 + m, :], m == 0, False, reads=[(scr, ("pTL", ip % 2)), (scr, "static")], writes=[po])
            for c in range(4):
                kb.MM(po[:, 0:65], pc[:, c * 128:(c + 1) * 128], VtokS[:, hh, 32 + c, :], False, c == 3, reads=[(scr, ("pTC", ip % 2)), (scr, "static")], writes=[po])
            st = g.nsmall()
            kb.I("dve", "reciprocal", reads=[po], writes=[st], out=st[:, 0:1], in_=po[:, 64:65])
            kb.I("dve", "tensor_scalar", reads=[po, st], writes=[(scr, ("OtS", ip))], out=OtS[:, ip, hb:hb + 64], in0=po[:, 0:64], scalar1=st[:, 0:1], scalar2=None, op0=ALU.mult)
    oTs = QT
    for c8 in range(4):
        pt = g.nbt()
        for jj in range(8):
            c = c8 * 8 + jj
            kb.TR(pt[:, jj * 128:(jj + 1) * 128], OtS[:, c, :], g.ident_b[:, :], reads=[(scr, None), g.ident_b], writes=[pt])
        evac(g, c8, oTs[:, c8 * 1024:(c8 + 1) * 1024], pt[:, :], [pt], [(scr, None)])
    kb.dma("sp", XO.gin[0], oTs[:, :], reads=[(scr, None)], writes=[(XO.gin, "na")])

    _rwkv(g, l, pinP, XS, lout, XO)

    XO.run()
    kb.dma("sp", g.hT[:, :, 1024:2048], XO.mine.t.ap().rearrange("(k p) t -> p k t", p=128), reads=[XO.mine], writes=[(g.hT, 2), (g.hT, 3)])
    out_proj(g, Wout, e, 16)


NS = 512
NBS = 4
NEG_E05 = -math.exp(-0.5)


def _rw_setup(g):
    if hasattr(g, "rw"):
        return g.rw
    kb, I = g.kb, g.ins
    R = Ctx()
    g.rw = R
    for nm, shape in (("rw_mu", [128, 2, 17, 2]), ("rw_w0", [128, 2, 2, 5]), ("rw_a0", [128, 2, 2, 5]), ("rw_bonus", [128, 2, 2, 5]),
                      ("rw_kk", [128, 2, 5]), ("rw_ka", [128, 2, 5]), ("rw_lnw", [128, 2, 5]), ("rw_lnb", [128, 2, 5])):
        t = kb.sb("p_" + nm, shape, F32)
        kb.dma("sp", t.t.ap(), I[nm].t.ap(), reads=[I[nm]], writes=[t])
        setattr(R, nm, t)
    R.c0 = kb.sb("rw_c0", [128, 2, 17], F32)
    kb.I("dve", "tensor_tensor", reads=[R.rw_mu], writes=[R.c0], out=R.c0[:, :, :], in0=R.rw_mu[:, :, :, 0], in1=R.rw_mu[:, :, :, 1], op=ALU.add)
    kb.I("dve", "tensor_scalar", reads=[R.c0], writes=[R.c0], out=R.c0[:, :, :], in0=R.c0[:, :, :], scalar1=-1.0, scalar2=1.0, op0=ALU.mult, op1=ALU.add)
    R.omk = kb.sb("rw_omk", [128, 2, 5], F32)
    kb.I("dve", "tensor_scalar", reads=[R.rw_ka], writes=[R.omk], out=R.omk[:, :, :], in0=R.rw_ka[:, :, :], scalar1=-1.0, scalar2=1.0, op0=ALU.mult, op1=ALU.add)
    R.wa2 = kb.sb("rw_wa2", [128, 2, 640], BF16)
    R.g2 = kb.sb("rw_g2s", [128, 640], BF16)
    R.m01 = kb.sb("rw_m01", [128, NS], F32)
    kb.I("pool", "memset", writes=[R.m01], ap=R.m01[:, :], constant=1.0)
    kb.I("pool", "memset", writes=[R.m01], ap=R.m01[:, :].rearrange("p (b t) -> p b t", b=NBS)[:, :, 0:1], constant=0.0)
    R.cm3 = kb.sb("rw_cm3", [128, 2, 384], BF16)
    R.cm2 = kb.sb("rw_cm2", [128, 2, 256], BF16)
    MK = I["masks"]
    for d in range(2):
        mS, mI, mN = (0, 1, 2) if d == 0 else (2, 3, 0)
        for k, m in enumerate((mS, mN, mI)):
            kb.dma("pool", R.cm3[:, d, k * 128:(k + 1) * 128], MK[:, m, :], reads=[MK], writes=[R.cm3])
        for k, m in enumerate((mS, mI)):
            kb.dma("pool", R.cm2[:, d, k * 128:(k + 1) * 128], MK[:, m, :], reads=[MK], writes=[R.cm2])
    R.lvl = kb.sb("rw_lvl", [128, 7, 128], BF16)
    kb.dma("pool", R.lvl[:, :, :], I["lvlmask"][:, :, :], reads=[I["lvlmask"]], writes=[R.lvl])
    R.S32 = kb.sb("rw_S32", [128, 2, 128], F32)
    R.S16 = kb.sb("rw_S16", [128, 2, 128], BF16)
    R.gam = kb.sb("rw_gam", [128, NBS], F32)
    return R


def _rw_layout(g):
    scr, ab = g.scr, g.actb_flat
    L = Ctx()
    o = 0

    def take(buf, nbytes, shape, dt):
        nonlocal o
        a = carve(buf, o, shape, dt)
        o += (nbytes + 63) // 64 * 64
        return a
    L.U5 = take(scr, 5 * 544 * 2, [128, 5, 544], BF16)
    L.hst = take(scr, 5 * 2 * 32, [128, 5, 2, 16], BF16)
    L.sr = take(scr, 2048, [128, NS], F32)
    L.sk = take(scr, 2048, [128, NS], F32)
    L.sv = take(scr, 2048, [128, NS], F32)
    L.kk = take(scr, 2048, [128, NS], F32)
    L.swa = take(scr, 1024, [128, NS], BF16)
    L.sg = take(scr, 1024, [128, NS], BF16)
    L.svb = take(scr, 1024, [128, NS], BF16)
    L.prod = take(scr, 6 * 1024, [128, 6, NS], BF16)
    L.tok = take(scr, 4 * 1024, [128, 4, NBS, 128], BF16)
    L.Vpad = take(scr, 2048, [128, NBS, 2, 128], BF16)
    L.G3 = [take(scr, 768, [128, 384], BF16) for _ in range(2)]
    L.G2 = [take(scr, 512, [128, 256], BF16) for _ in range(2)]
    L.TA = [[take(scr, 512, [128, 256], BF16) for _ in range(2)] for _ in range(2)]
    L.XB = [take(scr, 512, [128, 256], BF16) for _ in range(2)]
    L.NLl = [[take(scr, 512, [128, 256], BF16) for _ in range(2)] for _ in range(2)]
    L.Y1b = [take(scr, 128, [128, 64], BF16) for _ in range(2)]
    L.U0 = take(scr, 512, [128, 128], F32)
    L.Ub = take(scr, 256, [128, 128], BF16)
    L.Upad = take(scr, 768, [128, 384], BF16)
    L.WT = take(scr, 256, [128, 128], BF16)
    L.Yacc = take(scr, 2048, [128, NS], F32)
    L.Yb = take(scr, 1024, [128, NS], BF16)
    L.ob = take(scr, 1024, [128, NS], BF16)
    assert o <= 45056, o
    o = 0
    L.LW = take(ab, 2048, [128, NS], F32)
    L.L = take(ab, 2048, [128, NS], F32)
    L.A = take(ab, 2048, [128, NS], F32)
    L.KT = take(ab, 2048, [128, NS], F32)
    L.E = take(ab, 2048, [128, NS], F32)
    L.E2 = take(ab, 2048, [128, NS], F32)
    L.T1 = take(ab, 2048, [128, NS], F32)
    assert o <= 16384, o
    return L


def _rw_shift(g, R, L, e, jp, sample):
    kb = g.kb
    SEG = [(g.scr, "seg")]
    AB = [(g.actb_flat, "seg")]
    mub = [jp if jp < 4 else 14, 4 + jp if jp < 4 else 15, 8 + jp if jp < 4 else 16, 12, 13]
    dsts = [L.sr, L.sk, L.sv, L.T1, L.E]
    for a in range(5):
        dst, mb = dsts[a], mub[a]
        wr = SEG if a < 3 else AB
        if sample:
            cur, prv, nxt = L.U5[:, a, 16:528], L.U5[:, a, 15:527], L.U5[:, a, 17:529]
            d0, d1, d2 = dst[:, :], dst[:, :], dst[:, :]
        else:
            uv = L.U5[:, a, 16:528].rearrange("p (s t) -> p s t", s=2)
            dv = dst[:, :].rearrange("p (s t) -> p s t", s=2)
            cur, prv, nxt = L.U5[:, a, 16:528], uv[:, :, 0:255], uv[:, :, 1:256]
            d0, d1, d2 = dst[:, :], dv[:, :, 1:256], dv[:, :, 0:255]
        kb.I("dve", "tensor_scalar", reads=SEG + [R.c0], writes=wr, out=d0, in0=cur, scalar1=R.c0[:, e, mb:mb + 1], scalar2=None, op0=ALU.mult)
        kb.I("dve", "scalar_tensor_tensor", reads=SEG + wr + [R.rw_mu], writes=wr, out=d1, in0=prv, scalar=R.rw_mu[:, e, mb, 0:1], in1=d1, op0=ALU.mult, op1=ALU.add)
        kb.I("dve", "scalar_tensor_tensor", reads=SEG + wr + [R.rw_mu], writes=wr, out=d2, in0=nxt, scalar=R.rw_mu[:, e, mb, 1:2], in1=d2, op0=ALU.mult, op1=ALU.add)
    kb.I("act", "activation", reads=AB, writes=SEG, out=L.swa[0:64, :], in_=L.T1[0:64, :], func=AF.Tanh)
    kb.I("act", "copy", reads=AB, writes=SEG, out=L.swa[64:128, :], in_=L.T1[64:128, :])
    kb.I("act", "activation", reads=AB, writes=SEG, out=L.sg[:, :], in_=L.E[:, :], func=AF.Sigmoid)
    kb.I("act", "copy", reads=SEG, writes=SEG, out=L.svb[:, :], in_=L.sv[:, :])
    kb.I("dve", "tensor_scalar", reads=SEG + [R.rw_kk], writes=SEG, out=L.kk[:, :], in0=L.sk[:, :], scalar1=R.rw_kk[:, e, jp:jp + 1], scalar2=None, op0=ALU.mult)
    sq = g.nstage()
    kb.I("act", "activation", reads=SEG, writes=[sq], out=sq[:, :], in_=L.kk[:, :], func=AF.Square)
    pn = g.nb()
    kb.MM(pn[:, :], g.blk_b[:, :], sq[:, :], True, True, reads=[sq, g.blk_b], writes=[pn])
    kb.I("dve", "tensor_scalar", reads=[pn], writes=AB, out=L.E[:, :], in0=pn[:, :], scalar1=1e-12, scalar2=None, op0=ALU.max)
    kb.I("act", "activation", reads=AB, writes=AB, out=L.E[:, :], in_=L.E[:, :], func=AF.Sqrt)
    kb.I("dve", "reciprocal", reads=AB, writes=AB, out=L.E[:, :], in_=L.E[:, :])
    kb.I("dve", "tensor_tensor", reads=SEG + AB, writes=SEG, out=L.kk[:, :], in0=L.kk[:, :], in1=L.E[:, :], op=ALU.mult)


def _rw_lora_a_kt(g, R, L, e, jp, d, want_w):
    kb = g.kb
    SEG = [(g.scr, "seg")]
    AB = [(g.actb_flat, "seg")]
    c0 = jp * 128
    pa = g.nb()
    kb.MM(pa[:, :], R.wa2[64:128, d, c0:c0 + 128], L.swa[64:128, :], True, True, reads=SEG + [R.wa2], writes=[pa])
    kb.I("act", "activation", reads=[pa, R.rw_a0], writes=AB, out=L.A[:, :], in_=pa[:, :], func=AF.Sigmoid, bias=R.rw_a0[:, e, d, jp:jp + 1])
    kb.I("dve", "tensor_scalar", reads=AB + [R.rw_ka, R.omk], writes=AB, out=L.KT[:, :], in0=L.A[:, :], scalar1=R.rw_ka[:, e, jp:jp + 1], scalar2=R.omk[:, e, jp:jp + 1],
         op0=ALU.mult, op1=ALU.add)
    kb.I("dve", "tensor_tensor", reads=AB + SEG, writes=AB, out=L.KT[:, :], in0=L.KT[:, :], in1=L.sk[:, :], op=ALU.mult)
    if want_w:
        pw = g.nb()
        kb.MM(pw[:, :], R.wa2[0:64, d, c0:c0 + 128], L.swa[0:64, :], True, True, reads=SEG + [R.wa2], writes=[pw])
        kb.I("act", "activation", reads=[pw, R.rw_w0], writes=AB, out=L.LW[:, :], in_=pw[:, :], func=AF.Sigmoid, bias=R.rw_w0[:, e, d, jp:jp + 1])
        kb.I("dve", "tensor_scalar", reads=AB, writes=AB, out=L.LW[:, :], in0=L.LW[:, :], scalar1=NEG_E05, scalar2=None, op0=ALU.mult)


def _rw_dir_prep(g, R, L, e, jp, d):
    kb = g.kb
    SEG = [(g.scr, "seg")]
    AB = [(g.actb_flat, "seg")]
    _rw_lora_a_kt(g, R, L, e, jp, d, True)
    b3 = lambda ap: ap.rearrange("p (b t) -> p b t", b=NBS)
    if d == 0:
        kb.I("dve", "tensor_tensor_scan", reads=AB + [R.m01], writes=AB, out=L.L[:, :], data0=R.m01[:, :], data1=L.LW[:, :], initial=0.0, op0=ALU.mult, op1=ALU.add)
        ltot = b3(L.L[:, :])[:, :, 127:128]
    else:
        kb.I("dve", "tensor_tensor_scan", reads=AB + [R.m01], writes=AB, out=L.E[:, :], data0=R.m01[:, :], data1=L.LW[:, :], initial=0.0, op0=ALU.mult, op1=ALU.add)
        kb.I("dve", "tensor_tensor", reads=AB, writes=AB, out=L.L[:, :], in0=L.LW[:, :], in1=L.E[:, :], op=ALU.subtract)
        kb.I("dve", "tensor_tensor", reads=AB, writes=AB, out=b3(L.L[:, :]), in0=b3(L.L[:, :]), in1=b3(L.E[:, :])[:, :, 127:128].broadcast_to([128, NBS, 128]), op=ALU.add)
        ltot = b3(L.L[:, :])[:, :, 0:1]
    kb.I("act", "activation", reads=AB, writes=[R.gam], out=R.gam[:, :], in_=ltot.rearrange("p b o -> p (b o)"), func=AF.Exp)
    al, be, ka, rt, bh, kh = [L.prod[:, i, :] for i in range(6)]
    kb.I("dve", "tensor_tensor", reads=AB + SEG, writes=AB, out=L.A[:, :], in0=L.A[:, :], in1=L.kk[:, :], op=ALU.mult)
    kb.I("dve", "tensor_tensor", reads=AB, writes=AB, out=L.E2[:, :], in0=L.L[:, :], in1=L.LW[:, :], op=ALU.subtract)
    kb.I("act", "activation", reads=AB, writes=AB, out=L.E2[:, :], in_=L.E2[:, :], func=AF.Exp)
    kb.I("dve", "tensor_tensor", reads=AB + SEG, writes=SEG, out=al, in0=L.kk[:, :], in1=L.E2[:, :], op=ALU.mult)
    kb.I("act", "activation", reads=AB, writes=AB, out=L.E[:, :], in_=L.L[:, :], func=AF.Exp, scale=-1.0)
    kb.I("dve", "scalar_tensor_tensor", reads=AB, writes=SEG, out=be, in0=L.A[:, :], scalar=-1.0, in1=L.E[:, :], op0=ALU.mult, op1=ALU.mult)
    kb.I("dve", "tensor_tensor", reads=AB, writes=SEG, out=ka, in0=L.KT[:, :], in1=L.E[:, :], op=ALU.mult)
    kb.I("act", "activation", reads=AB, writes=AB, out=L.E[:, :], in_=L.L[:, :], func=AF.Exp)
    kb.I("dve", "tensor_tensor", reads=AB + SEG, writes=SEG, out=rt, in0=L.sr[:, :], in1=L.E[:, :], op=ALU.mult)
    kb.I("dve", "scalar_tensor_tensor", reads=AB, writes=AB, out=b3(L.E2[:, :]), in0=b3(L.L[:, :]), scalar=-1.0, in1=ltot.broadcast_to([128, NBS, 128]), op0=ALU.mult, op1=ALU.add)
    kb.I("act", "activation", reads=AB, writes=AB, out=L.E2[:, :], in_=L.E2[:, :], func=AF.Exp)
    kb.I("dve", "scalar_tensor_tensor", reads=AB, writes=SEG, out=bh, in0=L.A[:, :], scalar=-1.0, in1=L.E2[:, :], op0=ALU.mult, op1=ALU.mult)
    kb.I("dve", "tensor_tensor", reads=AB, writes=SEG, out=kh, in0=L.KT[:, :], in1=L.E2[:, :], op=ALU.mult)
    for bi in range(NBS):
        pt = g.nbt()
        for k, src in enumerate((al, bh, kh, L.svb[:, :])):
            kb.TR(pt[:, k * 128:(k + 1) * 128], src[:, bi * 128:(bi + 1) * 128], g.ident_b[:, :], reads=SEG + [g.ident_b], writes=[pt])
        evac(g, bi, L.tok[:, :, bi, :], pt[:, 0:512].rearrange("p (k c) -> p k c", k=4), [pt], SEG)
        for hh in range(2):
            kb.I("pool", "tensor_copy", reads=SEG, writes=SEG, out=L.Vpad[:, bi, hh, hh * 64:(hh + 1) * 64], in_=L.tok[:, 3, bi, hh * 64:(hh + 1) * 64])


def _rw_sets(g, L):
    s1 = lambda kc, a, b: g.hT[:, kc, 1024 + a:1024 + b]
    return [
        dict(G3=L.G3, G2=L.G2, WT=L.WT, U0=L.U0, key=[(g.scr, "pset0")]),
        dict(G3=[s1(0, 0, 384), s1(1, 0, 384)], G2=[s1(2, 0, 256), s1(2, 256, 512)], U0=s1(3, 0, 256).bitcast(F32), WT=s1(3, 256, 384), key=[(g.hT, 2)]),
    ]


def _rw_prep_block(g, R, L, d, bi, S):
    kb = g.kb
    SEG = [(g.scr, "seg")]
    KS = S["key"]
    al, be, ka, rt, bh, kh = [L.prod[:, i, :] for i in range(6)]
    ts = slice(bi * 128, (bi + 1) * 128)
    for hh in range(2):
        hp = slice(64 * hh, 64 * hh + 64)
        pg = g.nb()
        kb.MM(pg[:, 0:128], be[hp, ts], al[hp, ts], True, True, reads=SEG, writes=[pg])
        kb.MM(pg[:, 128:256], al[hp, ts], be[hp, ts], True, True, reads=SEG, writes=[pg])
        kb.MM(pg[:, 256:384], be[hp, ts], rt[hp, ts], True, True, reads=SEG, writes=[pg])
        kb.I("dve", "tensor_tensor", reads=[pg, R.cm3], writes=KS, out=S["G3"][hh], in0=pg[:, 0:384], in1=R.cm3[:, d, :], op=ALU.mult)
        pg2 = g.nb()
        kb.MM(pg2[:, 0:128], ka[hp, ts], al[hp, ts], True, True, reads=SEG, writes=[pg2])
        kb.MM(pg2[:, 128:256], ka[hp, ts], rt[hp, ts], True, True, reads=SEG, writes=[pg2])
        kb.I("dve", "tensor_tensor", reads=[pg2, R.cm2], writes=KS, out=S["G2"][hh], in0=pg2[:, 0:256], in1=R.cm2[:, d, :], op=ALU.mult)
    cur = [0, 0]
    for lv in range(7):
        for hh in range(2):
            nl = L.NLl[hh][lv % 2]
            HB = [(g.scr, ("pint", hh))]
            kb.I("pool" if hh == 0 else "dve", "tensor_tensor", reads=KS + [R.lvl], writes=HB, out=nl.rearrange("p (a b) -> p a b", a=2),
                 in0=S["G3"][hh][:, 0:256].rearrange("p (a b) -> p a b", a=2), in1=R.lvl[:, lv, :].unsqueeze(1).broadcast_to([128, 2, 128]), op=ALU.mult)
            if lv == 0:
                ta = L.TA[hh][0]
                kb.I("pool", "tensor_tensor", reads=HB + [g.ident_b], writes=HB, out=ta[:, 0:128], in0=nl[:, 128:256], in1=g.ident_b[:, :], op=ALU.add)
                kb.I("pool", "tensor_tensor", reads=HB + [g.ident_b], writes=HB, out=ta[:, 128:256], in0=nl[:, 0:128], in1=g.ident_b[:, :], op=ALU.add)
                continue
            ta, tn = L.TA[hh][cur[hh]], L.TA[hh][1 - cur[hh]]
            px = g.nb()
            kb.MM(px[:, 0:128], nl[:, 0:128], ta[:, 0:128], True, True, reads=HB, writes=[px])
            kb.MM(px[:, 128:256], nl[:, 128:256], ta[:, 128:256], True, True, reads=HB, writes=[px])
            evac(g, hh + lv, L.XB[hh][:, 0:256], px[:, 0:256], [px], HB)
            pq = g.nb()
            kb.MM(pq[:, 0:128], ta[:, 128:256], L.XB[hh][:, 0:128], True, True, reads=HB, writes=[pq])
            kb.MM(pq[:, 128:256], ta[:, 0:128], L.XB[hh][:, 128:256], True, True, reads=HB, writes=[pq])
            kb.I("dve", "tensor_tensor", reads=[pq] + HB, writes=HB, out=tn[:, 0:256], in0=pq[:, 0:256], in1=ta[:, 0:256], op=ALU.add)
            cur[hh] = 1 - cur[hh]
    pu0 = g.nb()
    for hh in range(2):
        hp = slice(64 * hh, 64 * hh + 64)
        HB = [(g.scr, ("pint", hh))]
        TT = L.TA[hh][cur[hh]][:, 128:256]
        pw = g.nb()
        kb.MM(pw[:, 0:128], L.tok[:, 0, bi, :], TT, True, True, reads=SEG + HB, writes=[pw])
        kb.I("act", "copy", reads=[pw], writes=KS, out=S["WT"][hp, :], in_=pw[hp, 0:128])
        kb.MM(pw[:, 128:192], S["G2"][hh][:, 0:128], L.tok[:, 3, bi, 64 * hh:64 * hh + 64], True, True, reads=SEG + KS, writes=[pw])
        kb.I("dve", "tensor_copy", reads=[pw], writes=HB, out=L.Y1b[hh], in_=pw[:, 128:192])
        kb.MM(pu0[:, 64 * hh:64 * hh + 64], TT, L.Y1b[hh], True, True, reads=HB, writes=[pu0])
    kb.I("act", "copy", reads=[pu0], writes=KS, out=S["U0"], in_=pu0[:, 0:128])


def _rw_chain_block(g, R, L, d, bi, S, ycb):
    kb = g.kb
    SEG = [(g.scr, "seg")]
    KS = S["key"]
    KC = [(g.scr, "cint")]
    rt = L.prod[:, 3, :]
    ts = slice(bi * 128, (bi + 1) * 128)
    S32, S16 = R.S32[:, d, :], R.S16[:, d, :]
    ST = [(R.S32, d), (R.S16, d)]
    pu = g.nb()
    kb.MM(pu[:, 0:128], S["WT"][:, :], S16[:, :], True, True, reads=KS + ST, writes=[pu])
    kb.I("dve", "tensor_tensor", reads=[pu] + KS, writes=KC, out=L.Ub, in0=pu[:, 0:128], in1=S["U0"], op=ALU.add)
    kb.I("pool", "tensor_copy", reads=KC, writes=KC, out=L.Upad.rearrange("p (a b) -> p a b", a=2)[:, :, 0:64], in_=L.Ub.rearrange("p (a b) -> p a b", a=2))
    py = g.nb()
    kb.MM(py[:, 0:128], S16[:, :], rt[:, ts], True, False, reads=SEG + ST, writes=[py])
    for hh in range(2):
        kb.MM(py[:, 0:128], L.Upad[:, hh * 128:(hh + 1) * 128], S["G3"][hh][:, 256:384], False, False, reads=KC + KS, writes=[py])
        kb.MM(py[:, 0:128], L.Vpad[:, bi, hh, :], S["G2"][hh][:, 128:256], False, hh == 1, reads=SEG + KS, writes=[py])
    ycb(bi, py)
    psn = g.nb()
    kb.MM(psn[:, 0:128], L.tok[:, 2, bi, :], L.tok[:, 3, bi, :], True, False, reads=SEG, writes=[psn])
    kb.MM(psn[:, 0:128], L.tok[:, 1, bi, :], L.Ub, False, True, reads=SEG + KC, writes=[psn])
    for hh in range(2):
        hp = slice(64 * hh, 64 * hh + 64)
        cs = slice(64 * hh, 64 * hh + 64)
        kb.I("dve", "scalar_tensor_tensor", reads=[psn, R.gam] + ST, writes=[(R.S32, d)], out=S32[hp, cs], in0=S32[hp, cs], scalar=R.gam[hp, bi:bi + 1], in1=psn[hp, cs],
             op0=ALU.mult, op1=ALU.add)
    kb.I("act", "copy", reads=[(R.S32, d)], writes=[(R.S16, d)], out=S16, in_=S32)


def _rw_epilogue(g, R, L, e, jp, emit_out):
    kb = g.kb
    SEG = [(g.scr, "seg")]
    AB = [(g.actb_flat, "seg")]
    kb.I("act", "copy", reads=SEG, writes=SEG, out=L.Yb[:, :], in_=L.Yacc[:, :])
    pm = g.nb()
    kb.MM(pm[:, :], g.blk_b[:, :], L.Yb[:, :], True, True, reads=SEG + [g.blk_b], writes=[pm])
    kb.I("dve", "scalar_tensor_tensor", reads=[pm] + SEG, writes=SEG, out=L.Yacc[:, :], in0=pm[:, :], scalar=-1.0 / 64, in1=L.Yacc[:, :], op0=ALU.mult, op1=ALU.add)
    kb.I("act", "activation", reads=SEG, writes=SEG, out=L.Yb[:, :], in_=L.Yacc[:, :], func=AF.Square)
    pv = g.nb()
    kb.MM(pv[:, :], g.blk_b[:, :], L.Yb[:, :], True, True, reads=SEG + [g.blk_b], writes=[pv])
    kb.I("act", "activation", reads=[pv, g.eps_t], writes=AB, out=L.E[:, :], in_=pv[:, :], func=AF.Sqrt, scale=1.0 / 64, bias=g.eps_t[:, 1:2])
    kb.I("dve", "reciprocal", reads=AB, writes=AB, out=L.E[:, :], in_=L.E[:, :])
    kb.I("dve", "tensor_tensor", reads=SEG + AB, writes=SEG, out=L.Yacc[:, :], in0=L.Yacc[:, :], in1=L.E[:, :], op=ALU.mult)
    kb.I("dve", "tensor_scalar", reads=SEG + [R.rw_lnw, R.rw_lnb], writes=SEG, out=L.Yacc[:, :], in0=L.Yacc[:, :], scalar1=R.rw_lnw[:, e, jp:jp + 1], scalar2=R.rw_lnb[:, e, jp:jp + 1],
         op0=ALU.mult, op1=ALU.add)
    for d in range(2):
        _rw_lora_a_kt(g, R, L, e, jp, d, False)
        if d == 0:
            kb.I("dve", "tensor_scalar", reads=AB + [R.rw_bonus], writes=AB, out=L.E2[:, :], in0=L.KT[:, :], scalar1=R.rw_bonus[:, e, d, jp:jp + 1], scalar2=None, op0=ALU.mult)
        else:
            kb.I("dve", "scalar_tensor_tensor", reads=AB + [R.rw_bonus], writes=AB, out=L.E2[:, :], in0=L.KT[:, :], scalar=R.rw_bonus[:, e, d, jp:jp + 1], in1=L.E2[:, :],
                 op0=ALU.mult, op1=ALU.add)
    kb.I("dve", "tensor_tensor", reads=AB + SEG, writes=SEG, out=L.Yb[:, :], in0=L.E2[:, :], in1=L.sr[:, :], op=ALU.mult)
    pbn = g.nb()
    kb.MM(pbn[:, :], g.blk_b[:, :], L.Yb[:, :], True, True, reads=SEG + [g.blk_b], writes=[pbn])
    kb.I("dve", "tensor_tensor", reads=[pbn] + SEG, writes=AB, out=L.E[:, :], in0=pbn[:, :], in1=L.sv[:, :], op=ALU.mult)
    kb.I("dve", "tensor_tensor", reads=SEG + AB, writes=SEG, out=L.Yacc[:, :], in0=L.Yacc[:, :], in1=L.E[:, :], op=ALU.add)
    pgt = g.nb()
    kb.MM(pgt[:, :], R.g2[:, jp * 128:(jp + 1) * 128], L.sg[:, :], True, True, reads=SEG + [R.g2], writes=[pgt])
    kb.I("dve", "tensor_tensor", reads=[pgt] + SEG, writes=SEG, out=L.ob[:, :], in0=L.Yacc[:, :], in1=pgt[:, :], op=ALU.mult)
    emit_out(L.ob)


def _rwkv_stub(g, XO):
    kb = g.kb
    z = g.nstage()
    kb.I("pool", "memset", writes=[z], ap=z[:, :], constant=0.0)
    for b in range(8):
        kb.dma("sp", XO.gin[1][:, b * 512:(b + 1) * 512], z[:, :], reads=[z], writes=[(XO.gin, ("rw", b))])
    for k in range(4, 8):
        for ti in range(2):
            kb.I("pool", "memset", writes=[(g.hT, ti)], ap=g.hT[:, k, ti * 512:(ti + 1) * 512], constant=0.0)


def _rwkv(g, l, pinP, XS, lout, XO):
    kb, I = g.kb, g.ins
    e = l // 2
    R = _rw_setup(g)
    L = _rw_layout(g)
    SETS = _rw_sets(g, L)
    SEG = [(g.scr, "seg")]
    ORW = g.outd["o_rw"]
    kb.dma("pool", R.wa2[0:64, :, :], I["rw_w2"][e].rearrange("d k n -> k d n"), reads=[I["rw_w2"]], writes=[R.wa2])
    kb.dma("pool", R.wa2[64:128, :, :], I["rw_a2"][e].rearrange("d k n -> k d n"), reads=[I["rw_a2"]], writes=[R.wa2])
    kb.dma("pool", R.g2[:, :], I["rw_g2"][e], reads=[I["rw_g2"]], writes=[R.g2])
    kb.I("pool", "memset", writes=[(g.scr, None)], ap=L.Vpad, constant=0.0)
    kb.I("pool", "memset", writes=[(g.scr, None)], ap=L.Upad, constant=0.0)
    import os
    STOP = os.environ.get("RWSTOP", "")
    if STOP == "setup":
        return _rwkv_stub(g, XO)

    def zero_state(d):
        kb.I("pool", "memset", writes=[(R.S32, d)], ap=R.S32[:, d, :], constant=0.0)
        kb.I("pool", "memset", writes=[(R.S16, d)], ap=R.S16[:, d, :], constant=0.0)

    for jp in range(4):
        for half in range(2):
            t0 = half * NS
            for a, blk in enumerate((12 + jp, 16 + jp, 20 + jp, 24, 25)):
                kb.dma("sp" if a % 2 == 0 else "act", L.U5[:, a, 16:528], pinP[blk * 128:(blk + 1) * 128, t0:t0 + NS], reads=[pinP], writes=SEG)
            _rw_shift(g, R, L, e, jp, False)
            for d in range(2):
                _rw_dir_prep(g, R, L, e, jp, d)

                def ycb(bi, py, d=d):
                    ts = slice(bi * 128, (bi + 1) * 128)
                    if d == 0:
                        kb.I("act", "copy", reads=[py], writes=SEG, out=L.Yacc[:, ts], in_=py[:, 0:128])
                    else:
                        kb.I("dve", "tensor_tensor", reads=[py] + SEG, writes=SEG, out=L.Yacc[:, ts], in0=py[:, 0:128], in1=L.Yacc[:, ts], op=ALU.add)
                steps = [(0, True, None), (1, False, 0), (2, True, None), (3, False, 1)] if d == 0 else [(1, True, None), (0, False, 0), (3, True, None), (2, False, 1)]
                _rw_prep_block(g, R, L, d, steps[0][0], SETS[0])
                for i, (bi, reset, fin) in enumerate(steps):
                    if i + 1 < len(steps):
                        _rw_prep_block(g, R, L, d, steps[i + 1][0], SETS[(i + 1) % 2])
                    if reset:
                        zero_state(d)
                    _rw_chain_block(g, R, L, d, bi, SETS[i % 2], ycb)
                    if fin is not None:
                        pst = g.nb()
                        kb.TR(pst[:, 0:128], R.S32[:, d, :], g.ident_f[:, :], reads=[(R.S32, d), g.ident_f], writes=[pst])
                        so = g.ntmp()
                        kb.I("act", "copy", reads=[pst], writes=[so], out=so[:, 0:128], in_=pst[:, 0:128])
                        seq = half * 2 + fin
                        for hh in range(2):
                            kb.dma("sp", ORW[e, seq, d, jp, :, 64 * hh:64 * hh + 64], so[64 * hh:64 * hh + 64, 64 * hh:64 * hh + 64], reads=[so], writes=[(ORW, (e, seq, d, jp, hh))])

            def emit_out(ob, jp=jp, t0=t0):
                kb.I("pool", "tensor_copy", reads=SEG, writes=[(g.hT, t0 // TT)], out=g.hT[:, 4 + jp, t0:t0 + NS], in_=ob[:, :])
            _rw_epilogue(g, R, L, e, 0 + jp, emit_out)

    if STOP == "prompt":
        z = g.nstage()
        kb.I("pool", "memset", writes=[z], ap=z[:, :], constant=0.0)
        for b in range(8):
            kb.dma("sp", XO.gin[1][:, b * 512:(b + 1) * 512], z[:, :], reads=[z], writes=[(XO.gin, ("rw", b))])
        return
    jp = 4
    yfw = kb.dram(f"yfw{l}", [8, 128, NS], F32)

    def load_seg(sg):
        T0 = sg * NS
        rr, cc = T0 // 1024, T0 % 1024
        for a in range(5):
            def src(rr_, lo, hi, a=a):
                if a < 3:
                    return XS.get(rr_, 3 + a)[:, lo:hi], XS.mine
                return lout[rr_, (a - 3) * 128:(a - 2) * 128, lo:hi], lout
            eng = "sp" if a % 2 == 0 else "act"
            ap, sb_ = src(rr, cc, cc + NS)
            kb.dma(eng, L.U5[:, a, 16:528], ap, reads=[sb_], writes=SEG)
            if T0 == 0:
                kb.I("pool", "memset", writes=SEG, ap=L.U5[:, a, 15:16], constant=0.0)
            else:
                Tm = T0 - 1
                ap, sb_ = src(Tm // 1024, Tm % 1024, Tm % 1024 + 1)
                kb.dma(eng, L.hst[:, a, 0, 0:1], ap, reads=[sb_], writes=SEG, allow_slow_non_contiguous=True)
                kb.I("pool", "tensor_copy", reads=SEG, writes=SEG, out=L.U5[:, a, 15:16], in_=L.hst[:, a, 0, 0:1])
            if T0 + NS == 4096:
                kb.I("pool", "memset", writes=SEG, ap=L.U5[:, a, 528:529], constant=0.0)
            else:
                Tp = T0 + NS
                ap, sb_ = src(Tp // 1024, Tp % 1024, Tp % 1024 + 1)
                kb.dma(eng, L.hst[:, a, 1, 0:1], ap, reads=[sb_], writes=SEG, allow_slow_non_contiguous=True)
                kb.I("pool", "tensor_copy", reads=SEG, writes=SEG, out=L.U5[:, a, 528:529], in_=L.hst[:, a, 1, 0:1])

    for d in range(2):
        zero_state(d)
        s0 = g.ntmp()
        kb.dma("sp", s0[0:64, 0:128], I["st_rw"][e, d], reads=[I["st_rw"]], writes=[s0])
        pst = g.nb()
        kb.TR(pst[:, 0:64], s0[0:64, 0:128], g.ident_f[0:64, 0:64], reads=[s0, g.ident_f], writes=[pst])
        for hh in range(2):
            hp = slice(64 * hh, 64 * hh + 64)
            kb.I("dve", "tensor_copy", reads=[pst], writes=[(R.S32, d)], out=R.S32[hp, d, 64 * hh:64 * hh + 64], in_=pst[hp, 0:64])
        kb.I("act", "copy", reads=[(R.S32, d)], writes=[(R.S16, d)], out=R.S16[:, d, :], in_=R.S32[:, d, :])
    for d in range(2):
        segs = range(8) if d == 0 else range(7, -1, -1)
        for sg in segs:
            load_seg(sg)
            _rw_shift(g, R, L, e, jp, True)
            _rw_dir_prep(g, R, L, e, jp, d)
            if d == 1:
                kb.dma("sp", L.Yacc[:, :], yfw[sg], reads=[(yfw, sg)], writes=SEG)

            def ycb(bi, py, d=d):
                ts = slice(bi * 128, (bi + 1) * 128)
                if d == 0:
                    kb.I("act", "copy", reads=[py], writes=SEG, out=L.Yacc[:, ts], in_=py[:, 0:128])
                else:
                    kb.I("dve", "tensor_tensor", reads=[py] + SEG, writes=SEG, out=L.Yacc[:, ts], in0=py[:, 0:128], in1=L.Yacc[:, ts], op=ALU.add)
            blks = list(range(NBS)) if d == 0 else list(range(NBS - 1, -1, -1))
            _rw_prep_block(g, R, L, d, blks[0], SETS[0])
            for i, bi in enumerate(blks):
                if i + 1 < NBS:
                    _rw_prep_block(g, R, L, d, blks[i + 1], SETS[(i + 1) % 2])
                _rw_chain_block(g, R, L, d, bi, SETS[i % 2], ycb)
            if d == 0:
                kb.dma("sp", yfw[sg], L.Yacc[:, :], reads=SEG, writes=[(yfw, sg)])
            else:
                def emit_out(ob, sg=sg):
                    kb.dma("sp", XO.gin[1][:, sg * NS:(sg + 1) * NS], ob[:, :], reads=SEG, writes=[(XO.gin, ("rw", sg))])
                _rw_epilogue(g, R, L, e, jp, emit_out)


def _fm(v, nblk):
    v = np.asarray(v)
    lead = v.shape[:-1]
    a = v.reshape(lead + (nblk, 128))
    a = np.moveaxis(a, -1, 0)
    return np.ascontiguousarray(a)


def _na_tables(rpb):
    E, H = rpb.shape[0], rpb.shape[1]
    tab = np.full((E, H, 128, 5, 5, 128), MASKV, np.float32)
    var_i = {1: 0, 2: 2, 3: 60, 4: 62}
    half = np.arange(128) // 64
    kcol = np.arange(128) % 64
    qo = np.arange(128) // 64
    j = np.arange(128) % 64
    c0 = np.clip(j - 8, 0, 48)
    colok = (kcol[:, None] >= c0[None, :]) & (kcol[:, None] < c0[None, :] + 16)
    dc = kcol[:, None] - j[None, :] + 15
    dcc = np.clip(dc, 0, 30)
    for v in range(5):
        for m in range(5):
            if v == 0:
                i = 10
                r0c = i - 4
            else:
                i = var_i[v]
                r0c = min(max(i - 4, 0), 56)
                if m == 4:
                    continue
            kr = r0c + 2 * m + half
            iq = i + qo
            r0q = np.clip(iq - 4, 0, 56)
            rowok = (kr[:, None] >= r0q[None, :]) & (kr[:, None] < r0q[None, :] + 8)
            dr = kr[:, None] - iq[None, :] + 7
            drc = np.clip(dr, 0, 14)
            ok = rowok & colok
            vals = rpb[:, :, drc, dcc]
            cur = tab[:, :, :, v, m, :]
            tab[:, :, :, v, m, :] = np.where(ok[None, None], vals, cur)
    return tab


def _rope_tables():
    t = np.arange(4096)
    pos = np.stack([t // 64, t % 64], -1).astype(np.float32)
    inv = (10000.0 ** (-np.arange(16, dtype=np.float32) / 16)).astype(np.float32)
    ang = pos[:, :, None] * inv
    cos, sin = np.cos(ang).astype(np.float32), np.sin(ang).astype(np.float32)
    C = np.zeros((4096, 2, 2, 16), np.float32)
    S = np.zeros((4096, 2, 2, 16), np.float32)
    C[:, :, 0, :] = cos; C[:, :, 1, :] = cos
    S[:, :, 0, :] = -sin; S[:, :, 1, :] = sin
    C = C.reshape(4096, 64); S = S.reshape(4096, 64)
    C = np.concatenate([C, C], 1).T
    S = np.concatenate([S, S], 1).T
    return np.ascontiguousarray(C), np.ascontiguousarray(S)


def _sw_perm():
    idx = np.arange(2048).reshape(-1, 2, 2, 16)
    return idx[:, :, ::-1, :].reshape(-1)


_CACHE = {}


def prep_inputs(inp):
    f32 = lambda a: np.ascontiguousarray(np.asarray(a, np.float32))
    x_prompt = f32(inp["x_prompt"]); x_sample = f32(inp["x_sample"]); c = f32(inp["c"])
    common = {}
    common["w_ada"] = f32(inp["w_ada"])
    common["b_ada"] = _fm(f32(inp["b_ada"]), 48)
    common["norm_mix"] = _fm(f32(inp["norm_mix"]), 8)
    common["norm_ffn"] = _fm(f32(inp["norm_ffn"]), 8)
    common["norm_final"] = _fm(f32(inp["norm_final"]), 8)
    common["w_in_even"] = f32(inp["w_in_even"])
    common["w_out_even"] = f32(inp["w_out_even"])
    wq = f32(inp["w_qkv_diff"])
    common["w_qkv_diff"] = wq
    common["w_qk_sw"] = np.ascontiguousarray(wq[:, :, :2048][:, :, _sw_perm()])
    common["w_out_diff"] = f32(inp["w_out_diff"])
    common["ffn_w1"] = f32(inp["ffn_w1"]); common["ffn_w3"] = f32(inp["ffn_w3"]); common["ffn_w2"] = f32(inp["ffn_w2"])
    mu = f32(inp["rw_mu"])
    common["rw_mu"] = np.ascontiguousarray(np.transpose(_fm(mu, 14), (0, 1, 3, 2)))
    common["rw_w0"] = _fm(f32(inp["rw_w0"]), 4)
    common["rw_a0"] = _fm(f32(inp["rw_a0"]), 4)
    common["rw_kk"] = _fm(f32(inp["rw_kk"]), 4)
    common["rw_ka"] = _fm(f32(inp["rw_ka"]), 4)
    common["rw_lnw"] = _fm(f32(inp["rw_lnw"]), 4)
    common["rw_lnb"] = _fm(f32(inp["rw_lnb"]), 4)
    common["rw_bonus"] = _fm(f32(inp["rw_bonus"]).reshape(2, 2, 512), 4)
    common["rw_w2"] = f32(inp["rw_w2"]); common["rw_a2"] = f32(inp["rw_a2"]); common["rw_g2"] = f32(inp["rw_g2"])
    rep = lambda a: np.ascontiguousarray(np.broadcast_to(a[None], (128,) + a.shape))
    common["lamq"] = rep(f32(inp["diff_lam_q"]).reshape(2, 128))
    common["lamk"] = rep(f32(inp["diff_lam_k"]).reshape(2, 128))
    common["subln"] = rep(f32(inp["diff_subln"]))
    s = np.arange(128)[:, None]; t = np.arange(128)[None, :]
    common["masks"] = np.ascontiguousarray(np.stack([(s < t), (s <= t), (s > t), (s >= t)], 1).astype(np.float32))
    common["ident"] = np.eye(128, dtype=np.float32)
    pi = np.arange(128)[:, None]; fi = np.arange(128)[None, :]
    common["lvlmask"] = np.ascontiguousarray(np.stack([((pi >> (lv + 1)) == (fi >> (lv + 1))) & ((pi >> lv) != (fi >> lv)) for lv in range(7)], 1).astype(np.float32))
    tab = _na_tables(f32(inp["na_rpb"]))
    ropeC, ropeS = _rope_tables()
    cna_k = f32(inp["cache_na_k"]); cna_v = f32(inp["cache_na_v"]); st = f32(inp["state_rwkv"])
    cdk = f32(inp["cache_diff_k"]); cdv = f32(inp["cache_diff_v"])
    c_ctx = f32(inp["c_ctx"])
    maps = []
    for core in range(8):
        gi, r = core // 4, core % 4
        m = dict(common)
        xin = np.concatenate([x_prompt[4 * core:4 * core + 4].reshape(1024, 1024), x_sample[gi, 1024 * r:1024 * (r + 1)]], 0)
        m["xin"] = np.ascontiguousarray(xin)
        m["cvec"] = np.ascontiguousarray(np.transpose(_fm(np.stack([c_ctx, c[gi]], 0), 8), (0, 2, 1)))
        m["rope_c"] = np.ascontiguousarray(ropeC[:, 1024 * r:1024 * (r + 1)])
        m["rope_s"] = np.ascontiguousarray(ropeS[:, 1024 * r:1024 * (r + 1)])
        m["na_tab"] = np.ascontiguousarray(tab[:, 2 * r:2 * r + 2])
        m["cna_k"] = np.ascontiguousarray(cna_k[gi][:, :, 2 * r:2 * r + 2, :].reshape(2, 512, 128))
        m["cna_v"] = np.ascontiguousarray(cna_v[gi][:, :, 2 * r:2 * r + 2, :].reshape(2, 512, 128))
        s2 = st[gi][:, :, 2 * r:2 * r + 2]
        m["st_rw"] = np.ascontiguousarray(np.transpose(s2, (0, 1, 3, 2, 4)).reshape(2, 2, 64, 128))
        m["cdf_k"] = np.ascontiguousarray(np.transpose(cdk[gi][:, :, 2 * r:2 * r + 2, :], (0, 2, 1, 3)))
        m["cdf_v"] = np.ascontiguousarray(np.transpose(cdv[gi][:, :, 2 * r:2 * r + 2, :], (0, 2, 1, 3)))
        mu = common["rw_mu"]
        m["rw_mu"] = np.ascontiguousarray(np.concatenate([mu, mu[:, :, [r, 4 + r, 8 + r], :]], axis=2))
        for nm in ("rw_w0", "rw_a0", "rw_bonus", "rw_kk", "rw_ka", "rw_lnw", "rw_lnb"):
            a = common[nm]
            m[nm] = np.ascontiguousarray(np.concatenate([a, a[..., r:r + 1]], axis=-1))
        for nm in ("rw_w2", "rw_a2", "rw_g2"):
            a = common[nm]
            m[nm] = np.ascontiguousarray(np.concatenate([a, a[..., 128 * r:128 * (r + 1)]], axis=-1))
        maps.append(m)
    return maps


def assemble(results):
    y = np.stack([r["y"] for r in results], 0)
    y_prompt = y[:, :1024].reshape(32, 256, 1024)
    y_sample = y[:, 1024:].reshape(2, 4096, 1024)
    nk = np.stack([r["o_na_k"] for r in results], 0)
    nv = np.stack([r["o_na_v"] for r in results], 0)
    to_cache = lambda a, hd: np.ascontiguousarray(np.transpose(a.reshape(8, 2, 4, 256, 8, hd), (0, 2, 1, 3, 4, 5)).reshape(32, 2, 256, 8, hd))
    new_na_k = to_cache(nk, 64); new_na_v = to_cache(nv, 64)
    dk = np.stack([r["o_df_k"] for r in results], 0); dv = np.stack([r["o_df_v"] for r in results], 0)
    new_diff_k = to_cache(dk, 128); new_diff_v = to_cache(dv, 128)
    rw = np.stack([r["o_rw"] for r in results], 0)
    rw = rw.reshape(8, 2, 4, 2, 4, 64, 2, 64)
    rw = np.transpose(rw, (0, 2, 1, 3, 4, 6, 5, 7)).reshape(32, 2, 2, 8, 64, 64)
    return (np.ascontiguousarray(y_prompt), np.ascontiguousarray(y_sample), new_na_k, new_na_v,
            np.ascontiguousarray(rw), new_diff_k, new_diff_v)


def kernel(**inputs):
    maps = prep_inputs(inputs)
    if "nc" not in _CACHE:
        _CACHE["nc"] = build_program()[0]
    nc = _CACHE["nc"]
    res = run_bass_kernel_spmd(nc, maps, core_ids=list(range(8)))
    return assemble(res.results)
```

```python
import contextlib
import math
import numpy as np
import concourse.bass as bass
import concourse.mybir as mybir
from concourse.bass_utils import run_bass_kernel_spmd
F32 = mybir.dt.float32
BF16 = mybir.dt.bfloat16
I32 = mybir.dt.int32
AF = mybir.ActivationFunctionType
ALU = mybir.AluOpType
AX = mybir.AxisListType

EPOCH = 30000


class Buf:
    def __init__(self, name, t):
        self.name = name
        self.t = t
        self.st = {}

    def states(self, key):
        if key is None:
            if None not in self.st:
                self.st[None] = [None, {}]
            return list(self.st.values())
        if key not in self.st:
            self.st[key] = [None, {}]
        out = [self.st[key]]
        if None in self.st:
            out.append(self.st[None])
        return out

    def __getitem__(self, idx):
        return self.t[idx]


class KB:
    ENGS = ("pe", "dve", "act", "pool", "sp")

    def __init__(self, nc, stack):
        self.nc = nc
        self.stack = stack
        self.h = {"pe": nc.tensor, "dve": nc.vector, "act": nc.scalar, "pool": nc.gpsimd, "sp": nc.sync}
        self.stream = {e: [] for e in self.ENGS}
        self.sem = {}
        self.cnt = {}
        self.nsem = 0
        self.pe_sems = set()
        for e in self.ENGS:
            self._new_eng_sem(e)
        self.dpool = {}
        self.dnext = {}
        for e in ("sp", "act", "pool"):
            self.dpool[e] = [[self._mksem(f"d_{e}_{i}"), 0] for i in range(6)]
            self.dnext[e] = 0
        self.waited = {e: {} for e in self.ENGS}
        self.ninstr = 0

    def _mksem(self, name):
        self.nsem += 1
        return self.stack.enter_context(self.nc.semaphore(name))

    def _new_eng_sem(self, e):
        k = sum(1 for n in self.sem if n[0] == e) if False else None
        s = self._mksem(f"s_{e}_{self.nsem}")
        if e == "pe":
            self.pe_sems.add(id(s))
        self.sem[e] = s
        self.cnt[e] = 0

    def sb(self, name, shape, dtype=F32):
        t = self.stack.enter_context(self.nc.sbuf_tensor("sb_" + name, list(shape), dtype))
        return Buf(name, t)

    def ps(self, name, shape, dtype=F32):
        t = self.stack.enter_context(self.nc.psum_tensor("ps_" + name, list(shape), dtype))
        return Buf(name, t)

    def dram(self, name, shape, dtype=F32, kind="Internal"):
        if kind == "Internal":
            t = self.nc.dram_tensor(name, list(shape), dtype)
        else:
            t = self.nc.dram_tensor(name, list(shape), dtype, kind=kind)
        return Buf(name, t)

    def _deps(self, eng, reads, writes):
        toks = []
        for (b, k) in reads:
            for st in b.states(k):
                if st[0] is not None:
                    toks.append(st[0])
        for (b, k) in writes:
            for st in b.states(k):
                if st[0] is not None:
                    toks.append(st[0])
                for t in st[1].values():
                    toks.append(t)
        best = {}
        for (s, v) in toks:
            key = id(s)
            if key not in best or best[key][1] < v:
                best[key] = (s, v)
        out = []
        w = self.waited[eng]
        for key, (s, v) in best.items():
            if eng == "pe" and key in self.pe_sems:
                continue
            if w.get(key, 0) >= v:
                continue
            w[key] = v
            out.append((s, v))
        return out

    def _record(self, eng, tok, reads, writes):
        for (b, k) in reads:
            if k is None:
                for st in b.states(None):
                    st[1][eng] = tok
            else:
                b.states(k)[0][1][eng] = tok
        for (b, k) in writes:
            if k is None:
                for st in b.states(None):
                    st[0] = tok
                    st[1] = {}
            else:
                st = b.states(k)[0]
                st[0] = tok
                st[1] = {}

    @staticmethod
    def _norm(lst):
        out = []
        for x in lst:
            if isinstance(x, Buf):
                out.append((x, None))
            else:
                out.append(x)
        return out

    def op(self, eng, fn, reads=(), writes=()):
        reads = self._norm(reads)
        writes = self._norm(writes)
        waits = self._deps(eng, reads, writes)
        if self.cnt[eng] >= EPOCH:
            self._new_eng_sem(eng)
        self.cnt[eng] += 1
        s = self.sem[eng]
        v = self.cnt[eng]
        tok = (s, v)
        self.stream[eng].append((waits, fn, s, 1))
        self._record(eng, tok, reads, writes)
        self.ninstr += 1 + len(waits)
        return tok

    def I(self, eng, name, reads=(), writes=(), **kwargs):
        def fn(e, name=name, kwargs=kwargs):
            return getattr(e, name)(**kwargs)
        return self.op(eng, fn, reads, writes)

    def MM(self, out, lhsT, rhs, start, stop, reads=(), writes=()):
        def fn(e, out=out, lhsT=lhsT, rhs=rhs, start=start, stop=stop):
            return e.matmul(out, lhsT=lhsT, rhs=rhs, start=start, stop=stop)
        return self.op("pe", fn, reads, writes)

    def TR(self, out, in_, ident, reads=(), writes=()):
        def fn(e, out=out, in_=in_, ident=ident):
            return e.transpose(out, in_, ident)
        return self.op("pe", fn, reads, writes)

    def dma(self, eng, out_ap, in_ap, reads=(), writes=(), **kw):
        reads = self._norm(reads)
        writes = self._norm(writes)
        waits = self._deps(eng, reads, writes)
        pool = self.dpool[eng]
        j = self.dnext[eng]
        self.dnext[eng] = (j + 1) % len(pool)
        s, c = pool[j]
        w = self.waited[eng]
        if c > 0 and w.get(id(s), 0) < c:
            waits.append((s, c))
            w[id(s)] = c
        c += 16
        pool[j][1] = c
        tok = (s, c)

        def fn(e, out_ap=out_ap, in_ap=in_ap, kw=kw):
            o = out_ap(e) if callable(out_ap) else out_ap
            i = in_ap(e) if callable(in_ap) else in_ap
            return e.dma_start(out=o, in_=i, **kw)

        self.stream[eng].append((waits, fn, s, 16))
        self._record("dma_" + eng + str(j), tok, reads, writes)
        self.ninstr += 1 + len(waits)
        return tok

    def custom(self, eng, fn, sem_inc, reads=(), writes=()):
        reads = self._norm(reads)
        writes = self._norm(writes)
        waits = self._deps(eng, reads, writes)
        s = self._mksem(f"c_{self.nsem}")
        tok = (s, sem_inc)
        self.stream[eng].append((waits, fn, s, sem_inc))
        self._record("cc" + str(self.nsem), tok, reads, writes)
        return tok

    def wait_all(self, eng, bufs):
        reads = self._norm(bufs)
        waits = self._deps(eng, reads, [])
        self.stream[eng].append((waits, None, None, 0))

    def emit(self):
        nc = self.nc
        streams = self.stream

        def run(e, lst):
            for (waits, fn, s, inc) in lst:
                for (ws, wv) in waits:
                    e.wait_ge(ws, wv)
                if fn is not None:
                    ins = fn(e)
                    ins.then_inc(s, inc)

        with nc.Block() as block:
            @block.sync
            def _(e):
                run(e, streams["sp"])

            @block.tensor
            def _(e):
                run(e, streams["pe"])

            @block.vector
            def _(e):
                run(e, streams["dve"])

            @block.scalar
            def _(e):
                run(e, streams["act"])

            @block.gpsimd
            def _(e):
                run(e, streams["pool"])


D = 1024
KC = 8
NTOK = 2048
TT = 512
NT = 4
DFF = 2816
SCALE = 0.125
MASKV = -30000.0
GROUPS4 = [[0, 1, 2, 3], [4, 5, 6, 7]]


class Ctx:
    pass


def build_program(dbg=None, nlayers=4, mixers=True):
    nc = bass.Bass("TRN2", target_bir_lowering=False)
    st = contextlib.ExitStack()
    with st:
        kb = KB(nc, st)
        g = Ctx()
        g.nc, g.kb = nc, kb
        g.dbg = dbg or []
        g.dbg_out = {}
        _declare_io(g)
        _alloc(g)
        _consts(g)
        _load_x(g)
        for l in range(nlayers):
            _adaln(g, l)
            _norm_mod(g, l, 0)
            if mixers:
                if l % 2 == 0:
                    _even_mixer(g, l)
                else:
                    _odd_mixer(g, l)
            _norm_mod(g, l, 1)
            _ffn(g, l)
        _final(g)
        kb.wait_all("sp", g.outs)
        kb.wait_all("pool", g.outs)
        kb.wait_all("act", g.outs)
        kb.emit()
    return nc, g


def IN(g, name, shape, dt=F32):
    t = g.nc.dram_tensor(name, list(shape), dt, kind="ExternalInput")
    b = Buf(name, t)
    g.ins[name] = b
    return b


def OUT(g, name, shape, dt=F32):
    t = g.nc.dram_tensor(name, list(shape), dt, kind="ExternalOutput")
    b = Buf(name, t)
    g.outs.append(b)
    g.outd[name] = b
    return b


def _declare_io(g):
    g.ins = {}
    g.outs = []
    g.outd = {}
    IN(g, "xin", [NTOK, D])
    IN(g, "cvec", [128, KC, 2])
    IN(g, "w_ada", [4, D, 6 * D])
    IN(g, "b_ada", [128, 4, 48])
    IN(g, "norm_mix", [128, 4, 8])
    IN(g, "norm_ffn", [128, 4, 8])
    IN(g, "norm_final", [128, 8])
    IN(g, "w_in_even", [2, D, 3328])
    IN(g, "w_out_even", [2, D, D])
    IN(g, "w_qkv_diff", [2, D, 3072])
    IN(g, "w_qk_sw", [2, D, 2048])
    IN(g, "w_out_diff", [2, D, D])
    IN(g, "ffn_w1", [4, D, DFF])
    IN(g, "ffn_w3", [4, D, DFF])
    IN(g, "ffn_w2", [4, DFF, D])
    IN(g, "rope_c", [128, 1024])
    IN(g, "rope_s", [128, 1024])
    IN(g, "rw_mu", [128, 2, 17, 2])
    IN(g, "rw_w0", [128, 2, 2, 5])
    IN(g, "rw_a0", [128, 2, 2, 5])
    IN(g, "rw_kk", [128, 2, 5])
    IN(g, "rw_ka", [128, 2, 5])
    IN(g, "rw_lnw", [128, 2, 5])
    IN(g, "rw_lnb", [128, 2, 5])
    IN(g, "rw_bonus", [128, 2, 2, 5])
    IN(g, "rw_w2", [2, 2, 64, 640])
    IN(g, "rw_a2", [2, 2, 64, 640])
    IN(g, "rw_g2", [2, 128, 640])
    IN(g, "na_tab", [2, 2, 128, 5, 5, 128])
    IN(g, "cna_k", [2, 512, 128])
    IN(g, "cna_v", [2, 512, 128])
    IN(g, "st_rw", [2, 2, 64, 128])
    IN(g, "cdf_k", [2, 2, 512, 128])
    IN(g, "cdf_v", [2, 2, 512, 128])
    IN(g, "lamq", [128, 2, 128])
    IN(g, "lamk", [128, 2, 128])
    IN(g, "subln", [128, 2, 128])
    IN(g, "masks", [128, 4, 128])
    IN(g, "ident", [128, 128])
    IN(g, "lvlmask", [128, 7, 128])
    OUT(g, "y", [NTOK, D])
    OUT(g, "o_na_k", [2, 1024, 512])
    OUT(g, "o_na_v", [2, 1024, 512])
    OUT(g, "o_rw", [2, 4, 2, 4, 64, 128])
    OUT(g, "o_df_k", [2, 1024, 1024])
    OUT(g, "o_df_v", [2, 1024, 1024])
    for (name, shape, dt) in g.dbg:
        g.dbg_out[name] = OUT(g, "dbg_" + name, shape, dt)


class Rot:
    def __init__(self, bufs):
        self.bufs = bufs
        self.i = 0

    def __call__(self):
        b = self.bufs[self.i % len(self.bufs)]
        self.i += 1
        return b


def _alloc(g):
    kb = g.kb
    g.xT = kb.sb("xT", [128, KC, NTOK], F32)
    g.hT = kb.sb("hT", [128, KC, NTOK], BF16)
    g.PBL = [kb.ps(f"pb{i}", [128, 512], F32) for i in range(6)]
    g.nb = Rot(g.PBL)
    g.nbt = Rot([kb.ps(f"pt{i}", [128, 1024], BF16) for i in range(2)])
    g.nw = Rot([kb.sb(f"wbuf{i}", [128, KC * 512], BF16) for i in range(3)])
    g.mod = kb.sb("mod", [128, 48, 2], F32)
    g.modA = kb.sb("modA", [128, 2, 8, 2], F32)
    g.actb_flat = kb.sb("actb", [128, 4 * NTOK], BF16)
    g.actb = g.actb_flat
    g.scr = kb.sb("scr", [128, 22528], BF16)
    g.nstage = Rot([kb.sb(f"stage{i}", [128, 512], BF16) for i in range(3)])
    g.ntmp = Rot([kb.sb(f"tmpf{i}", [128, 512], F32) for i in range(3)])
    g.nR = Rot([kb.sb(f"Rb{i}", [128, 512], F32) for i in range(1)])
    g.nsmall = Rot([kb.sb(f"small{i}", [128, 8], F32) for i in range(4)])
    g.sm_neglam = kb.sb("neglam", [128, 1], F32)
    g.sm_gsub = kb.sb("gsub", [128, 128], F32)


def _consts(g):
    kb = g.kb
    I = g.ins
    g.ident_f = kb.sb("ident_f", [128, 128], F32)
    g.ident_b = kb.sb("ident_b", [128, 128], BF16)
    kb.dma("sp", g.ident_f[:, :], I["ident"][:, :], reads=[I["ident"]], writes=[g.ident_f])
    kb.dma("pool", g.ident_b[:, :], I["ident"][:, :], reads=[I["ident"]], writes=[g.ident_b])
    g.ones_b = kb.sb("ones_b", [128, 128], BF16)
    kb.I("pool", "memset", writes=[g.ones_b], ap=g.ones_b[:, :], constant=1.0)
    g.blk_b = kb.sb("blk_b", [128, 128], BF16)
    kb.I("pool", "memset", writes=[g.blk_b], ap=g.blk_b[:, :], constant=0.0)
    kb.I("pool", "memset", writes=[g.blk_b], ap=g.blk_b[0:64, 0:64], constant=1.0)
    kb.I("pool", "memset", writes=[g.blk_b], ap=g.blk_b[64:128, 64:128], constant=1.0)
    g.cv = kb.sb("cv", [128, KC, 2], F32)
    kb.dma("sp", g.cv[:, :, :], I["cvec"][:, :, :], reads=[I["cvec"]], writes=[g.cv])
    g.cvf = kb.sb("cvf", [128, KC, 2], F32)
    kb.I("act", "activation", reads=[g.cv], writes=[g.cvf], out=g.cvf[:, :, :], in_=g.cv[:, :, :], func=AF.Silu)
    g.b_ada = kb.sb("b_ada", [128, 4, 48], F32)
    kb.dma("sp", g.b_ada[:, :, :], I["b_ada"][:, :, :], reads=[I["b_ada"]], writes=[g.b_ada])
    g.nrm = kb.sb("nrm", [128, 2, 4, 8], F32)
    kb.dma("sp", g.nrm[:, 0, :, :], I["norm_mix"][:, :, :], reads=[I["norm_mix"]], writes=[g.nrm])
    kb.dma("sp", g.nrm[:, 1, :, :], I["norm_ffn"][:, :, :], reads=[I["norm_ffn"]], writes=[g.nrm])
    g.nrmf = kb.sb("nrmf", [128, 8], F32)
    kb.dma("sp", g.nrmf[:, :], I["norm_final"][:, :], reads=[I["norm_final"]], writes=[g.nrmf])
    g.eps_t = kb.sb("eps_t", [128, 2], F32)
    kb.I("pool", "memset", writes=[g.eps_t], ap=g.eps_t[:, 0:1], constant=1e-6)
    kb.I("pool", "memset", writes=[g.eps_t], ap=g.eps_t[:, 1:2], constant=64e-5)


def dbg_dump(g, name, ap, buf, key=None):
    if name in g.dbg_out:
        o = g.dbg_out[name]
        g.kb.dma("sp", o.t.ap(), ap, reads=[(buf, key)], writes=[o])


def evac(g, i, out, in_, reads, writes):
    if i % 2 == 0:
        g.kb.I("dve", "tensor_copy", reads=reads, writes=writes, out=out, in_=in_)
    else:
        g.kb.I("act", "copy", reads=reads, writes=writes, out=out, in_=in_)


def _load_x(g):
    kb = g.kb
    X = g.ins["xin"]
    for tb in range(NTOK // 128):
        xt = g.ntmp()
        xt2 = g.ntmp()
        kb.dma("sp", xt[:, :], X[tb * 128:(tb + 1) * 128, 0:512], reads=[X], writes=[xt])
        kb.dma("act", xt2[:, :], X[tb * 128:(tb + 1) * 128, 512:1024], reads=[X], writes=[xt2])
        for half, src in ((0, xt), (1, xt2)):
            pb = g.nb()
            for j in range(4):
                kb.TR(pb[:, j * 128:(j + 1) * 128], src[:, j * 128:(j + 1) * 128], g.ident_f[:, :], reads=[src, g.ident_f], writes=[pb])
            dst = g.xT[:, half * 4:(half + 1) * 4, tb * 128:(tb + 1) * 128]
            evac(g, half, dst, pb[:, :].rearrange("p (j t) -> p j t", j=4), [pb], [(g.xT, tb // 4)])


def wload(g, src_buf, ap, kc, wdt):
    wb = g.nw()
    view = wb[:, 0:kc * wdt].rearrange("p (k n) -> p k n", k=kc)
    g.kb.dma("pool", view, ap, reads=[src_buf], writes=[wb])
    return wb, view


def wcols(W, idx, c0, c1):
    return W[idx].rearrange("(kc p) n -> p kc n", p=128)[:, :, c0:c1]


def _adaln(g, l):
    kb = g.kb
    W = g.ins["w_ada"]
    pb = g.nb()
    for gi, c0 in enumerate(range(0, 6 * D, 256)):
        wb = g.nw()
        wv = wb[:, 0:KC * 512].bitcast(F32).rearrange("p (k n) -> p k n", k=KC)
        kb.dma("sp" if gi % 2 == 0 else "act", wv, wcols(W, l, c0, c0 + 256), reads=[W], writes=[wb])
        for j in range(2):
            blk = c0 // 128 + j
            for kc in range(KC):
                kb.MM(pb[:, blk * 2:(blk + 1) * 2], wv[:, kc, j * 128:(j + 1) * 128], g.cvf[:, kc, :], kc == 0, kc == KC - 1, reads=[wb, g.cvf], writes=[pb])
    kb.I("dve", "tensor_tensor", reads=[pb, g.b_ada], writes=[g.mod], out=g.mod[:, :, :], in0=pb[:, 0:96].rearrange("p (b j) -> p b j", j=2),
         in1=g.b_ada[:, l, :].unsqueeze(2).broadcast_to([128, 48, 2]), op=ALU.add)
    for which in range(2):
        sc0 = 8 if which == 0 else 32
        kb.I("dve", "scalar_tensor_tensor", reads=[g.mod, g.nrm], writes=[g.modA], out=g.modA[:, which, :, :], in0=g.mod[:, sc0:sc0 + 8, :], scalar=1.0,
             in1=g.nrm[:, which, l, :].unsqueeze(2).broadcast_to([128, 8, 2]), op0=ALU.add, op1=ALU.mult)


def _rms_scale(g, ti, R):
    kb = g.kb
    t0 = ti * TT
    pb = g.nb()
    for kc in range(KC):
        sq = g.nstage()
        kb.I("act", "activation", reads=[(g.xT, ti)], writes=[sq], out=sq[:, :], in_=g.xT[:, kc, t0:t0 + TT], func=AF.Square)
        kb.MM(pb[:, :], g.ones_b[:, :], sq[:, :], kc == 0, kc == KC - 1, reads=[sq, g.ones_b], writes=[pb])
    kb.I("act", "activation", reads=[pb, g.eps_t], writes=[R], out=R[:, :], in_=pb[:, :], func=AF.Sqrt, scale=1.0 / D, bias=g.eps_t[:, 0:1])
    kb.I("dve", "reciprocal", reads=[R], writes=[R], out=R[:, :], in_=R[:, :])


def _norm_mod(g, l, which):
    kb = g.kb
    sh0 = 0 if which == 0 else 24
    for ti in range(NT):
        t0 = ti * TT
        grp = 0 if ti < 2 else 1
        R = g.nR()
        _rms_scale(g, ti, R)
        for kc in range(KC):
            tmp = g.ntmp()
            kb.I("dve", "tensor_tensor", reads=[(g.xT, ti), R], writes=[tmp], out=tmp[:, :], in0=g.xT[:, kc, t0:t0 + TT], in1=R[:, :], op=ALU.mult)
            kb.I("act", "activation", reads=[tmp, g.modA, g.mod], writes=[(g.hT, ti)], out=g.hT[:, kc, t0:t0 + TT], in_=tmp[:, :], func=AF.Identity,
                 scale=g.modA[:, which, kc, grp:grp + 1], bias=g.mod[:, sh0 + kc, grp:grp + 1])


def _ffn(g, l):
    kb = g.kb
    W1, W3, W2 = g.ins["ffn_w1"], g.ins["ffn_w3"], g.ins["ffn_w2"]
    actv = g.actb_flat[:, :].rearrange("p (b t) -> p b t", b=4)
    for c0 in range(0, DFF, 512):
        wdt = min(512, DFF - c0)
        nblk = wdt // 128
        w1b, w1v = wload(g, W1, wcols(W1, l, c0, c0 + wdt), KC, wdt)
        w3b, w3v = wload(g, W3, wcols(W3, l, c0, c0 + wdt), KC, wdt)
        for j in range(nblk):
            for ti in range(NT):
                t0 = ti * TT
                pa = g.nb()
                pbb = g.nb()
                for kc in range(KC):
                    kb.MM(pa[:, :], w1v[:, kc, j * 128:(j + 1) * 128], g.hT[:, kc, t0:t0 + TT], kc == 0, kc == KC - 1, reads=[w1b, (g.hT, ti)], writes=[pa])
                for kc in range(KC):
                    kb.MM(pbb[:, :], w3v[:, kc, j * 128:(j + 1) * 128], g.hT[:, kc, t0:t0 + TT], kc == 0, kc == KC - 1, reads=[w3b, (g.hT, ti)], writes=[pbb])
                sa = g.ntmp()
                kb.I("act", "activation", reads=[pa], writes=[sa], out=sa[:, :], in_=pa[:, :], func=AF.Silu)
                kb.I("dve", "tensor_tensor", reads=[sa, pbb], writes=[(g.actb, ti)], out=actv[:, j, t0:t0 + TT], in0=sa[:, :], in1=pbb[:, :], op=ALU.mult)
        w2b = g.nw()
        w2v = w2b[:, 0:nblk * 1024].rearrange("p (k n) -> p k n", k=nblk)
        kb.dma("pool", w2v, W2[l, c0:c0 + wdt, :].rearrange("(kc p) n -> p kc n", p=128), reads=[W2], writes=[w2b])
        for ob in range(8):
            for ti in range(NT):
                t0 = ti * TT
                grp = 0 if ti < 2 else 1
                py = g.nb()
                for j in range(nblk):
                    kb.MM(py[:, :], w2v[:, j, ob * 128:(ob + 1) * 128], actv[:, j, t0:t0 + TT], j == 0, j == nblk - 1, reads=[w2b, (g.actb, ti)], writes=[py])
                kb.I("dve", "scalar_tensor_tensor", reads=[py, g.mod, (g.xT, ti)], writes=[(g.xT, ti)], out=g.xT[:, ob, t0:t0 + TT], in0=py[:, :],
                     scalar=g.mod[:, 40 + ob, grp:grp + 1], in1=g.xT[:, ob, t0:t0 + TT], op0=ALU.mult, op1=ALU.add)


def _final(g):
    kb = g.kb
    Y = g.outd["y"]
    for ti in range(NT):
        t0 = ti * TT
        R = g.nR()
        _rms_scale(g, ti, R)
        for kc in range(KC):
            kb.I("dve", "scalar_tensor_tensor", reads=[(g.xT, ti), R, g.nrmf], writes=[(g.xT, ti)], out=g.xT[:, kc, t0:t0 + TT], in0=g.xT[:, kc, t0:t0 + TT],
                 scalar=g.nrmf[:, kc:kc + 1], in1=R[:, :], op0=ALU.mult, op1=ALU.mult)
        for tb in range(4):
            tt0 = t0 + tb * 128
            for half in range(2):
                pb = g.nb()
                for j in range(4):
                    kc = half * 4 + j
                    kb.TR(pb[:, j * 128:(j + 1) * 128], g.xT[:, kc, tt0:tt0 + 128], g.ident_f[:, :], reads=[(g.xT, ti), g.ident_f], writes=[pb])
                o = g.ntmp()
                evac(g, half, o[:, :], pb[:, :], [pb], [o])
                kb.dma("sp", Y[tt0:tt0 + 128, half * 512:(half + 1) * 512], o[:, :], reads=[o], writes=[(Y, (tt0, half))])


def carve(buf, off_bytes, shape, dtype):
    n = 1
    for s in shape[1:]:
        n *= s
    esz = 4 if dtype == F32 else 2
    a = buf[:, off_bytes // 2: off_bytes // 2 + n * esz // 2]
    if dtype == F32:
        a = a.bitcast(F32)
    if len(shape) == 2:
        return a
    names = " ".join(f"d{i}" for i in range(1, len(shape)))
    kw = {f"d{i}": shape[i] for i in range(1, len(shape) - 1)}
    return a.rearrange(f"p ({names}) -> p {names}", **kw)


_RANKC = {}


def rank_expr(e):
    k = id(e)
    if k not in _RANKC:
        _RANKC.clear()
        _RANKC[k] = e.snap(e.partition_id() % 4)
    return _RANKC[k]


class Xchg:
    def __init__(self, g, name, nb):
        kb = g.kb
        self.g, self.nb, self.h = g, nb, nb // 2
        self.gin = kb.dram(name + "_gin", [4, nb, 128, 1024], BF16)
        self.gout = kb.dram(name + "_gout", [4, 2, 4, self.h * 128, 1024], BF16)
        self.mine = kb.dram(name + "_mine", [2, 4, self.h * 128, 1024], BF16)

    def row(self, rk, j):
        return self.gin[rk, j]

    def run(self):
        kb, h = self.g.kb, self.h
        for rk in range(4):
            for part in range(2):
                i_ap = self.gin[rk, part * h:(part + 1) * h].rearrange("a p t -> (a p) t")
                o_ap = self.gout[rk, part].rearrange("r q t -> (r q) t")
                kb.custom("pool", (lambda en, i_ap=i_ap, o_ap=o_ap: en.collective_compute("AllGather", ALU.bypass, replica_groups=GROUPS4, ins=[i_ap.opt()], outs=[o_ap.opt()])), 1,
                          reads=[self.gin], writes=[self.gout])
        for part in range(2):
            src = self.gout.t.ap()
            kb.dma("pool", self.mine[part].rearrange("r q t -> r (q t)"),
                   (lambda en, part=part, src=src: src[bass.ds(rank_expr(en), 1), part].rearrange("a r q t -> (a r) (q t)")), reads=[self.gout], writes=[self.mine])

    def get(self, rr, j):
        return self.mine[j // self.h, rr, (j % self.h) * 128:(j % self.h + 1) * 128, :]


class XchgOut:
    def __init__(self, g, name):
        kb = g.kb
        self.g = g
        self.gin = kb.dram(name + "_gin", [2, 128, 4096], BF16)
        self.gout = kb.dram(name + "_gout", [2, 4, 128, 4096], BF16)
        self.mine = kb.dram(name + "_mine", [1024, 1024], BF16)

    def run(self):
        kb = self.g.kb
        for w in range(2):
            i_ap = self.gin[w]
            o_ap = self.gout[w].rearrange("r p t -> (r p) t")
            kb.custom("pool", (lambda en, i_ap=i_ap, o_ap=o_ap: en.collective_compute("AllGather", ALU.bypass, replica_groups=GROUPS4, ins=[i_ap.opt()], outs=[o_ap.opt()])), 1,
                      reads=[self.gin], writes=[self.gout])
        src = self.gout.t.ap().rearrange("w r p (k t) -> (w r p) k t", k=4)
        kb.dma("pool", self.mine.t.ap(), (lambda en: src[:, bass.ds(rank_expr(en), 1), :].rearrange("f a t -> f (a t)")), reads=[self.gout], writes=[self.mine])


def out_proj(g, W, widx, gate0):
    kb = g.kb
    for c0 in range(0, D, 512):
        wb, wv = wload(g, W, wcols(W, widx, c0, c0 + 512), KC, 512)
        for j in range(4):
            ob = c0 // 128 + j
            for ti in range(NT):
                t0 = ti * TT
                grp = 0 if ti < 2 else 1
                py = g.nb()
                for kc in range(KC):
                    kb.MM(py[:, :], wv[:, kc, j * 128:(j + 1) * 128], g.hT[:, kc, t0:t0 + TT], kc == 0, kc == KC - 1, reads=[wb, (g.hT, ti)], writes=[py])
                kb.I("dve", "scalar_tensor_tensor", reads=[py, g.mod, (g.xT, ti)], writes=[(g.xT, ti)], out=g.xT[:, ob, t0:t0 + TT], in0=py[:, :],
                     scalar=g.mod[:, gate0 + ob, grp:grp + 1], in1=g.xT[:, ob, t0:t0 + TT], op0=ALU.mult, op1=ALU.add)


def tokmajor_out(g, wb, wv, ncols, OUTB, oidx, col0, extra=None):
    kb = g.kb
    for tb in range(8):
        pb = g.nb()
        for kc in range(KC):
            kb.MM(pb[:, 0:ncols], g.hT[:, kc, tb * 128:(tb + 1) * 128], wv[:, kc, 0:ncols], kc == 0, kc == KC - 1, reads=[wb, (g.hT, tb // 4)], writes=[pb])
        o32 = g.ntmp()
        evac(g, tb, o32[:, 0:ncols], pb[:, 0:ncols], [pb], [o32])
        kb.dma("sp", OUTB[oidx, tb * 128:(tb + 1) * 128, col0:col0 + ncols], o32[:, 0:ncols], reads=[o32], writes=[(OUTB, (oidx, tb, col0))])
        if extra is not None:
            extra(tb, o32)


def diff_combine(g, A0, B0, A1, B1, dst, neglam, gsub):
    kb = g.kb
    st = g.nsmall()
    kb.I("dve", "reciprocal", reads=[B0], writes=[st], out=st[:, 0:1], in_=A0[:, 128:129])
    kb.I("dve", "reciprocal", reads=[B1], writes=[st], out=st[:, 1:2], in_=A1[:, 128:129])
    kb.I("dve", "tensor_tensor", reads=[st, neglam], writes=[st], out=st[:, 2:3], in0=st[:, 1:2], in1=neglam[:, 0:1], op=ALU.mult)
    O = g.ntmp()
    kb.I("dve", "tensor_scalar", reads=[B0, st], writes=[O], out=O[:, 0:128], in0=A0[:, 0:128], scalar1=st[:, 0:1], scalar2=None, op0=ALU.mult)
    kb.I("dve", "scalar_tensor_tensor", reads=[B1, st, O], writes=[O], out=O[:, 0:128], in0=A1[:, 0:128], scalar=st[:, 2:3], in1=O[:, 0:128], op0=ALU.mult, op1=ALU.add)
    kb.I("act", "activation", reads=[O], writes=[O, st], out=O[:, 128:256], in_=O[:, 0:128], func=AF.Square, accum_out=st[:, 3:4])
    kb.I("act", "activation", reads=[st, g.eps_t], writes=[st], out=st[:, 4:5], in_=st[:, 3:4], func=AF.Sqrt, scale=1.0 / 128, bias=g.eps_t[:, 0:1])
    kb.I("dve", "reciprocal", reads=[st], writes=[st], out=st[:, 5:6], in_=st[:, 4:5])
    kb.I("dve", "scalar_tensor_tensor", reads=[O, st, gsub], writes=[dst[1]], out=dst[0], in0=O[:, 0:128], scalar=st[:, 5:6], in1=gsub[:, :], op0=ALU.mult, op1=ALU.mult)


def _odd_mixer(g, l):
    kb = g.kb
    I = g.ins
    o = l // 2
    lam_init = 0.8 - 0.6 * math.exp(-0.3 * l)
    W, Wsw, Wout = I["w_qkv_diff"], I["w_qk_sw"], I["w_out_diff"]
    KO, VO = g.outd["o_df_k"], g.outd["o_df_v"]
    pinP = kb.dram(f"pinP{l}", [3072, 1024], BF16)
    XS = Xchg(g, f"oxs{l}", 6)
    XO = XchgOut(g, f"oxo{l}")
    scr = g.scr
    neglam = g.sm_neglam
    gsub = g.sm_gsub
    t = g.ntmp()
    kb.dma("sp", t[:, 256:384], I["lamq"][:, o, :], reads=[I["lamq"]], writes=[t])
    kb.dma("sp", t[:, 384:512], I["lamk"][:, o, :], reads=[I["lamk"]], writes=[t])
    kb.I("dve", "tensor_tensor", reads=[t], writes=[t], out=t[:, 0:128], in0=t[:, 256:384], in1=t[:, 384:512], op=ALU.mult)
    kb.I("dve", "tensor_reduce", reads=[t], writes=[t], out=t[:, 128:130], in_=t[:, 0:128].rearrange("p (a b) -> p a b", a=2), axis=AX.X, op=ALU.add)
    kb.I("act", "activation", reads=[t], writes=[t], out=t[:, 130:132], in_=t[:, 128:130], func=AF.Exp)
    kb.I("dve", "tensor_tensor", reads=[t], writes=[neglam], out=neglam[:, 0:1], in0=t[:, 131:132], in1=t[:, 130:131], op=ALU.subtract)
    kb.I("dve", "tensor_scalar", reads=[neglam], writes=[neglam], out=neglam[:, 0:1], in0=neglam[:, 0:1], scalar1=-lam_init, scalar2=None, op0=ALU.add)
    kb.dma("sp", gsub[:, :], I["subln"][:, o, :], reads=[I["subln"]], writes=[gsub])
    kb.I("dve", "tensor_scalar", reads=[gsub], writes=[gsub], out=gsub[:, :], in0=gsub[:, :], scalar1=1.0 - lam_init, scalar2=None, op0=ALU.mult)
    ropeC = carve(g.actb_flat, 0, [128, 1024], F32)
    ropeS = carve(g.actb_flat, 4096, [128, 1024], F32)
    kb.dma("sp", ropeC, I["rope_c"][:, :], reads=[I["rope_c"]], writes=[g.actb_flat])
    kb.dma("sp", ropeS, I["rope_s"][:, :], reads=[I["rope_s"]], writes=[g.actb_flat])
    vtokP = carve(scr, 0, [128, 8, 8, 129], BF16)
    kb.I("pool", "memset", writes=[(scr, "vtokP")], ap=vtokP[:, :, :, 128:129], constant=1.0)
    ev = 0
    for gi in range(6):
        c0 = gi * 512
        wb, wv = wload(g, W, wcols(W, o, c0, c0 + 512), KC, 512)
        if gi < 4:
            wsb, wsv = wload(g, Wsw, wcols(Wsw, o, c0, c0 + 512), KC, 512)
        for j in range(4):
            blk = gi * 4 + j
            for ti in range(NT):
                t0 = ti * TT
                pb = g.nb()
                for kc in range(KC):
                    kb.MM(pb[:, :], wv[:, kc, j * 128:(j + 1) * 128], g.hT[:, kc, t0:t0 + TT], kc == 0, kc == KC - 1, reads=[wb, (g.hT, ti)], writes=[pb])
                stg = g.nstage()
                if ti < 2 or gi >= 4:
                    evac(g, ev, stg[:, :], pb[:, :], [pb], [stg]); ev += 1
                else:
                    pb2 = g.nb()
                    for kc in range(KC):
                        kb.MM(pb2[:, :], wsv[:, kc, j * 128:(j + 1) * 128], g.hT[:, kc, t0:t0 + TT], kc == 0, kc == KC - 1, reads=[wsb, (g.hT, ti)], writes=[pb2])
                    cs = (ti - 2) * 512
                    t1 = g.ntmp(); t2 = g.ntmp()
                    kb.I("dve", "tensor_tensor", reads=[pb, g.actb_flat], writes=[t1], out=t1[:, :], in0=pb[:, :], in1=ropeC[:, cs:cs + 512], op=ALU.mult)
                    kb.I("dve", "tensor_tensor", reads=[pb2, g.actb_flat], writes=[t2], out=t2[:, :], in0=pb2[:, :], in1=ropeS[:, cs:cs + 512], op=ALU.mult)
                    kb.I("pool", "tensor_tensor", reads=[t1, t2], writes=[stg], out=stg[:, :], in0=t1[:, :], in1=t2[:, :], op=ALU.add)
                cc = (ti % 2) * 512
                if ti < 2:
                    kb.dma("sp", pinP[blk * 128:(blk + 1) * 128, cc:cc + 512], stg[:, :], reads=[stg], writes=[(pinP, (blk, ti))])
                else:
                    kind, H = blk // 8, blk % 8
                    kb.dma("sp", XS.row(H // 2, kind * 2 + H % 2)[:, cc:cc + 512], stg[:, :], reads=[stg], writes=[(XS.gin, (blk, ti))])
        if gi >= 2:
            if gi < 4:
                tokmajor_out(g, wb, wv, 512, KO, o, (gi - 2) * 512)
            else:
                h0 = (gi - 4) * 4

                def extra(tb, o32, h0=h0):
                    kb.I("pool", "tensor_copy", reads=[o32], writes=[(scr, "vtokP")], out=vtokP[:, tb, h0:h0 + 4, 0:128], in_=o32[:, :].rearrange("p (h d) -> p h d", h=4))
                tokmajor_out(g, wb, wv, 512, VO, o, (gi - 4) * 512, extra)
    import os
    STOP = os.environ.get("ODDSTOP", "")
    XS.run()
    if STOP == "proj":
        return
    qk = carve(scr, 16512, [128, 2, 8, 256], BF16)
    pTs = [carve(scr, 24704 + i * 1024, [128, 512], BF16) for i in range(2)]
    Otok = carve(scr, 26752, [128, 2, 1024], BF16)
    for b in range(4):
        for w in range(2):
            kb.dma("sp", qk[:, w, :, :], pinP[w * 1024:(w + 1) * 1024, b * 256:(b + 1) * 256].rearrange("(h p) t -> p h t", p=128), reads=[pinP], writes=[(scr, "qk")])
        for h in range(8):
            pos = []
            for s in range(2):
                ps = g.nb()
                for kc2 in range(2):
                    kb.MM(ps[:, kc2 * 256:(kc2 + 1) * 256], qk[64 * s:64 * s + 64, 1, h, kc2 * 128:(kc2 + 1) * 128], qk[64 * s:64 * s + 64, 0, h, :], True, True,
                          reads=[(scr, "qk")], writes=[ps])
                pT = pTs[s]
                kb.I("act", "activation", reads=[ps], writes=[(scr, ("pT", s))], out=pT, in_=ps[:, :], func=AF.Exp, scale=SCALE)
                po = g.nb()
                for qb in range(2):
                    for kc2 in range(2):
                        kb.MM(po[:, qb * 129:(qb + 1) * 129], pT[:, kc2 * 256 + qb * 128:kc2 * 256 + (qb + 1) * 128], vtokP[:, b * 2 + kc2, h, :], kc2 == 0, kc2 == 1,
                              reads=[(scr, ("pT", s)), (scr, "vtokP")], writes=[po])
                pos.append(po)
            for qb in range(2):
                diff_combine(g, pos[0][:, qb * 129:(qb + 1) * 129], pos[0], pos[1][:, qb * 129:(qb + 1) * 129], pos[1],
                             (Otok[:, qb, h * 128:(h + 1) * 128], (scr, "Otok")), neglam, gsub)
        for qb in range(2):
            pt = g.nbt()
            for blk in range(8):
                kb.TR(pt[:, blk * 128:(blk + 1) * 128], Otok[:, qb, blk * 128:(blk + 1) * 128], g.ident_b[:, :], reads=[(scr, "Otok"), g.ident_b], writes=[pt])
            evac(g, qb, g.hT[:, :, b * 256 + qb * 128:b * 256 + (qb + 1) * 128], pt[:, :].rearrange("p (k t) -> p k t", k=8), [pt], [(g.hT, b // 2)])
    if STOP == "prompt":
        return
    QT = carve(scr, 0, [128, 4096], BF16)
    KT = carve(scr, 8192, [128, 4608], BF16)
    Vtok = carve(scr, 17408, [128, 36, 129], BF16)
    pTs = [carve(scr, 26696 + i * 1024, [128, 512], BF16) for i in range(2)]
    OtS = carve(scr, 28744, [128, 32, 128], BF16)
    VTt = carve(scr, 28744, [128, 4096], BF16)
    O1s = carve(scr, 36936, [128, 4, 129], F32)
    ACC = g.PBL[0:4]
    PSR = Rot(g.PBL[4:6])
    for hh in range(2):
        allS = [(scr, None)]
        for rr in range(4):
            for (dst, kind, nm) in ((QT, 0, "QT"), (KT, 1, "KT"), (VTt, 2, "VT")):
                off = 512 if nm == "KT" else 0
                kb.dma("sp" if rr % 2 == 0 else "act", dst[:, off + rr * 1024:off + (rr + 1) * 1024], XS.get(rr, kind * 2 + hh), reads=[XS.mine], writes=allS)
        kst = g.nstage()
        kstv = kst[:, :].rearrange("p (c d) -> p c d", c=4)
        kb.dma("pool", kstv, I["cdf_k"][o, hh].rearrange("(c p) d -> p c d", p=128), reads=[I["cdf_k"]], writes=[kst])
        pt = g.nbt()
        for c in range(4):
            kb.TR(pt[:, c * 128:(c + 1) * 128], kstv[:, c, :], g.ident_b[:, :], reads=[kst, g.ident_b], writes=[pt])
        evac(g, 0, KT[:, 0:512], pt[:, 0:512], [pt], allS)
        vst = g.nstage()
        vstv = vst[:, :].rearrange("p (c d) -> p c d", c=4)
        kb.dma("pool", vstv, I["cdf_v"][o, hh].rearrange("(c p) d -> p c d", p=128), reads=[I["cdf_v"]], writes=[vst])
        kb.I("pool", "tensor_copy", reads=[vst], writes=allS, out=Vtok[:, 0:4, 0:128], in_=vstv)
        kb.I("pool", "memset", writes=allS, ap=Vtok[:, :, 128:129], constant=1.0)
        for c8 in range(4):
            pt = g.nbt()
            for j in range(8):
                c = c8 * 8 + j
                kb.TR(pt[:, j * 128:(j + 1) * 128], VTt[:, c * 128:(c + 1) * 128], g.ident_b[:, :], reads=allS + [g.ident_b], writes=[pt])
            evac(g, c8, Vtok[:, 4 + c8 * 8:4 + (c8 + 1) * 8, 0:128], pt[:, :].rearrange("p (c d) -> p c d", c=8), [pt], allS)
        if STOP == "sload":
            return
        for qg in range(8 if STOP != "sattn1" else 1):
            for s in range(2):
                for kc in range(36):
                    ps = PSR()
                    kb.MM(ps[:, :], KT[64 * s:64 * s + 64, kc * 128:(kc + 1) * 128], QT[64 * s:64 * s + 64, qg * 512:(qg + 1) * 512], True, True, reads=allS, writes=[ps])
                    pT = pTs[kc % 2]
                    kb.I("act", "activation", reads=[ps], writes=[(scr, ("pTs", kc % 2))], out=pT, in_=ps[:, :], func=AF.Exp, scale=SCALE)
                    for qb in range(4):
                        kb.MM(ACC[qb][:, 0:129], pT[:, qb * 128:(qb + 1) * 128], Vtok[:, kc, :], kc == 0, kc == 35,
                              reads=[(scr, ("pTs", kc % 2)), (scr, "static")], writes=[ACC[qb]])
                if s == 0:
                    for qb in range(4):
                        evac(g, qb, O1s[:, qb, :], ACC[qb][:, 0:129], [ACC[qb]], [(scr, ("O1s", qb))])
            for qb in range(4):
                diff_combine(g, O1s[:, qb, :], (scr, ("O1s", qb)), ACC[qb][:, 0:129], ACC[qb], (OtS[:, qg * 4 + qb, :], (scr, ("OtS", qg))), neglam, gsub)
        oTs = QT
        for c8 in range(4):
            pt = g.nbt()
            for j in range(8):
                c = c8 * 8 + j
                kb.TR(pt[:, j * 128:(j + 1) * 128], OtS[:, c, :], g.ident_b[:, :], reads=[(scr, None), g.ident_b], writes=[pt])
            evac(g, c8, oTs[:, c8 * 1024:(c8 + 1) * 1024], pt[:, :], [pt], [(scr, None)])
        kb.dma("sp", XO.gin[hh], oTs[:, :], reads=[(scr, None)], writes=[(XO.gin, hh)])
    XO.run()
    mo = XO.mine.t.ap().rearrange("(w rr p) t -> p w rr t", w=2, rr=4)
    for w in range(2):
        kb.dma("sp" if w == 0 else "act", g.hT[:, :, 1024:2048].rearrange("p (rr w) t -> p w rr t", w=2)[:, w], mo[:, w], reads=[XO.mine], writes=[(g.hT, 2), (g.hT, 3)])
    out_proj(g, Wout, o, 16)


def _even_mixer(g, l):
    kb = g.kb
    I = g.ins
    e = l // 2
    W, Wout = I["w_in_even"], I["w_out_even"]
    KO, VO = g.outd["o_na_k"], g.outd["o_na_v"]
    pinP = kb.dram(f"epinP{l}", [3328, 1024], BF16)
    XS = Xchg(g, f"exs{l}", 6)
    XO = XchgOut(g, f"exo{l}")
    lin = kb.dram(f"elin{l}", [256, 1024], BF16)
    lout = kb.dram(f"elout{l}", [4, 256, 1024], BF16)
    scr = g.scr
    vtokP = carve(scr, 0, [128, 8, 8, 65], BF16)
    kb.I("pool", "memset", writes=[(scr, "vtokP")], ap=vtokP[:, :, :, 64:65], constant=1.0)
    ev = 0
    for gi, c0 in enumerate(range(0, 3328, 512)):
        wdt = min(512, 3328 - c0)
        wb, wv = wload(g, W, wcols(W, e, c0, c0 + wdt), KC, wdt)
        for j in range(wdt // 128):
            blk = c0 // 128 + j
            for ti in range(NT):
                t0 = ti * TT
                pb = g.nb()
                for kc in range(KC):
                    kb.MM(pb[:, :], wv[:, kc, j * 128:(j + 1) * 128], g.hT[:, kc, t0:t0 + TT], kc == 0, kc == KC - 1, reads=[wb, (g.hT, ti)], writes=[pb])
                stg = g.nstage()
                evac(g, ev, stg[:, :], pb[:, :], [pb], [stg]); ev += 1
                cc = (ti % 2) * 512
                if ti < 2:
                    kb.dma("sp", pinP[blk * 128:(blk + 1) * 128, cc:cc + 512], stg[:, :], reads=[stg], writes=[(pinP, (blk, ti))])
                elif blk < 24:
                    kb.dma("sp", XS.row(blk % 4, blk // 4)[:, cc:cc + 512], stg[:, :], reads=[stg], writes=[(XS.gin, (blk, ti))])
                else:
                    kb.dma("sp", lin[(blk - 24) * 128:(blk - 23) * 128, cc:cc + 512], stg[:, :], reads=[stg], writes=[(lin, (blk, ti))])
        if gi == 1:
            tokmajor_out(g, wb, wv, 512, KO, e, 0)
        if gi == 2:
            def extra(tb, o32):
                kb.I("pool", "tensor_copy", reads=[o32], writes=[(scr, "vtokP")], out=vtokP[:, tb, :, 0:64], in_=o32[:, :].rearrange("p (h d) -> p h d", h=8))
            tokmajor_out(g, wb, wv, 512, VO, e, 0, extra)
    XS.run()
    kb.custom("pool", (lambda en: en.collective_compute("AllGather", ALU.bypass, replica_groups=GROUPS4, ins=[lin.t.ap().opt()], outs=[lout.t.ap().rearrange("r q t -> (r q) t").opt()])), 1,
              reads=[lin], writes=[lout])

    qk = carve(scr, 8320, [128, 2, 4, 256], BF16)
    pTp = [carve(scr, 12416 + i * 1024, [128, 512], BF16) for i in range(2)]
    Otok = carve(scr, 14464, [128, 2, 512], BF16)
    for b in range(4):
        kb.dma("sp", qk, pinP[0:1024, b * 256:(b + 1) * 256].rearrange("(w j p) t -> p w j t", w=2, j=4), reads=[pinP], writes=[(scr, "qk")])
        for h in range(8):
            j, hb = h // 2, 64 * (h % 2)
            ps = g.nb()
            for kc2 in range(2):
                kb.MM(ps[:, kc2 * 256:(kc2 + 1) * 256], qk[hb:hb + 64, 1, j, kc2 * 128:(kc2 + 1) * 128], qk[hb:hb + 64, 0, j, :], True, True, reads=[(scr, "qk")], writes=[ps])
            pT = pTp[h % 2]
            kb.I("act", "activation", reads=[ps], writes=[(scr, ("pTp", h % 2))], out=pT, in_=ps[:, :], func=AF.Exp, scale=SCALE)
            po = g.nb()
            for qb in range(2):
                for kc2 in range(2):
                    kb.MM(po[:, qb * 65:(qb + 1) * 65], pT[:, kc2 * 256 + qb * 128:kc2 * 256 + (qb + 1) * 128], vtokP[:, b * 2 + kc2, h, :], kc2 == 0, kc2 == 1,
                          reads=[(scr, ("pTp", h % 2)), (scr, "vtokP")], writes=[po])
            st = g.nsmall()
            kb.I("dve", "reciprocal", reads=[po], writes=[st], out=st[:, 0:2], in_=po[:, 0:130].rearrange("p (q c) -> p q c", q=2)[:, :, 64])
            for qb in range(2):
                kb.I("dve", "tensor_scalar", reads=[po, st], writes=[(scr, "Otok")], out=Otok[:, qb, h * 64:(h + 1) * 64], in0=po[:, qb * 65:qb * 65 + 64],
                     scalar1=st[:, qb:qb + 1], scalar2=None, op0=ALU.mult)
        for qb in range(2):
            pt = g.nbt()
            for blk in range(4):
                kb.TR(pt[:, blk * 128:(blk + 1) * 128], Otok[:, qb, blk * 128:(blk + 1) * 128], g.ident_b[:, :], reads=[(scr, "Otok"), g.ident_b], writes=[pt])
            evac(g, qb, g.hT[:, 0:4, b * 256 + qb * 128:b * 256 + (qb + 1) * 128], pt[:, 0:512].rearrange("p (k t) -> p k t", k=4), [pt], [(g.hT, b // 2)])

    allS = [(scr, None)]
    QT = carve(scr, 0, [128, 4096], BF16)
    KT = carve(scr, 8192, [128, 4608], BF16)
    VtokS = carve(scr, 17408, [128, 2, 36, 65], BF16)
    pTL = [carve(scr, 26768 + i * 1280, [128, 640], BF16) for i in range(2)]
    pTC = [carve(scr, 29328 + i * 1024, [128, 512], BF16) for i in range(2)]
    tmpL = [carve(scr, 31376 + i * 2560, [128, 640], F32) for i in range(2)]
    OtS = carve(scr, 36496, [128, 32, 128], BF16)
    VTt = carve(scr, 36496, [128, 4096], BF16)
    tabv = carve(g.actb_flat, 0, [128, 5, 5, 128], F32)
    for rr in range(4):
        for (dst, kind) in ((QT, 0), (KT, 1), (VTt, 2)):
            kb.dma("sp" if rr % 2 == 0 else "act", dst[:, rr * 1024:(rr + 1) * 1024], XS.get(rr, kind), reads=[XS.mine], writes=allS)
    kst = g.nstage()
    kstv = kst[:, :].rearrange("p (c d) -> p c d", c=4)
    kb.dma("pool", kstv, I["cna_k"][e].rearrange("(c p) d -> p c d", p=128), reads=[I["cna_k"]], writes=[kst])
    pt = g.nbt()
    for c in range(4):
        kb.TR(pt[:, c * 128:(c + 1) * 128], kstv[:, c, :], g.ident_b[:, :], reads=[kst, g.ident_b], writes=[pt])
    evac(g, 0, KT[:, 4096:4608], pt[:, 0:512], [pt], allS)
    vst = g.nstage()
    vstv = vst[:, :].rearrange("p (c d) -> p c d", c=4)
    kb.dma("pool", vstv, I["cna_v"][e].rearrange("(c p) d -> p c d", p=128), reads=[I["cna_v"]], writes=[vst])
    kb.I("pool", "tensor_copy", reads=[vst], writes=allS, out=VtokS[:, :, 32:36, 0:64], in_=vstv.rearrange("p c (h d) -> p h c d", h=2))
    kb.I("pool", "memset", writes=allS, ap=VtokS[:, :, :, 64:65], constant=1.0)
    for c8 in range(4):
        pt = g.nbt()
        for jj in range(8):
            c = c8 * 8 + jj
            kb.TR(pt[:, jj * 128:(jj + 1) * 128], VTt[:, c * 128:(c + 1) * 128], g.ident_b[:, :], reads=allS + [g.ident_b], writes=[pt])
        evac(g, c8, VtokS[:, :, c8 * 8:(c8 + 1) * 8, 0:64], pt[:, :].rearrange("p (c h d) -> p h c d", c=8, h=2), [pt], allS)
    for hh in range(2):
        hb = 64 * hh
        kb.dma("sp", tabv, I["na_tab"][e, hh], reads=[I["na_tab"]], writes=[g.actb_flat])
        for ip in range(32):
            i = 2 * ip
            if i < 4:
                var, r0c, nch = 1 + i // 2, 0, 4
            elif i >= 60:
                var, r0c, nch = 3 + (i - 60) // 2, 56, 4
            else:
                var, r0c, nch = 0, i - 4, 5
            q_ap = QT[hb:hb + 64, i * 64:i * 64 + 128]
            psL0 = g.nb()
            for m in range(4):
                kb.MM(psL0[:, m * 128:(m + 1) * 128], KT[hb:hb + 64, (r0c + 2 * m) * 64:(r0c + 2 * m) * 64 + 128], q_ap, True, True, reads=allS, writes=[psL0])
            tl = tmpL[ip % 2]
            kb.I("dve", "scalar_tensor_tensor", reads=[psL0, g.actb_flat], writes=[(scr, ("tmpL", ip % 2))], out=tl[:, 0:512], in0=psL0[:, :], scalar=SCALE,
                 in1=tabv[:, var, 0:4, :].rearrange("p m q -> p (m q)"), op0=ALU.mult, op1=ALU.add)
            if nch == 5:
                psL1 = g.nb()
                kb.MM(psL1[:, 0:128], KT[hb:hb + 64, (r0c + 8) * 64:(r0c + 8) * 64 + 128], q_ap, True, True, reads=allS, writes=[psL1])
                kb.I("dve", "scalar_tensor_tensor", reads=[psL1, g.actb_flat], writes=[(scr, ("tmpL", ip % 2))], out=tl[:, 512:640], in0=psL1[:, 0:128], scalar=SCALE,
                     in1=tabv[:, var, 4, :], op0=ALU.mult, op1=ALU.add)
            psC = g.nb()
            for c in range(4):
                kb.MM(psC[:, c * 128:(c + 1) * 128], KT[hb:hb + 64, 4096 + c * 128:4096 + (c + 1) * 128], q_ap, True, True, reads=allS, writes=[psC])
            pl, pc = pTL[ip % 2], pTC[ip % 2]
            kb.I("act", "activation", reads=[(scr, ("tmpL", ip % 2))], writes=[(scr, ("pTL", ip % 2))], out=pl[:, 0:nch * 128], in_=tl[:, 0:nch * 128], func=AF.Exp)
            kb.I("act", "activation", reads=[psC], writes=[(scr, ("pTC", ip % 2))], out=pc, in_=psC[:, :], func=AF.Exp, scale=SCALE)
            po = g.nb()
            for m in range(nch):
                kb.MM(po[:, 0:65], pl[:, m * 128:(m + 1) * 128], VtokS[:, hh, r0c // 2 + m, :], m == 0, False, reads=[(scr, ("pTL", ip % 2)), (scr, "static")], writes=[po])
            for c in range(4):
                kb.MM(po[:, 0:65], pc[:, c * 128:(c + 1) * 128], VtokS[:, hh, 32 + c, :], False, c == 3, reads=[(scr, ("pTC", ip % 2)), (scr, "static")], writes=[po])
            st = g.nsmall()
            kb.I("dve", "reciprocal", reads=[po], writes=[st], out=st[:, 0:1], in_=po[:, 64:65])
            kb.I("dve", "tensor_scalar", reads=[po, st], writes=[(scr, ("OtS", ip))], out=OtS[:, ip, hb:hb + 64], in0=po[:, 0:64], scalar1=st[:, 0:1], scalar2=None, op0=ALU.mult)
    oTs = QT
    for c8 in range(4):
        pt = g.nbt()
        for jj in range(8):
            c = c8 * 8 + jj
            kb.TR(pt[:, jj * 128:(jj + 1) * 128], OtS[:, c, :], g.ident_b[:, :], reads=[(scr, None), g.ident_b], writes=[pt])
        evac(g, c8, oTs[:, c8 * 1024:(c8 + 1) * 1024], pt[:, :], [pt], [(scr, None)])
    kb.dma("sp", XO.gin[0], oTs[:, :], reads=[(scr, None)], writes=[(XO.gin, "na")])

    _rwkv(g, l, pinP, XS, lout, XO)

    XO.run()
    kb.dma("sp", g.hT[:, :, 1024:2048], XO.mine.t.ap().rearrange("(k p) t -> p k t", p=128), reads=[XO.mine], writes=[(g.hT, 2), (g.hT, 3)])
    out_proj(g, Wout, e, 16)


NS = 512
NBS = 4
NEG_E05 = -math.exp(-0.5)


def _rw_setup(g):
    if hasattr(g, "rw"):
        return g.rw
    kb, I = g.kb, g.ins
    R = Ctx()
    g.rw = R
    for nm, shape in (("rw_mu", [128, 2, 17, 2]), ("rw_w0", [128, 2, 2, 5]), ("rw_a0", [128, 2, 2, 5]), ("rw_bonus", [128, 2, 2, 5]),
                      ("rw_kk", [128, 2, 5]), ("rw_ka", [128, 2, 5]), ("rw_lnw", [128, 2, 5]), ("rw_lnb", [128, 2, 5])):
        t = kb.sb("p_" + nm, shape, F32)
        kb.dma("sp", t.t.ap(), I[nm].t.ap(), reads=[I[nm]], writes=[t])
        setattr(R, nm, t)
    R.c0 = kb.sb("rw_c0", [128, 2, 17], F32)
    kb.I("dve", "tensor_tensor", reads=[R.rw_mu], writes=[R.c0], out=R.c0[:, :, :], in0=R.rw_mu[:, :, :, 0], in1=R.rw_mu[:, :, :, 1], op=ALU.add)
    kb.I("dve", "tensor_scalar", reads=[R.c0], writes=[R.c0], out=R.c0[:, :, :], in0=R.c0[:, :, :], scalar1=-1.0, scalar2=1.0, op0=ALU.mult, op1=ALU.add)
    R.omk = kb.sb("rw_omk", [128, 2, 5], F32)
    kb.I("dve", "tensor_scalar", reads=[R.rw_ka], writes=[R.omk], out=R.omk[:, :, :], in0=R.rw_ka[:, :, :], scalar1=-1.0, scalar2=1.0, op0=ALU.mult, op1=ALU.add)
    R.wa2 = kb.sb("rw_wa2", [128, 2, 640], BF16)
    R.g2 = kb.sb("rw_g2s", [128, 640], BF16)
    R.m01 = kb.sb("rw_m01", [128, NS], F32)
    kb.I("pool", "memset", writes=[R.m01], ap=R.m01[:, :], constant=1.0)
    kb.I("pool", "memset", writes=[R.m01], ap=R.m01[:, :].rearrange("p (b t) -> p b t", b=NBS)[:, :, 0:1], constant=0.0)
    R.cm3 = kb.sb("rw_cm3", [128, 2, 384], BF16)
    R.cm2 = kb.sb("rw_cm2", [128, 2, 256], BF16)
    MK = I["masks"]
    for d in range(2):
        mS, mI, mN = (0, 1, 2) if d == 0 else (2, 3, 0)
        for k, m in enumerate((mS, mN, mI)):
            kb.dma("pool", R.cm3[:, d, k * 128:(k + 1) * 128], MK[:, m, :], reads=[MK], writes=[R.cm3])
        for k, m in enumerate((mS, mI)):
            kb.dma("pool", R.cm2[:, d, k * 128:(k + 1) * 128], MK[:, m, :], reads=[MK], writes=[R.cm2])
    R.lvl = kb.sb("rw_lvl", [128, 7, 128], BF16)
    kb.dma("pool", R.lvl[:, :, :], I["lvlmask"][:, :, :], reads=[I["lvlmask"]], writes=[R.lvl])
    R.S32 = kb.sb("rw_S32", [128, 2, 128], F32)
    R.S16 = kb.sb("rw_S16", [128, 2, 128], BF16)
    R.gam = kb.sb("rw_gam", [128, NBS], F32)
    return R


def _rw_layout(g):
    scr, ab = g.scr, g.actb_flat
    L = Ctx()
    o = 0

    def take(buf, nbytes, shape, dt):
        nonlocal o
        a = carve(buf, o, shape, dt)
        o += (nbytes + 63) // 64 * 64
        return a
    L.U5 = take(scr, 5 * 544 * 2, [128, 5, 544], BF16)
    L.hst = take(scr, 5 * 2 * 32, [128, 5, 2, 16], BF16)
    L.sr = take(scr, 2048, [128, NS], F32)
    L.sk = take(scr, 2048, [128, NS], F32)
    L.sv = take(scr, 2048, [128, NS], F32)
    L.kk = take(scr, 2048, [128, NS], F32)
    L.swa = take(scr, 1024, [128, NS], BF16)
    L.sg = take(scr, 1024, [128, NS], BF16)
    L.svb = take(scr, 1024, [128, NS], BF16)
    L.prod = take(scr, 6 * 1024, [128, 6, NS], BF16)
    L.tok = take(scr, 4 * 1024, [128, 4, NBS, 128], BF16)
    L.Vpad = take(scr, 2048, [128, NBS, 2, 128], BF16)
    L.G3 = [take(scr, 768, [128, 384], BF16) for _ in range(2)]
    L.G2 = [take(scr, 512, [128, 256], BF16) for _ in range(2)]
    L.TA = [[take(scr, 512, [128, 256], BF16) for _ in range(2)] for _ in range(2)]
    L.XB = [take(scr, 512, [128, 256], BF16) for _ in range(2)]
    L.NLl = [[take(scr, 512, [128, 256], BF16) for _ in range(2)] for _ in range(2)]
    L.Y1b = [take(scr, 128, [128, 64], BF16) for _ in range(2)]
    L.U0 = take(scr, 512, [128, 128], F32)
    L.Ub = take(scr, 256, [128, 128], BF16)
    L.Upad = take(scr, 768, [128, 384], BF16)
    L.WT = take(scr, 256, [128, 128], BF16)
    L.Yacc = take(scr, 2048, [128, NS], F32)
    L.Yb = take(scr, 1024, [128, NS], BF16)
    L.ob = take(scr, 1024, [128, NS], BF16)
    assert o <= 45056, o
    o = 0
    L.LW = take(ab, 2048, [128, NS], F32)
    L.L = take(ab, 2048, [128, NS], F32)
    L.A = take(ab, 2048, [128, NS], F32)
    L.KT = take(ab, 2048, [128, NS], F32)
    L.E = take(ab, 2048, [128, NS], F32)
    L.E2 = take(ab, 2048, [128, NS], F32)
    L.T1 = take(ab, 2048, [128, NS], F32)
    assert o <= 16384, o
    return L


def _rw_shift(g, R, L, e, jp, sample):
    kb = g.kb
    SEG = [(g.scr, "seg")]
    AB = [(g.actb_flat, "seg")]
    mub = [jp if jp < 4 else 14, 4 + jp if jp < 4 else 15, 8 + jp if jp < 4 else 16, 12, 13]
    dsts = [L.sr, L.sk, L.sv, L.T1, L.E]
    for a in range(5):
        dst, mb = dsts[a], mub[a]
        wr = SEG if a < 3 else AB
        if sample:
            cur, prv, nxt = L.U5[:, a, 16:528], L.U5[:, a, 15:527], L.U5[:, a, 17:529]
            d0, d1, d2 = dst[:, :], dst[:, :], dst[:, :]
        else:
            uv = L.U5[:, a, 16:528].rearrange("p (s t) -> p s t", s=2)
            dv = dst[:, :].rearrange("p (s t) -> p s t", s=2)
            cur, prv, nxt = L.U5[:, a, 16:528], uv[:, :, 0:255], uv[:, :, 1:256]
            d0, d1, d2 = dst[:, :], dv[:, :, 1:256], dv[:, :, 0:255]
        kb.I("dve", "tensor_scalar", reads=SEG + [R.c0], writes=wr, out=d0, in0=cur, scalar1=R.c0[:, e, mb:mb + 1], scalar2=None, op0=ALU.mult)
        kb.I("dve", "scalar_tensor_tensor", reads=SEG + wr + [R.rw_mu], writes=wr, out=d1, in0=prv, scalar=R.rw_mu[:, e, mb, 0:1], in1=d1, op0=ALU.mult, op1=ALU.add)
        kb.I("dve", "scalar_tensor_tensor", reads=SEG + wr + [R.rw_mu], writes=wr, out=d2, in0=nxt, scalar=R.rw_mu[:, e, mb, 1:2], in1=d2, op0=ALU.mult, op1=ALU.add)
    kb.I("act", "activation", reads=AB, writes=SEG, out=L.swa[0:64, :], in_=L.T1[0:64, :], func=AF.Tanh)
    kb.I("act", "copy", reads=AB, writes=SEG, out=L.swa[64:128, :], in_=L.T1[64:128, :])
    kb.I("act", "activation", reads=AB, writes=SEG, out=L.sg[:, :], in_=L.E[:, :], func=AF.Sigmoid)
    kb.I("act", "copy", reads=SEG, writes=SEG, out=L.svb[:, :], in_=L.sv[:, :])
    kb.I("dve", "tensor_scalar", reads=SEG + [R.rw_kk], writes=SEG, out=L.kk[:, :], in0=L.sk[:, :], scalar1=R.rw_kk[:, e, jp:jp + 1], scalar2=None, op0=ALU.mult)
    sq = g.nstage()
    kb.I("act", "activation", reads=SEG, writes=[sq], out=sq[:, :], in_=L.kk[:, :], func=AF.Square)
    pn = g.nb()
    kb.MM(pn[:, :], g.blk_b[:, :], sq[:, :], True, True, reads=[sq, g.blk_b], writes=[pn])
    kb.I("dve", "tensor_scalar", reads=[pn], writes=AB, out=L.E[:, :], in0=pn[:, :], scalar1=1e-12, scalar2=None, op0=ALU.max)
    kb.I("act", "activation", reads=AB, writes=AB, out=L.E[:, :], in_=L.E[:, :], func=AF.Sqrt)
    kb.I("dve", "reciprocal", reads=AB, writes=AB, out=L.E[:, :], in_=L.E[:, :])
    kb.I("dve", "tensor_tensor", reads=SEG + AB, writes=SEG, out=L.kk[:, :], in0=L.kk[:, :], in1=L.E[:, :], op=ALU.mult)


def _rw_lora_a_kt(g, R, L, e, jp, d, want_w):
    kb = g.kb
    SEG = [(g.scr, "seg")]
    AB = [(g.actb_flat, "seg")]
    c0 = jp * 128
    pa = g.nb()
    kb.MM(pa[:, :], R.wa2[64:128, d, c0:c0 + 128], L.swa[64:128, :], True, True, reads=SEG + [R.wa2], writes=[pa])
    kb.I("act", "activation", reads=[pa, R.rw_a0], writes=AB, out=L.A[:, :], in_=pa[:, :], func=AF.Sigmoid, bias=R.rw_a0[:, e, d, jp:jp + 1])
    kb.I("dve", "tensor_scalar", reads=AB + [R.rw_ka, R.omk], writes=AB, out=L.KT[:, :], in0=L.A[:, :], scalar1=R.rw_ka[:, e, jp:jp + 1], scalar2=R.omk[:, e, jp:jp + 1],
         op0=ALU.mult, op1=ALU.add)
    kb.I("dve", "tensor_tensor", reads=AB + SEG, writes=AB, out=L.KT[:, :], in0=L.KT[:, :], in1=L.sk[:, :], op=ALU.mult)
    if want_w:
        pw = g.nb()
        kb.MM(pw[:, :], R.wa2[0:64, d, c0:c0 + 128], L.swa[0:64, :], True, True, reads=SEG + [R.wa2], writes=[pw])
        kb.I("act", "activation", reads=[pw, R.rw_w0], writes=AB, out=L.LW[:, :], in_=pw[:, :], func=AF.Sigmoid, bias=R.rw_w0[:, e, d, jp:jp + 1])
        kb.I("dve", "tensor_scalar", reads=AB, writes=AB, out=L.LW[:, :], in0=L.LW[:, :], scalar1=NEG_E05, scalar2=None, op0=ALU.mult)


def _rw_dir_prep(g, R, L, e, jp, d):
    kb = g.kb
    SEG = [(g.scr, "seg")]
    AB = [(g.actb_flat, "seg")]
    _rw_lora_a_kt(g, R, L, e, jp, d, True)
    b3 = lambda ap: ap.rearrange("p (b t) -> p b t", b=NBS)
    if d == 0:
        kb.I("dve", "tensor_tensor_scan", reads=AB + [R.m01], writes=AB, out=L.L[:, :], data0=R.m01[:, :], data1=L.LW[:, :], initial=0.0, op0=ALU.mult, op1=ALU.add)
        ltot = b3(L.L[:, :])[:, :, 127:128]
    else:
        kb.I("dve", "tensor_tensor_scan", reads=AB + [R.m01], writes=AB, out=L.E[:, :], data0=R.m01[:, :], data1=L.LW[:, :], initial=0.0, op0=ALU.mult, op1=ALU.add)
        kb.I("dve", "tensor_tensor", reads=AB, writes=AB, out=L.L[:, :], in0=L.LW[:, :], in1=L.E[:, :], op=ALU.subtract)
        kb.I("dve", "tensor_tensor", reads=AB, writes=AB, out=b3(L.L[:, :]), in0=b3(L.L[:, :]), in1=b3(L.E[:, :])[:, :, 127:128].broadcast_to([128, NBS, 128]), op=ALU.add)
        ltot = b3(L.L[:, :])[:, :, 0:1]
    kb.I("act", "activation", reads=AB, writes=[R.gam], out=R.gam[:, :], in_=ltot.rearrange("p b o -> p (b o)"), func=AF.Exp)
    al, be, ka, rt, bh, kh = [L.prod[:, i, :] for i in range(6)]
    kb.I("dve", "tensor_tensor", reads=AB + SEG, writes=AB, out=L.A[:, :], in0=L.A[:, :], in1=L.kk[:, :], op=ALU.mult)
    kb.I("dve", "tensor_tensor", reads=AB, writes=AB, out=L.E2[:, :], in0=L.L[:, :], in1=L.LW[:, :], op=ALU.subtract)
    kb.I("act", "activation", reads=AB, writes=AB, out=L.E2[:, :], in_=L.E2[:, :], func=AF.Exp)
    kb.I("dve", "tensor_tensor", reads=AB + SEG, writes=SEG, out=al, in0=L.kk[:, :], in1=L.E2[:, :], op=ALU.mult)
    kb.I("act", "activation", reads=AB, writes=AB, out=L.E[:, :], in_=L.L[:, :], func=AF.Exp, scale=-1.0)
    kb.I("dve", "scalar_tensor_tensor", reads=AB, writes=SEG, out=be, in0=L.A[:, :], scalar=-1.0, in1=L.E[:, :], op0=ALU.mult, op1=ALU.mult)
    kb.I("dve", "tensor_tensor", reads=AB, writes=SEG, out=ka, in0=L.KT[:, :], in1=L.E[:, :], op=ALU.mult)
    kb.I("act", "activation", reads=AB, writes=AB, out=L.E[:, :], in_=L.L[:, :], func=AF.Exp)
    kb.I("dve", "tensor_tensor", reads=AB + SEG, writes=SEG, out=rt, in0=L.sr[:, :], in1=L.E[:, :], op=ALU.mult)
    kb.I("dve", "scalar_tensor_tensor", reads=AB, writes=AB, out=b3(L.E2[:, :]), in0=b3(L.L[:, :]), scalar=-1.0, in1=ltot.broadcast_to([128, NBS, 128]), op0=ALU.mult, op1=ALU.add)
    kb.I("act", "activation", reads=AB, writes=AB, out=L.E2[:, :], in_=L.E2[:, :], func=AF.Exp)
    kb.I("dve", "scalar_tensor_tensor", reads=AB, writes=SEG, out=bh, in0=L.A[:, :], scalar=-1.0, in1=L.E2[:, :], op0=ALU.mult, op1=ALU.mult)
    kb.I("dve", "tensor_tensor", reads=AB, writes=SEG, out=kh, in0=L.KT[:, :], in1=L.E2[:, :], op=ALU.mult)
    for bi in range(NBS):
        pt = g.nbt()
        for k, src in enumerate((al, bh, kh, L.svb[:, :])):
            kb.TR(pt[:, k * 128:(k + 1) * 128], src[:, bi * 128:(bi + 1) * 128], g.ident_b[:, :], reads=SEG + [g.ident_b], writes=[pt])
        evac(g, bi, L.tok[:, :, bi, :], pt[:, 0:512].rearrange("p (k c) -> p k c", k=4), [pt], SEG)
        for hh in range(2):
            kb.I("pool", "tensor_copy", reads=SEG, writes=SEG, out=L.Vpad[:, bi, hh, hh * 64:(hh + 1) * 64], in_=L.tok[:, 3, bi, hh * 64:(hh + 1) * 64])


def _rw_run(*gens):
    gens = [x for x in gens if x is not None]
    while gens:
        for x in list(gens):
            try:
                next(x)
            except StopIteration:
                gens.remove(x)


def _rw_sets(g, L):
    s1 = lambda kc, a, b: g.hT[:, kc, 1024 + a:1024 + b]
    return [
        dict(G3=L.G3, G2=L.G2, WT=L.WT, U0=L.U0, key=[(g.scr, "pset0")]),
        dict(G3=[s1(0, 0, 384), s1(1, 0, 384)], G2=[s1(2, 0, 256), s1(2, 256, 512)], U0=s1(3, 0, 256).bitcast(F32), WT=s1(3, 256, 384), key=[(g.hT, 2)]),
    ]


def _rw_prep_block(g, R, L, d, bi, S):
    kb = g.kb
    SEG = [(g.scr, "seg")]
    KS = S["key"]
    al, be, ka, rt, bh, kh = [L.prod[:, i, :] for i in range(6)]
    ts = slice(bi * 128, (bi + 1) * 128)
    for hh in range(2):
        hp = slice(64 * hh, 64 * hh + 64)
        pg = g.nb()
        kb.MM(pg[:, 0:128], be[hp, ts], al[hp, ts], True, True, reads=SEG, writes=[pg])
        kb.MM(pg[:, 128:256], al[hp, ts], be[hp, ts], True, True, reads=SEG, writes=[pg])
        kb.MM(pg[:, 256:384], be[hp, ts], rt[hp, ts], True, True, reads=SEG, writes=[pg])
        kb.I("dve", "tensor_tensor", reads=[pg, R.cm3], writes=KS, out=S["G3"][hh], in0=pg[:, 0:384], in1=R.cm3[:, d, :], op=ALU.mult)
        pg2 = g.nb()
        kb.MM(pg2[:, 0:128], ka[hp, ts], al[hp, ts], True, True, reads=SEG, writes=[pg2])
        kb.MM(pg2[:, 128:256], ka[hp, ts], rt[hp, ts], True, True, reads=SEG, writes=[pg2])
        kb.I("dve", "tensor_tensor", reads=[pg2, R.cm2], writes=KS, out=S["G2"][hh], in0=pg2[:, 0:256], in1=R.cm2[:, d, :], op=ALU.mult)
    yield
    cur = [0, 0]
    for lv in range(7):
        if lv > 0:
            yield
        for hh in range(2):
            nl = L.NLl[hh][lv % 2]
            HB = [(g.scr, ("pint", hh))]
            kb.I("pool" if hh == 0 else "dve", "tensor_tensor", reads=KS + [R.lvl], writes=HB, out=nl.rearrange("p (a b) -> p a b", a=2),
                 in0=S["G3"][hh][:, 0:256].rearrange("p (a b) -> p a b", a=2), in1=R.lvl[:, lv, :].unsqueeze(1).broadcast_to([128, 2, 128]), op=ALU.mult)
            if lv == 0:
                ta = L.TA[hh][0]
                kb.I("pool", "tensor_tensor", reads=HB + [g.ident_b], writes=HB, out=ta[:, 0:128], in0=nl[:, 128:256], in1=g.ident_b[:, :], op=ALU.add)
                kb.I("pool", "tensor_tensor", reads=HB + [g.ident_b], writes=HB, out=ta[:, 128:256], in0=nl[:, 0:128], in1=g.ident_b[:, :], op=ALU.add)
                continue
            ta, tn = L.TA[hh][cur[hh]], L.TA[hh][1 - cur[hh]]
            px = g.nb()
            kb.MM(px[:, 0:128], nl[:, 0:128], ta[:, 0:128], True, True, reads=HB, writes=[px])
            kb.MM(px[:, 128:256], nl[:, 128:256], ta[:, 128:256], True, True, reads=HB, writes=[px])
            evac(g, hh + lv, L.XB[hh][:, 0:256], px[:, 0:256], [px], HB)
            pq = g.nb()
            kb.MM(pq[:, 0:128], ta[:, 128:256], L.XB[hh][:, 0:128], True, True, reads=HB, writes=[pq])
            kb.MM(pq[:, 128:256], ta[:, 0:128], L.XB[hh][:, 128:256], True, True, reads=HB, writes=[pq])
            kb.I("dve", "tensor_tensor", reads=[pq] + HB, writes=HB, out=tn[:, 0:256], in0=pq[:, 0:256], in1=ta[:, 0:256], op=ALU.add)
            cur[hh] = 1 - cur[hh]
    yield
    pu0 = g.nb()
    for hh in range(2):
        hp = slice(64 * hh, 64 * hh + 64)
        HB = [(g.scr, ("pint", hh))]
        TT = L.TA[hh][cur[hh]][:, 128:256]
        pw = g.nb()
        kb.MM(pw[:, 0:128], L.tok[:, 0, bi, :], TT, True, True, reads=SEG + HB, writes=[pw])
        kb.I("act", "copy", reads=[pw], writes=KS, out=S["WT"][hp, :], in_=pw[hp, 0:128])
        kb.MM(pw[:, 128:192], S["G2"][hh][:, 0:128], L.tok[:, 3, bi, 64 * hh:64 * hh + 64], True, True, reads=SEG + KS, writes=[pw])
        kb.I("dve", "tensor_copy", reads=[pw], writes=HB, out=L.Y1b[hh], in_=pw[:, 128:192])
        kb.MM(pu0[:, 64 * hh:64 * hh + 64], TT, L.Y1b[hh], True, True, reads=HB, writes=[pu0])
    kb.I("act", "copy", reads=[pu0], writes=KS, out=S["U0"], in_=pu0[:, 0:128])


def _rw_chain_block(g, R, L, d, bi, S, ycb):
    kb = g.kb
    SEG = [(g.scr, "seg")]
    KS = S["key"]
    KC = [(g.scr, "cint")]
    rt = L.prod[:, 3, :]
    ts = slice(bi * 128, (bi + 1) * 128)
    S32, S16 = R.S32[:, d, :], R.S16[:, d, :]
    ST = [(R.S32, d), (R.S16, d)]
    pu = g.nb()
    kb.MM(pu[:, 0:128], S["WT"][:, :], S16[:, :], True, True, reads=KS + ST, writes=[pu])
    kb.I("dve", "tensor_tensor", reads=[pu] + KS, writes=KC, out=L.Ub, in0=pu[:, 0:128], in1=S["U0"], op=ALU.add)
    kb.I("pool", "tensor_copy", reads=KC, writes=KC, out=L.Upad.rearrange("p (a b) -> p a b", a=2)[:, :, 0:64], in_=L.Ub.rearrange("p (a b) -> p a b", a=2))
    yield
    py = g.nb()
    kb.MM(py[:, 0:128], S16[:, :], rt[:, ts], True, False, reads=SEG + ST, writes=[py])
    for hh in range(2):
        kb.MM(py[:, 0:128], L.Upad[:, hh * 128:(hh + 1) * 128], S["G3"][hh][:, 256:384], False, False, reads=KC + KS, writes=[py])
        kb.MM(py[:, 0:128], L.Vpad[:, bi, hh, :], S["G2"][hh][:, 128:256], False, hh == 1, reads=SEG + KS, writes=[py])
    ycb(bi, py)
    yield
    psn = g.nb()
    kb.MM(psn[:, 0:128], L.tok[:, 2, bi, :], L.tok[:, 3, bi, :], True, False, reads=SEG, writes=[psn])
    kb.MM(psn[:, 0:128], L.tok[:, 1, bi, :], L.Ub, False, True, reads=SEG + KC, writes=[psn])
    yield
    for hh in range(2):
        hp = slice(64 * hh, 64 * hh + 64)
        cs = slice(64 * hh, 64 * hh + 64)
        kb.I("dve", "scalar_tensor_tensor", reads=[psn, R.gam] + ST, writes=[(R.S32, d)], out=S32[hp, cs], in0=S32[hp, cs], scalar=R.gam[hp, bi:bi + 1], in1=psn[hp, cs],
             op0=ALU.mult, op1=ALU.add)
    kb.I("act", "copy", reads=[(R.S32, d)], writes=[(R.S16, d)], out=S16, in_=S32)


def _rw_epilogue(g, R, L, e, jp, emit_out):
    kb = g.kb
    SEG = [(g.scr, "seg")]
    AB = [(g.actb_flat, "seg")]
    kb.I("act", "copy", reads=SEG, writes=SEG, out=L.Yb[:, :], in_=L.Yacc[:, :])
    pm = g.nb()
    kb.MM(pm[:, :], g.blk_b[:, :], L.Yb[:, :], True, True, reads=SEG + [g.blk_b], writes=[pm])
    kb.I("dve", "scalar_tensor_tensor", reads=[pm] + SEG, writes=SEG, out=L.Yacc[:, :], in0=pm[:, :], scalar=-1.0 / 64, in1=L.Yacc[:, :], op0=ALU.mult, op1=ALU.add)
    kb.I("act", "activation", reads=SEG, writes=SEG, out=L.Yb[:, :], in_=L.Yacc[:, :], func=AF.Square)
    pv = g.nb()
    kb.MM(pv[:, :], g.blk_b[:, :], L.Yb[:, :], True, True, reads=SEG + [g.blk_b], writes=[pv])
    kb.I("act", "activation", reads=[pv, g.eps_t], writes=AB, out=L.E[:, :], in_=pv[:, :], func=AF.Sqrt, scale=1.0 / 64, bias=g.eps_t[:, 1:2])
    kb.I("dve", "reciprocal", reads=AB, writes=AB, out=L.E[:, :], in_=L.E[:, :])
    kb.I("dve", "tensor_tensor", reads=SEG + AB, writes=SEG, out=L.Yacc[:, :], in0=L.Yacc[:, :], in1=L.E[:, :], op=ALU.mult)
    kb.I("dve", "tensor_scalar", reads=SEG + [R.rw_lnw, R.rw_lnb], writes=SEG, out=L.Yacc[:, :], in0=L.Yacc[:, :], scalar1=R.rw_lnw[:, e, jp:jp + 1], scalar2=R.rw_lnb[:, e, jp:jp + 1],
         op0=ALU.mult, op1=ALU.add)
    for d in range(2):
        _rw_lora_a_kt(g, R, L, e, jp, d, False)
        if d == 0:
            kb.I("dve", "tensor_scalar", reads=AB + [R.rw_bonus], writes=AB, out=L.E2[:, :], in0=L.KT[:, :], scalar1=R.rw_bonus[:, e, d, jp:jp + 1], scalar2=None, op0=ALU.mult)
        else:
            kb.I("dve", "scalar_tensor_tensor", reads=AB + [R.rw_bonus], writes=AB, out=L.E2[:, :], in0=L.KT[:, :], scalar=R.rw_bonus[:, e, d, jp:jp + 1], in1=L.E2[:, :],
                 op0=ALU.mult, op1=ALU.add)
    kb.I("dve", "tensor_tensor", reads=AB + SEG, writes=SEG, out=L.Yb[:, :], in0=L.E2[:, :], in1=L.sr[:, :], op=ALU.mult)
    pbn = g.nb()
    kb.MM(pbn[:, :], g.blk_b[:, :], L.Yb[:, :], True, True, reads=SEG + [g.blk_b], writes=[pbn])
    kb.I("dve", "tensor_tensor", reads=[pbn] + SEG, writes=AB, out=L.E[:, :], in0=pbn[:, :], in1=L.sv[:, :], op=ALU.mult)
    kb.I("dve", "tensor_tensor", reads=SEG + AB, writes=SEG, out=L.Yacc[:, :], in0=L.Yacc[:, :], in1=L.E[:, :], op=ALU.add)
    pgt = g.nb()
    kb.MM(pgt[:, :], R.g2[:, jp * 128:(jp + 1) * 128], L.sg[:, :], True, True, reads=SEG + [R.g2], writes=[pgt])
    kb.I("dve", "tensor_tensor", reads=[pgt] + SEG, writes=SEG, out=L.ob[:, :], in0=L.Yacc[:, :], in1=pgt[:, :], op=ALU.mult)
    emit_out(L.ob)


def _rwkv_stub(g, XO):
    kb = g.kb
    z = g.nstage()
    kb.I("pool", "memset", writes=[z], ap=z[:, :], constant=0.0)
    for b in range(8):
        kb.dma("sp", XO.gin[1][:, b * 512:(b + 1) * 512], z[:, :], reads=[z], writes=[(XO.gin, ("rw", b))])
    for k in range(4, 8):
        for ti in range(2):
            kb.I("pool", "memset", writes=[(g.hT, ti)], ap=g.hT[:, k, ti * 512:(ti + 1) * 512], constant=0.0)


def _rwkv(g, l, pinP, XS, lout, XO):
    kb, I = g.kb, g.ins
    e = l // 2
    R = _rw_setup(g)
    L = _rw_layout(g)
    SETS = _rw_sets(g, L)
    SEG = [(g.scr, "seg")]
    ORW = g.outd["o_rw"]
    kb.dma("pool", R.wa2[0:64, :, :], I["rw_w2"][e].rearrange("d k n -> k d n"), reads=[I["rw_w2"]], writes=[R.wa2])
    kb.dma("pool", R.wa2[64:128, :, :], I["rw_a2"][e].rearrange("d k n -> k d n"), reads=[I["rw_a2"]], writes=[R.wa2])
    kb.dma("pool", R.g2[:, :], I["rw_g2"][e], reads=[I["rw_g2"]], writes=[R.g2])
    kb.I("pool", "memset", writes=[(g.scr, None)], ap=L.Vpad, constant=0.0)
    kb.I("pool", "memset", writes=[(g.scr, None)], ap=L.Upad, constant=0.0)
    import os
    STOP = os.environ.get("RWSTOP", "")
    if STOP == "setup":
        return _rwkv_stub(g, XO)

    def zero_state(d):
        kb.I("pool", "memset", writes=[(R.S32, d)], ap=R.S32[:, d, :], constant=0.0)
        kb.I("pool", "memset", writes=[(R.S16, d)], ap=R.S16[:, d, :], constant=0.0)

    for jp in range(4):
        for half in range(2):
            t0 = half * NS
            for a, blk in enumerate((12 + jp, 16 + jp, 20 + jp, 24, 25)):
                kb.dma("sp" if a % 2 == 0 else "act", L.U5[:, a, 16:528], pinP[blk * 128:(blk + 1) * 128, t0:t0 + NS], reads=[pinP], writes=SEG)
            _rw_shift(g, R, L, e, jp, False)
            for d in range(2):
                _rw_dir_prep(g, R, L, e, jp, d)

                def ycb(bi, py, d=d):
                    ts = slice(bi * 128, (bi + 1) * 128)
                    if d == 0:
                        kb.I("act", "copy", reads=[py], writes=SEG, out=L.Yacc[:, ts], in_=py[:, 0:128])
                    else:
                        kb.I("dve", "tensor_tensor", reads=[py] + SEG, writes=SEG, out=L.Yacc[:, ts], in0=py[:, 0:128], in1=L.Yacc[:, ts], op=ALU.add)
                steps = [(0, True, None), (1, False, 0), (2, True, None), (3, False, 1)] if d == 0 else [(1, True, None), (0, False, 0), (3, True, None), (2, False, 1)]
                _rw_run(_rw_prep_block(g, R, L, d, steps[0][0], SETS[0]))
                for i, (bi, reset, fin) in enumerate(steps):
                    if reset:
                        zero_state(d)
                    nxt = _rw_prep_block(g, R, L, d, steps[i + 1][0], SETS[(i + 1) % 2]) if i + 1 < len(steps) else None
                    _rw_run(_rw_chain_block(g, R, L, d, bi, SETS[i % 2], ycb), nxt)
                    if fin is not None:
                        pst = g.nb()
                        kb.TR(pst[:, 0:128], R.S32[:, d, :], g.ident_f[:, :], reads=[(R.S32, d), g.ident_f], writes=[pst])
                        so = g.ntmp()
                        kb.I("act", "copy", reads=[pst], writes=[so], out=so[:, 0:128], in_=pst[:, 0:128])
                        seq = half * 2 + fin
                        for hh in range(2):
                            kb.dma("sp", ORW[e, seq, d, jp, :, 64 * hh:64 * hh + 64], so[64 * hh:64 * hh + 64, 64 * hh:64 * hh + 64], reads=[so], writes=[(ORW, (e, seq, d, jp, hh))])

            def emit_out(ob, jp=jp, t0=t0):
                kb.I("pool", "tensor_copy", reads=SEG, writes=[(g.hT, t0 // TT)], out=g.hT[:, 4 + jp, t0:t0 + NS], in_=ob[:, :])
            _rw_epilogue(g, R, L, e, 0 + jp, emit_out)

    if STOP == "prompt":
        z = g.nstage()
        kb.I("pool", "memset", writes=[z], ap=z[:, :], constant=0.0)
        for b in range(8):
            kb.dma("sp", XO.gin[1][:, b * 512:(b + 1) * 512], z[:, :], reads=[z], writes=[(XO.gin, ("rw", b))])
        return
    jp = 4
    yfw = kb.dram(f"yfw{l}", [8, 128, NS], F32)

    def load_seg(sg):
        T0 = sg * NS
        rr, cc = T0 // 1024, T0 % 1024
        for a in range(5):
            def src(rr_, lo, hi, a=a):
                if a < 3:
                    return XS.get(rr_, 3 + a)[:, lo:hi], XS.mine
                return lout[rr_, (a - 3) * 128:(a - 2) * 128, lo:hi], lout
            eng = "sp" if a % 2 == 0 else "act"
            ap, sb_ = src(rr, cc, cc + NS)
            kb.dma(eng, L.U5[:, a, 16:528], ap, reads=[sb_], writes=SEG)
            if T0 == 0:
                kb.I("pool", "memset", writes=SEG, ap=L.U5[:, a, 15:16], constant=0.0)
            else:
                Tm = T0 - 1
                ap, sb_ = src(Tm // 1024, Tm % 1024, Tm % 1024 + 1)
                kb.dma(eng, L.hst[:, a, 0, 0:1], ap, reads=[sb_], writes=SEG, allow_slow_non_contiguous=True)
                kb.I("pool", "tensor_copy", reads=SEG, writes=SEG, out=L.U5[:, a, 15:16], in_=L.hst[:, a, 0, 0:1])
            if T0 + NS == 4096:
                kb.I("pool", "memset", writes=SEG, ap=L.U5[:, a, 528:529], constant=0.0)
            else:
                Tp = T0 + NS
                ap, sb_ = src(Tp // 1024, Tp % 1024, Tp % 1024 + 1)
                kb.dma(eng, L.hst[:, a, 1, 0:1], ap, reads=[sb_], writes=SEG, allow_slow_non_contiguous=True)
                kb.I("pool", "tensor_copy", reads=SEG, writes=SEG, out=L.U5[:, a, 528:529], in_=L.hst[:, a, 1, 0:1])

    for d in range(2):
        zero_state(d)
        s0 = g.ntmp()
        kb.dma("sp", s0[0:64, 0:128], I["st_rw"][e, d], reads=[I["st_rw"]], writes=[s0])
        pst = g.nb()
        kb.TR(pst[:, 0:64], s0[0:64, 0:128], g.ident_f[0:64, 0:64], reads=[s0, g.ident_f], writes=[pst])
        for hh in range(2):
            hp = slice(64 * hh, 64 * hh + 64)
            kb.I("dve", "tensor_copy", reads=[pst], writes=[(R.S32, d)], out=R.S32[hp, d, 64 * hh:64 * hh + 64], in_=pst[hp, 0:64])
        kb.I("act", "copy", reads=[(R.S32, d)], writes=[(R.S16, d)], out=R.S16[:, d, :], in_=R.S32[:, d, :])
    for d in range(2):
        segs = range(8) if d == 0 else range(7, -1, -1)
        for sg in segs:
            load_seg(sg)
            _rw_shift(g, R, L, e, jp, True)
            _rw_dir_prep(g, R, L, e, jp, d)
            if d == 1:
                kb.dma("sp", L.Yacc[:, :], yfw[sg], reads=[(yfw, sg)], writes=SEG)

            def ycb(bi, py, d=d):
                ts = slice(bi * 128, (bi + 1) * 128)
                if d == 0:
                    kb.I("act", "copy", reads=[py], writes=SEG, out=L.Yacc[:, ts], in_=py[:, 0:128])
                else:
                    kb.I("dve", "tensor_tensor", reads=[py] + SEG, writes=SEG, out=L.Yacc[:, ts], in0=py[:, 0:128], in1=L.Yacc[:, ts], op=ALU.add)
            blks = list(range(NBS)) if d == 0 else list(range(NBS - 1, -1, -1))
            _rw_run(_rw_prep_block(g, R, L, d, blks[0], SETS[0]))
            for i, bi in enumerate(blks):
                nxt = _rw_prep_block(g, R, L, d, blks[i + 1], SETS[(i + 1) % 2]) if i + 1 < NBS else None
                _rw_run(_rw_chain_block(g, R, L, d, bi, SETS[i % 2], ycb), nxt)
            if d == 0:
                kb.dma("sp", yfw[sg], L.Yacc[:, :], reads=SEG, writes=[(yfw, sg)])
            else:
                def emit_out(ob, sg=sg):
                    kb.dma("sp", XO.gin[1][:, sg * NS:(sg + 1) * NS], ob[:, :], reads=SEG, writes=[(XO.gin, ("rw", sg))])
                _rw_epilogue(g, R, L, e, jp, emit_out)


def _fm(v, nblk):
    v = np.asarray(v)
    lead = v.shape[:-1]
    a = v.reshape(lead + (nblk, 128))
    a = np.moveaxis(a, -1, 0)
    return np.ascontiguousarray(a)


def _na_tables(rpb):
    E, H = rpb.shape[0], rpb.shape[1]
    tab = np.full((E, H, 128, 5, 5, 128), MASKV, np.float32)
    var_i = {1: 0, 2: 2, 3: 60, 4: 62}
    half = np.arange(128) // 64
    kcol = np.arange(128) % 64
    qo = np.arange(128) // 64
    j = np.arange(128) % 64
    c0 = np.clip(j - 8, 0, 48)
    colok = (kcol[:, None] >= c0[None, :]) & (kcol[:, None] < c0[None, :] + 16)
    dc = kcol[:, None] - j[None, :] + 15
    dcc = np.clip(dc, 0, 30)
    for v in range(5):
        for m in range(5):
            if v == 0:
                i = 10
                r0c = i - 4
            else:
                i = var_i[v]
                r0c = min(max(i - 4, 0), 56)
                if m == 4:
                    continue
            kr = r0c + 2 * m + half
            iq = i + qo
            r0q = np.clip(iq - 4, 0, 56)
            rowok = (kr[:, None] >= r0q[None, :]) & (kr[:, None] < r0q[None, :] + 8)
            dr = kr[:, None] - iq[None, :] + 7
            drc = np.clip(dr, 0, 14)
            ok = rowok & colok
            vals = rpb[:, :, drc, dcc]
            cur = tab[:, :, :, v, m, :]
            tab[:, :, :, v, m, :] = np.where(ok[None, None], vals, cur)
    return tab


def _rope_tables():
    t = np.arange(4096)
    pos = np.stack([t // 64, t % 64], -1).astype(np.float32)
    inv = (10000.0 ** (-np.arange(16, dtype=np.float32) / 16)).astype(np.float32)
    ang = pos[:, :, None] * inv
    cos, sin = np.cos(ang).astype(np.float32), np.sin(ang).astype(np.float32)
    C = np.zeros((4096, 2, 2, 16), np.float32)
    S = np.zeros((4096, 2, 2, 16), np.float32)
    C[:, :, 0, :] = cos; C[:, :, 1, :] = cos
    S[:, :, 0, :] = -sin; S[:, :, 1, :] = sin
    C = C.reshape(4096, 64); S = S.reshape(4096, 64)
    C = np.concatenate([C, C], 1).T
    S = np.concatenate([S, S], 1).T
    return np.ascontiguousarray(C), np.ascontiguousarray(S)


def _sw_perm():
    idx = np.arange(2048).reshape(-1, 2, 2, 16)
    return idx[:, :, ::-1, :].reshape(-1)


_CACHE = {}


def prep_inputs(inp):
    f32 = lambda a: np.ascontiguousarray(np.asarray(a, np.float32))
    x_prompt = f32(inp["x_prompt"]); x_sample = f32(inp["x_sample"]); c = f32(inp["c"])
    common = {}
    common["w_ada"] = f32(inp["w_ada"])
    common["b_ada"] = _fm(f32(inp["b_ada"]), 48)
    common["norm_mix"] = _fm(f32(inp["norm_mix"]), 8)
    common["norm_ffn"] = _fm(f32(inp["norm_ffn"]), 8)
    common["norm_final"] = _fm(f32(inp["norm_final"]), 8)
    common["w_in_even"] = f32(inp["w_in_even"])
    common["w_out_even"] = f32(inp["w_out_even"])
    wq = f32(inp["w_qkv_diff"])
    common["w_qkv_diff"] = wq
    common["w_qk_sw"] = np.ascontiguousarray(wq[:, :, :2048][:, :, _sw_perm()])
    common["w_out_diff"] = f32(inp["w_out_diff"])
    common["ffn_w1"] = f32(inp["ffn_w1"]); common["ffn_w3"] = f32(inp["ffn_w3"]); common["ffn_w2"] = f32(inp["ffn_w2"])
    mu = f32(inp["rw_mu"])
    common["rw_mu"] = np.ascontiguousarray(np.transpose(_fm(mu, 14), (0, 1, 3, 2)))
    common["rw_w0"] = _fm(f32(inp["rw_w0"]), 4)
    common["rw_a0"] = _fm(f32(inp["rw_a0"]), 4)
    common["rw_kk"] = _fm(f32(inp["rw_kk"]), 4)
    common["rw_ka"] = _fm(f32(inp["rw_ka"]), 4)
    common["rw_lnw"] = _fm(f32(inp["rw_lnw"]), 4)
    common["rw_lnb"] = _fm(f32(inp["rw_lnb"]), 4)
    common["rw_bonus"] = _fm(f32(inp["rw_bonus"]).reshape(2, 2, 512), 4)
    common["rw_w2"] = f32(inp["rw_w2"]); common["rw_a2"] = f32(inp["rw_a2"]); common["rw_g2"] = f32(inp["rw_g2"])
    rep = lambda a: np.ascontiguousarray(np.broadcast_to(a[None], (128,) + a.shape))
    common["lamq"] = rep(f32(inp["diff_lam_q"]).reshape(2, 128))
    common["lamk"] = rep(f32(inp["diff_lam_k"]).reshape(2, 128))
    common["subln"] = rep(f32(inp["diff_subln"]))
    s = np.arange(128)[:, None]; t = np.arange(128)[None, :]
    common["masks"] = np.ascontiguousarray(np.stack([(s < t), (s <= t), (s > t), (s >= t)], 1).astype(np.float32))
    common["ident"] = np.eye(128, dtype=np.float32)
    pi = np.arange(128)[:, None]; fi = np.arange(128)[None, :]
    common["lvlmask"] = np.ascontiguousarray(np.stack([((pi >> (lv + 1)) == (fi >> (lv + 1))) & ((pi >> lv) != (fi >> lv)) for lv in range(7)], 1).astype(np.float32))
    tab = _na_tables(f32(inp["na_rpb"]))
    ropeC, ropeS = _rope_tables()
    cna_k = f32(inp["cache_na_k"]); cna_v = f32(inp["cache_na_v"]); st = f32(inp["state_rwkv"])
    cdk = f32(inp["cache_diff_k"]); cdv = f32(inp["cache_diff_v"])
    c_ctx = f32(inp["c_ctx"])
    maps = []
    for core in range(8):
        gi, r = core // 4, core % 4
        m = dict(common)
        xin = np.concatenate([x_prompt[4 * core:4 * core + 4].reshape(1024, 1024), x_sample[gi, 1024 * r:1024 * (r + 1)]], 0)
        m["xin"] = np.ascontiguousarray(xin)
        m["cvec"] = np.ascontiguousarray(np.transpose(_fm(np.stack([c_ctx, c[gi]], 0), 8), (0, 2, 1)))
        m["rope_c"] = np.ascontiguousarray(ropeC[:, 1024 * r:1024 * (r + 1)])
        m["rope_s"] = np.ascontiguousarray(ropeS[:, 1024 * r:1024 * (r + 1)])
        m["na_tab"] = np.ascontiguousarray(tab[:, 2 * r:2 * r + 2])
        m["cna_k"] = np.ascontiguousarray(cna_k[gi][:, :, 2 * r:2 * r + 2, :].reshape(2, 512, 128))
        m["cna_v"] = np.ascontiguousarray(cna_v[gi][:, :, 2 * r:2 * r + 2, :].reshape(2, 512, 128))
        s2 = st[gi][:, :, 2 * r:2 * r + 2]
        m["st_rw"] = np.ascontiguousarray(np.transpose(s2, (0, 1, 3, 2, 4)).reshape(2, 2, 64, 128))
        m["cdf_k"] = np.ascontiguousarray(np.transpose(cdk[gi][:, :, 2 * r:2 * r + 2, :], (0, 2, 1, 3)))
        m["cdf_v"] = np.ascontiguousarray(np.transpose(cdv[gi][:, :, 2 * r:2 * r + 2, :], (0, 2, 1, 3)))
        mu = common["rw_mu"]
        m["rw_mu"] = np.ascontiguousarray(np.concatenate([mu, mu[:, :, [r, 4 + r, 8 + r], :]], axis=2))
        for nm in ("rw_w0", "rw_a0", "rw_bonus", "rw_kk", "rw_ka", "rw_lnw", "rw_lnb"):
            a = common[nm]
            m[nm] = np.ascontiguousarray(np.concatenate([a, a[..., r:r + 1]], axis=-1))
        for nm in ("rw_w2", "rw_a2", "rw_g2"):
            a = common[nm]
            m[nm] = np.ascontiguousarray(np.concatenate([a, a[..., 128 * r:128 * (r + 1)]], axis=-1))
        maps.append(m)
    return maps


def assemble(results):
    y = np.stack([r["y"] for r in results], 0)
    y_prompt = y[:, :1024].reshape(32, 256, 1024)
    y_sample = y[:, 1024:].reshape(2, 4096, 1024)
    nk = np.stack([r["o_na_k"] for r in results], 0)
    nv = np.stack([r["o_na_v"] for r in results], 0)
    to_cache = lambda a, hd: np.ascontiguousarray(np.transpose(a.reshape(8, 2, 4, 256, 8, hd), (0, 2, 1, 3, 4, 5)).reshape(32, 2, 256, 8, hd))
    new_na_k = to_cache(nk, 64); new_na_v = to_cache(nv, 64)
    dk = np.stack([r["o_df_k"] for r in results], 0); dv = np.stack([r["o_df_v"] for r in results], 0)
    new_diff_k = to_cache(dk, 128); new_diff_v = to_cache(dv, 128)
    rw = np.stack([r["o_rw"] for r in results], 0)
    rw = rw.reshape(8, 2, 4, 2, 4, 64, 2, 64)
    rw = np.transpose(rw, (0, 2, 1, 3, 4, 6, 5, 7)).reshape(32, 2, 2, 8, 64, 64)
    return (np.ascontiguousarray(y_prompt), np.ascontiguousarray(y_sample), new_na_k, new_na_v,
            np.ascontiguousarray(rw), new_diff_k, new_diff_v)


def kernel(**inputs):
    maps = prep_inputs(inputs)
    if "nc" not in _CACHE:
        _CACHE["nc"] = build_program()[0]
    nc = _CACHE["nc"]
    res = run_bass_kernel_spmd(nc, maps, core_ids=list(range(8)))
    return assemble(res.results)
```
